# Optimizing a Trainium2 kernel written in Bass

```python
import jax, jax.numpy as jnp
from jax import lax
import numpy as np

D_MODEL = 2048
BATCH = 2
SEQ = 4096
DEPTH = 1

D_CONV = D_MODEL // 2
CONV_GROUPS = 8
CONV_WIDTH = 3
GLA_HEADS = 4
D_GLA_V = D_MODEL // 2
D_GLA_K = D_GLA_V // 2
HEAD_K = D_GLA_K // GLA_HEADS
HEAD_V = D_GLA_V // GLA_HEADS
GATE_RANK = 16
GATE_TAU = 16.0
CHUNK = 64
D_FF = 4 * D_MODEL
LN_EPS = 1e-5
RMS_EPS = 1e-6
DN_ALPHA = (2.0 * DEPTH) ** 0.25
DN_BETA = (8.0 * DEPTH) ** -0.25

PROJ_SIZES = (D_CONV, D_CONV, D_CONV, D_GLA_K, D_GLA_K, D_GLA_V, D_GLA_V, GATE_RANK)
D_IN_PROJ = sum(PROJ_SIZES)
PROJ_SPLITS = tuple(int(s) for s in np.cumsum(PROJ_SIZES)[:-1])

kernel_name = "hymba_conv_gla_deepnorm_block"


def layer_norm(x, g, b):
    xf = x.astype(jnp.float32)
    mu = jnp.mean(xf, axis=-1, keepdims=True)
    var = jnp.mean(jnp.square(xf - mu), axis=-1, keepdims=True)
    y = (xf - mu) * lax.rsqrt(var + LN_EPS)
    return (y * g.astype(jnp.float32) + b.astype(jnp.float32)).astype(x.dtype)


def group_rms_norm(x, g, groups):
    shp = x.shape
    xf = x.astype(jnp.float32).reshape(shp[:-1] + (groups, shp[-1] // groups))
    xf = xf * lax.rsqrt(jnp.mean(jnp.square(xf), axis=-1, keepdims=True) + RMS_EPS)
    return (xf.reshape(shp) * g.astype(jnp.float32)).astype(x.dtype)


def short_gated_conv(b_gate, c_gate, u, conv_w):
    h = c_gate * u
    y = lax.conv_general_dilated(
        h, conv_w[:, None, :].astype(h.dtype), window_strides=(1,),
        padding=[(CONV_WIDTH - 1, 0)], dimension_numbers=("NWC", "WIO", "NWC"),
        feature_group_count=h.shape[-1])
    return b_gate * y


def gla_chunked(q, k, v, log_a):
    bsz, seq = q.shape[0], q.shape[1]
    n_chunks = seq // CHUNK

    def to_chunks(t):
        return t.reshape(bsz, n_chunks, CHUNK, GLA_HEADS, t.shape[-1]).transpose(0, 3, 1, 2, 4).astype(jnp.float32)

    qc = to_chunks(q) * (HEAD_K ** -0.5)
    kc, vc, gc = to_chunks(k), to_chunks(v), to_chunks(log_a)
    bcum = jnp.cumsum(gc, axis=3)
    b_last = bcum[:, :, :, -1:, :]
    q_dec = qc * jnp.exp(bcum)
    k_inv = kc * jnp.exp(-bcum)
    k_end = kc * jnp.exp(b_last - bcum)

    causal = jnp.tril(jnp.ones((CHUNK, CHUNK), dtype=bool))
    scores = jnp.where(causal, jnp.einsum("bhncd,bhnsd->bhncs", q_dec, k_inv), 0.0)
    o_intra = jnp.einsum("bhncs,bhnse->bhnce", scores, vc)

    delta = jnp.einsum("bhncd,bhnce->bhnde", k_end, vc)
    decay = jnp.exp(b_last[:, :, :, 0, :])

    def step(state, inp):
        dec, dlt = inp
        return dec[..., None] * state + dlt, state

    init = jnp.zeros((bsz, GLA_HEADS, HEAD_K, HEAD_V), jnp.float32)
    _, states = lax.scan(step, init, (jnp.moveaxis(decay, 2, 0), jnp.moveaxis(delta, 2, 0)))
    states = jnp.moveaxis(states, 0, 2)
    o_inter = jnp.einsum("bhncd,bhnde->bhnce", q_dec, states)
    o = o_intra + o_inter
    return o.transpose(0, 2, 3, 1, 4).reshape(bsz, seq, GLA_HEADS * HEAD_V)


def setup_inputs(seed: int = 0) -> dict:
    key = jax.random.key(seed)
    ks = jax.random.split(key, 16)
    f32 = jnp.float32
    x = jax.random.normal(ks[0], (BATCH, SEQ, D_MODEL), f32)
    col_scale = jnp.concatenate([
        jnp.full((D_CONV,), 1.0, f32), jnp.full((D_CONV,), 1.0, f32), jnp.full((D_CONV,), DN_BETA, f32),
        jnp.full((D_GLA_K,), 1.0, f32), jnp.full((D_GLA_K,), 1.0, f32), jnp.full((D_GLA_V,), DN_BETA, f32),
        jnp.full((D_GLA_V,), 1.0, f32), jnp.full((GATE_RANK,), 1.0, f32)])
    w_in = jax.random.normal(ks[1], (DEPTH, D_MODEL, D_IN_PROJ), f32) * (D_MODEL ** -0.5) * col_scale
    conv_w = jax.random.normal(ks[2], (DEPTH, CONV_WIDTH, D_CONV), f32) * (CONV_WIDTH ** -0.5)
    conv_norm_g = 1.0 + 0.02 * jax.random.normal(ks[3], (DEPTH, D_CONV), f32)
    w_gate_up = jax.random.normal(ks[4], (DEPTH, GATE_RANK, D_GLA_K), f32) * (GATE_RANK ** -0.5)
    gate_bias = 0.1 * jax.random.normal(ks[5], (DEPTH, D_GLA_K), f32)
    gla_norm_g = 1.0 + 0.02 * jax.random.normal(ks[6], (DEPTH, D_GLA_V), f32)
    w_out = jax.random.normal(ks[7], (DEPTH, D_CONV + D_GLA_V, D_MODEL), f32) * ((D_CONV + D_GLA_V) ** -0.5) * DN_BETA
    ln1_g = 1.0 + 0.02 * jax.random.normal(ks[8], (DEPTH, D_MODEL), f32)
    ln1_b = 0.02 * jax.random.normal(ks[9], (DEPTH, D_MODEL), f32)
    w_ff_up = jax.random.normal(ks[10], (DEPTH, D_MODEL, D_FF), f32) * (D_MODEL ** -0.5) * DN_BETA
    w_ff_down = jax.random.normal(ks[11], (DEPTH, D_FF, D_MODEL), f32) * (D_FF ** -0.5) * DN_BETA
    ln2_g = 1.0 + 0.02 * jax.random.normal(ks[12], (DEPTH, D_MODEL), f32)
    ln2_b = 0.02 * jax.random.normal(ks[13], (DEPTH, D_MODEL), f32)
    return {"x": x, "w_in": w_in, "conv_w": conv_w, "conv_norm_g": conv_norm_g,
            "w_gate_up": w_gate_up, "gate_bias": gate_bias, "gla_norm_g": gla_norm_g,
            "w_out": w_out, "ln1_g": ln1_g, "ln1_b": ln1_b, "w_ff_up": w_ff_up,
            "w_ff_down": w_ff_down, "ln2_g": ln2_g, "ln2_b": ln2_b}


def reference(x, w_in, conv_w, conv_norm_g, w_gate_up, gate_bias, gla_norm_g, w_out,
              ln1_g, ln1_b, w_ff_up, w_ff_down, ln2_g, ln2_b):
    bsz, seq = x.shape[0], x.shape[1]
    for l in range(DEPTH):
        proj = x @ w_in[l]
        b_gate, c_gate, u, q, k, v, r, z_low = jnp.split(proj, PROJ_SPLITS, axis=-1)

        y_conv = short_gated_conv(b_gate, c_gate, u, conv_w[l])
        y_conv = group_rms_norm(y_conv, conv_norm_g[l], CONV_GROUPS)

        z = (z_low @ w_gate_up[l] + gate_bias[l]).astype(jnp.float32)
        log_a = jax.nn.log_sigmoid(z) / GATE_TAU
        hk = (bsz, seq, GLA_HEADS, HEAD_K)
        o = gla_chunked(q.reshape(hk), k.reshape(hk), v.reshape(bsz, seq, GLA_HEADS, HEAD_V), log_a.reshape(hk))
        o = group_rms_norm(o.astype(x.dtype), gla_norm_g[l], GLA_HEADS)
        y_gla = o * jax.nn.silu(r)

        mix = jnp.concatenate([y_conv, y_gla], axis=-1) @ w_out[l]
        x = layer_norm(DN_ALPHA * x + mix, ln1_g[l], ln1_b[l])

        ff = jnp.square(jax.nn.relu(x @ w_ff_up[l])) @ w_ff_down[l]
        x = layer_norm(DN_ALPHA * x + ff, ln2_g[l], ln2_b[l])
    return x
```

```python
import numpy as np
import concourse.bass as bass
import concourse.mybir as mybir
from concourse.bass_utils import run_bass_kernel_spmd

F32 = mybir.dt.float32
BF16 = mybir.dt.bfloat16
ALU = mybir.AluOpType
AF = mybir.ActivationFunctionType

ENGS = ("pe", "act", "dve", "pool", "sp")

D = 2048
NT = 1024
NPRE = 3072
DFF = 8192
COL_B, COL_C, COL_U, COL_Q, COL_K, COL_V, COL_R, COL_Z = 0, 1024, 2048, 3072, 3584, 4096, 5120, 6144
DIN = 6160
ALPHA = 2.0 ** 0.25
LN_EPS = 1e-5
RMS_EPS = 1e-6
QSCALE = 128.0 ** -0.5
GI = 1.0 / 16.0


class Op:
    __slots__ = ("eng", "fn", "deps", "dma_sem", "dma_val", "milestone", "signals", "inc", "seq")
    _ctr = [0]

    def __init__(self, eng, fn):
        self.eng = eng
        self.fn = fn
        self.deps = ()
        self.dma_sem = None
        self.dma_val = 0
        self.milestone = 0
        self.signals = False
        Op._ctr[0] += 1
        self.seq = Op._ctr[0]


class Sched:
    def __init__(self, nc):
        self.nc = nc
        self.ops = {e: [] for e in ENGS}
        self.last_writer = {}
        self.readers = {}
        self.dma_cnt = {}
        self.dma_semh = {}
        self.eng_semh = None
        self.eng_cnt = {e: 0 for e in ENGS}
        self.nflush = 0

    def _deps(self, op, reads, writes):
        deps = {}
        lw = self.last_writer
        for r in reads:
            w = lw.get(r)
            if w is not None:
                deps[id(w)] = w
        for r in writes:
            w = lw.get(r)
            if w is not None:
                deps[id(w)] = w
            for rd in self.readers.get(r, ()):
                deps[id(rd)] = rd
        deps.pop(id(op), None)
        if op.eng == "pe" and op.dma_sem is None:
            deps = {k: d for k, d in deps.items() if not (d.eng == "pe" and d.dma_sem is None)}
        keep, young = [], {}
        for d in deps.values():
            if d.dma_sem is not None:
                keep.append(d)
            elif d.eng not in young or d.seq > young[d.eng].seq:
                young[d.eng] = d
        op.deps = tuple(keep) + tuple(young.values())
        for d in op.deps:
            d.signals = True
        for r in reads:
            self.readers.setdefault(r, []).append(op)
        for r in writes:
            lw[r] = op
            self.readers[r] = []

    def add(self, eng, fn, reads=(), writes=()):
        op = Op(eng, fn)
        self._deps(op, reads, writes)
        self.ops[eng].append(op)
        return op

    def dma(self, eng, fn, key, reads=(), writes=(), inc=16):
        op = Op(eng, fn)
        op.inc = inc
        self.dma_cnt[key] = self.dma_cnt.get(key, 0) + inc
        op.dma_sem = key
        op.dma_val = self.dma_cnt[key]
        self._deps(op, reads, writes)
        self.ops[eng].append(op)
        return op

    def fence_all(self, eng, fn, key):
        op = Op(eng, fn)
        deps = {}
        for e in ENGS:
            if self.ops[e]:
                d = self.ops[e][-1]
                deps[id(d)] = d
        w = self.last_writer.get(key)
        if w is not None:
            deps[id(w)] = w
        op.deps = tuple(deps.values())
        for d in op.deps:
            d.signals = True
        self.last_writer[key] = op
        self.readers[key] = []
        self.ops[eng].append(op)
        return op

    def flush(self, final_wait_ops=()):
        nc = self.nc
        if self.eng_semh is None:
            self.eng_semh = {e: nc.alloc_semaphore(name="ms_" + e) for e in ENGS}
        for k in self.dma_cnt:
            if k not in self.dma_semh:
                self.dma_semh[k] = nc.alloc_semaphore(name="dq_%d" % len(self.dma_semh))
        for e in ENGS:
            c = self.eng_cnt[e]
            for op in self.ops[e]:
                if op.dma_sem is None and op.signals and op.milestone == 0:
                    c += 1
                    op.milestone = c
            self.eng_cnt[e] = c
        ops_snapshot = {e: self.ops[e] for e in ENGS}
        esem, dsem = self.eng_semh, self.dma_semh

        def stream(e):
            def body(eng):
                waited = {}
                for op in ops_snapshot[e]:
                    need = {}
                    for d in op.deps:
                        if d.dma_sem is not None:
                            k = ("d", d.dma_sem)
                            v = d.dma_val
                        else:
                            k = ("e", d.eng)
                            v = d.milestone
                            if v <= 0:
                                continue
                        if v > need.get(k, 0):
                            need[k] = v
                    for k, v in need.items():
                        if waited.get(k, 0) >= v:
                            continue
                        waited[k] = v
                        eng.wait_ge(dsem[k[1]] if k[0] == "d" else esem[k[1]], v)
                    inst = op.fn(eng)
                    if op.dma_sem is not None:
                        inst.then_inc(dsem[op.dma_sem], op.inc)
                    elif op.signals:
                        inst.then_inc(esem[e], 1)
                if e == "sp":
                    for op in final_wait_ops:
                        eng.wait_ge(dsem[op.dma_sem], op.dma_val)
            return body

        with nc.Block() as block:
            block.tensor(stream("pe"))
            block.scalar(stream("act"))
            block.vector(stream("dve"))
            block.gpsimd(stream("pool"))
            block.sync(stream("sp"))
        for e in ENGS:
            for op in self.ops[e]:
                if op.dma_sem is None and not op.signals:
                    op.milestone = -1
        self.ops = {e: [] for e in ENGS}
        self.nflush += 1


class Arena:
    def __init__(self, nc, base=16512, top=229344):
        self.nc = nc
        self.cur = base
        self.top = top
        self.n = 0
        self.peak = base
        self.offs = {}

    def alloc(self, name, shape, dtype):
        esz = 4 if dtype == F32 else 2
        nbytes = esz
        for s in shape[1:]:
            nbytes *= s
        off = (self.cur + 63) // 64 * 64
        assert off + nbytes <= self.top, (name, off, nbytes, self.top)
        self.cur = off + nbytes
        self.peak = max(self.peak, self.cur)
        self.n += 1
        self.offs[name] = off
        return self.nc.alloc_sbuf_tensor_at("%s_%d" % (name, self.n), list(shape), dtype, offset=off)

    def alias(self, name, shape, dtype, off):
        self.n += 1
        return self.nc.alloc_sbuf_tensor_at("%s_%d" % (name, self.n), list(shape), dtype, offset=off)

    def alloc_top(self, name, shape, dtype):
        esz = 4 if dtype == F32 else 2
        nbytes = esz
        for s in shape[1:]:
            nbytes *= s
        off = (self.top - nbytes) // 64 * 64
        assert off >= self.cur, (name, off, self.cur)
        self.top = off
        self.n += 1
        self.offs[name] = off
        return self.nc.alloc_sbuf_tensor_at("%s_%d" % (name, self.n), list(shape), dtype, offset=off)

    def mark(self):
        return self.cur

    def reset(self, m):
        self.cur = m


def build_program(debug=False):
    nc = bass.Bass("TRN2", target_bir_lowering=False)

    def dram_in(name, shape):
        return nc.dram_tensor(name, list(shape), F32, kind="ExternalInput").ap()

    xT_d = dram_in("xT", [D, NT])
    xTh_d = dram_in("xTh", [D, 2])
    wmask_d = dram_in("wmask", [128, 4])
    x_d = dram_in("x", [NT, D])
    w_in = dram_in("w_in", [D, DIN])
    w_out = dram_in("w_out", [D, D])
    w_up = dram_in("w_up", [D, DFF])
    w_down = dram_in("w_down", [DFF, D])
    wg1_d = dram_in("wg1", [17, 512])
    convw_d = dram_in("convw", [128, 24])
    convg_d = dram_in("convg", [128, 8])
    glag_d = dram_in("glag", [128, 1024])
    ln1g_d = dram_in("ln1g", [128, D])
    ln1b_d = dram_in("ln1b", [128, D])
    ln2g_d = dram_in("ln2g", [128, D])
    ln2b_d = dram_in("ln2b", [128, D])
    cst_d = dram_in("cst", [128, 1024])
    out_d = nc.dram_tensor("out", [NT, D], F32, kind="ExternalOutput").ap()
    dbg = {}
    if debug:
        dbg["yT"] = nc.dram_tensor("dbg_yT", [128, 16 * NT], BF16, kind="ExternalOutput").ap()
        dbg["x1"] = nc.dram_tensor("dbg_x1", [128, 8 * D], F32, kind="ExternalOutput").ap()

    S = Sched(nc)
    A = Arena(nc)
    ps = [nc.alloc_psum_tensor("psb%d" % i, [128, 512], F32) for i in range(8)]
    bank_ctr = [0]

    def bank():
        b = bank_ctr[0]
        bank_ctr[0] = (b + 1) % 8
        return b

    def mm(out_ap, lhsT, rhs, start, stop, reads, writes):
        S.add("pe", lambda e: e.matmul(out_ap, lhsT, rhs, start=start, stop=stop), reads=reads, writes=writes)

    def mm_group(out_ap, pairs, reads, b):
        n = len(pairs)
        for i, (l, r) in enumerate(pairs):
            mm(out_ap, l, r, i == 0, i == n - 1, reads, [("ps", b)])

    def act(out, in_, func, reads, writes, **kw):
        S.add("act", lambda e: e.activation(out=out, in_=in_, func=func, **kw), reads=reads, writes=writes)

    def dve_tt(out, in0, in1, op, reads, writes):
        S.add("dve", lambda e: e.tensor_tensor(out=out, in0=in0, in1=in1, op=op), reads=reads, writes=writes)

    def dve_stt(out, in0, scalar, in1, op0, op1, reads, writes):
        S.add("dve", lambda e: e.scalar_tensor_tensor(out=out, in0=in0, scalar=scalar, in1=in1, op0=op0, op1=op1),
              reads=reads, writes=writes)

    def dve_ts(out, in0, s1, s2, op0, op1, reads, writes):
        S.add("dve", lambda e: e.tensor_scalar(out=out, in0=in0, scalar1=s1, scalar2=s2, op0=op0, op1=op1),
              reads=reads, writes=writes)

    def dve_copy(out, in_, reads, writes):
        S.add("dve", lambda e: e.tensor_copy(out=out, in_=in_), reads=reads, writes=writes)

    def load(eng, out, in_, key, writes, reads=(), **kw):
        return S.dma(eng, lambda e: e.dma_start(out=out, in_=in_, **kw), key=key, reads=reads, writes=writes)

    cst = A.alloc("cst", [128, 1024], F32)
    ident_bf = A.alloc("ident", [128, 128], BF16)
    tri_f = cst[:, 128:256]
    tri4_f = cst[:, 128:640]
    ustr_f = cst[:, 640:768]
    ones_f = cst[:, 768:896]
    wg1 = A.alloc("wg1", [17, 512], F32)
    convw = A.alloc("convw", [128, 24], F32)
    convg = A.alloc("convg", [128, 8], F32)
    Sst = A.alloc("Sst", [128, 4, 256], F32)
    halo = A.alloc("halo", [128, 16, 2], BF16)
    Wz = A.alloc("Wz", [128, 16, 16], BF16)
    fz = A.alloc("fz", [128, 8], F32)
    m_persist0 = A.mark()
    xT = A.alloc("xT", [128, 16, NT], BF16)

    load("sp", cst[:], cst_d, "cst", ["cst"])
    load("sp", wg1[:], wg1_d, "wg1", ["wg1"])
    load("sp", convw[:], convw_d, "convw", ["convw"])
    load("sp", convg[:], convg_d, "convg", ["convg"])
    dve_copy(ident_bf[:], cst[:, 0:128], ["cst"], ["ident"])
    S.add("dve", lambda e: e.memset(Sst[:].rearrange("p h e -> p (h e)"), 0.0), writes=["Sst"])

    def run_interleaved(g1, g2, n1=1, n2=1):
        d1 = g1 is None
        d2 = g2 is None
        while not (d1 and d2):
            for _ in range(n1):
                if not d1:
                    try:
                        next(g1)
                    except StopIteration:
                        d1 = True
            for _ in range(n2):
                if not d2:
                    try:
                        next(g2)
                    except StopIteration:
                        d2 = True

    top0 = A.top
    yT = A.alloc_top("yT", [128, 16, NT], BF16)
    glag = A.alloc("glag", [128, 1024], F32)
    wsl = [A.alloc("wsl%d" % i, [128, 16, 384], BF16) for i in range(2)]
    qT = A.alloc("qT", [128, 4, NT], BF16)
    kT = A.alloc("kT", [128, 4, NT], BF16)
    ktok = A.alloc("ktok", [128, 8, 512], BF16)
    vtok = A.alloc("vtok", [128, 8, 1024], BF16)
    gr = A.alloc("gr", [128, 8, 1024], BF16)
    zl1m = A.alloc("zl1m", [17, NT], F32)
    silu_t = [A.alloc("silu%d" % i, [128, 256], F32) for i in range(2)]
    Lb = A.alloc("Lb", [128, 2, 512], F32)
    E1 = A.alloc("E1", [128, 2, 512], F32)
    E2 = A.alloc("E2", [128, 2, 512], F32)
    E3 = E2
    scm = A.alloc("scm", [128, 2, 512], BF16)
    Sbf = A.alloc("Sbf", [128, 1024], BF16)
    ytok = A.alloc("ytok", [128, 2, 1024], BF16)
    ssq = A.alloc("ssq", [128, 2, 4], F32)
    rstd = A.alloc("rstd", [128, 2, 4], F32)
    decs = A.alloc("decs", [128, 8, 4], F32)
    dtot = A.alloc("dtot", [128, 4], F32)
    wmask = A.alloc("wmask", [128, 4], F32)
    xchg = A.alloc("xchg", [128, 1028], F32)
    gbuf = A.alloc("gbuf", [128, 1028], F32)
    tbuf = Lb[:].rearrange("p l c -> p (l c)")
    TB = [("Lb", 0), ("Lb", 1)]
    junk = A.alloc("junk", [128, 256], BF16)
    hbuf = A.alloc("hbuf", [128, NT + 2], F32)
    ybuf = A.alloc("ybuf", [128, NT], F32)
    sqbuf = A.alloc("sqbuf", [128, NT], F32)
    inb = nc.dram_tensor("xch_in", [128, 1028], F32)
    outb = nc.dram_tensor("xch_out", [512, 1028], F32)

    def load_xT(half, after=()):
        load("pool", xT[:, :, half * 512:(half + 1) * 512], xT_d[:, half * 512:(half + 1) * 512].rearrange("(k p) c -> p k c", p=128),
             ("xT", half), [("xT", half)], reads=list(after))
    load("sp", glag[:], glag_d, "glag", ["glag"])
    load("sp", wmask[:], wmask_d, "wmask", ["wmask"])
    S.add("dve", lambda e: e.memset(zl1m[:], 1.0), writes=["zl1m"])
    S.add("dve", lambda e: e.memset(dtot[:], 1.0), writes=["dtot"])

    wblocks = [("k", 0, COL_K, 256), ("k", 1, COL_K + 256, 256), ("q", 0, COL_Q, 256), ("q", 1, COL_Q + 256, 256)]
    wblocks += [("v", i, COL_V + 256 * i, 256) for i in range(4)]
    wblocks += [("r", 0, COL_R, 256)]
    for i in range(1, 4):
        wblocks += [("r", i, COL_R + 256 * i, 256), ("c", i - 1, None, 384)]
    wblocks += [("c", j, None, 384) for j in range(3, 8)]
    WIDX = {(b[0], b[1]): n for n, b in enumerate(wblocks)}

    def load_wblock(n):
        if n >= len(wblocks):
            return
        kind, idx, c0, ncol = wblocks[n]
        s = n % 2
        if kind == "c":
            for g, cb in enumerate((COL_B, COL_C, COL_U)):
                src = w_in[:, cb + idx * 128:cb + (idx + 1) * 128].rearrange("(k p) c -> p k c", p=128)
                load("pool", wsl[s][:, :, g * 128:(g + 1) * 128], src, ("wsl", s, g), [("wsl", s, g)])
        else:
            src = w_in[:, c0:c0 + ncol].rearrange("(k p) c -> p k c", p=128)
            load("pool", wsl[s][:, :, 0:ncol], src, ("wsl", s, 0), [("wsl", s, 0), ("wsl", s, 1), ("wsl", s, 2)])

    load_xT(0)
    load_wblock(0)
    load("pool", Wz[:], w_in[:, COL_Z:COL_Z + 16].rearrange("(k p) c -> p k c", p=128), "Wz", ["Wz"])
    load_xT(1, after=[("wsl", 0, 0)])
    load_wblock(1)
    load("pool", halo[:], xTh_d.rearrange("(k p) c -> p k c", p=128), "halo", ["halo"])

    def gla_proj_gen(n):
        kind, idx, c0, ncol = wblocks[n]
        s = n % 2
        wr = [("wsl", s, 0), ("wsl", s, 1), ("wsl", s, 2)]
        w = wsl[s]
        if kind in ("q", "k"):
            dstT = qT if kind == "q" else kT
            for half in range(2):
                for hh in range(2):
                    h = idx * 2 + hh
                    b = bank()
                    mm_group(ps[b][:, :], [(w[:, kc, hh * 128:(hh + 1) * 128], xT[:, kc, half * 512:(half + 1) * 512])
                                           for kc in range(16)], wr + [("xT", half)], b)
                    if kind == "q":
                        act(dstT[:, h, half * 512:(half + 1) * 512], ps[b][:, :], AF.Copy, [("ps", b)],
                            [("qT", h, half)], scale=QSCALE)
                    else:
                        act(dstT[:, h, half * 512:(half + 1) * 512], ps[b][:, :], AF.Copy, [("ps", b)],
                            [("kT", h, half)])
                    yield
        if kind in ("v", "r"):
            for i in range(8):
                b = bank()
                mm_group(ps[b][:, 0:256], [(xT[:, kc, i * 128:(i + 1) * 128], w[:, kc, 0:256]) for kc in range(16)],
                         wr + [("xT", i // 4)], b)
                if kind == "k":
                    dve_copy(ktok[:, i, idx * 256:(idx + 1) * 256], ps[b][:, 0:256], [("ps", b)], [("ktok", i, idx)])
                elif kind == "v":
                    dve_copy(vtok[:, i, idx * 256:(idx + 1) * 256], ps[b][:, 0:256], [("ps", b)], [("vtok", i, idx)])
                else:
                    st = silu_t[i % 2]
                    act(st[:], ps[b][:, 0:256], AF.Silu, [("ps", b)], [("silu", i % 2)])
                    dve_tt(gr[:, i, idx * 256:(idx + 1) * 256], st[:], glag[:, idx * 256:(idx + 1) * 256], ALU.mult,
                           [("silu", i % 2), "glag"], [("gr", i, idx)])
                yield
        load_wblock(n + 2)

    run_interleaved(gla_proj_gen(0), None)
    for half in range(2):
        b = bank()
        mm_group(ps[b][0:16, :], [(Wz[:, kc, :], xT[:, kc, half * 512:(half + 1) * 512]) for kc in range(16)],
                 ["Wz", ("xT", half)], b)
        dve_copy(zl1m[0:16, half * 512:(half + 1) * 512], ps[b][0:16, :], [("ps", b)], ["zl1m"])

    for n in range(1, 4):
        run_interleaved(gla_proj_gen(n), None)

    r4 = "p (h t) -> p h t"
    QK = lambda nm, i: [(nm, h, i // 4) for h in range(4)]

    def state_update(i, dec_ap_fn, bank_pair_reads):
        vr = [("vtok", i, j) for j in range(4)]
        for hp in range(2):
            b = bank()
            for hh in range(2):
                h = hp * 2 + hh
                mm(ps[b][:, hh * 256:(hh + 1) * 256], ktok[:, i, h * 128:(h + 1) * 128],
                   vtok[:, i, h * 256:(h + 1) * 256], True, True, [("ktok", i, 0), ("ktok", i, 1)] + vr, [("ps", b)])
            for hh in range(2):
                h = hp * 2 + hh
                dve_stt(Sst[:, h, :], Sst[:, h, :], dec_ap_fn(h), ps[b][:, hh * 256:(hh + 1) * 256], ALU.mult, ALU.add,
                        ["Sst", ("ps", b)] + bank_pair_reads, ["Sst"])

    def p1_a(i):
        l = i % 2
        b = bank()
        mm(ps[b][:, :], zl1m[0:17, i * 128:(i + 1) * 128], wg1[0:17, :], True, True, ["zl1m", "wg1"], [("ps", b)])
        act(Lb[:, l, :], ps[b][:, :], AF.Exp, [("ps", b)], [("Lb", l)], scale=-1.0)
        act(Lb[:, l, :], Lb[:, l, :], AF.Ln, [("Lb", l)], [("Lb", l)], bias=1.0)

    def p1_b(i):
        l = i % 2
        b = bank()
        for h in range(4):
            mm(ps[b][:, h * 128:(h + 1) * 128], Lb[:, l, h * 128:(h + 1) * 128], tri_f, True, True,
               [("Lb", l), "cst"], [("ps", b)])
        act(E1[:, l, :], ps[b][:, :], AF.Exp, [("ps", b)], [("E1", l)], scale=-GI)
        act(E2[:, l, :], ps[b][:, :], AF.Exp, [("ps", b)], [("E2", l)], scale=GI)
        sl = slice(i * 128, (i + 1) * 128)
        S.add("dve", lambda e, i=i, l=l: e.tensor_copy(out=decs[:, i, :], in_=E1[:, l, :].rearrange(r4, h=4)[:, :, 127]),
              reads=[("E1", l)], writes=[("decs", i)])
        dve_tt(dtot[:], dtot[:], decs[:, i, :], ALU.mult, ["dtot", ("decs", i)], ["dtot"])
        for h in range(4):
            dve_stt(scm[:, l, h * 128:(h + 1) * 128], kT[:, h, sl], decs[:, i, h:h + 1], E2[:, l, h * 128:(h + 1) * 128],
                    ALU.mult, ALU.mult, QK("kT", i) + [("decs", i), ("E2", l)], [("scm", l)])
        dve_tt(kT[:, :, sl], kT[:, :, sl], E2[:, l, :].rearrange(r4, h=4), ALU.mult, QK("kT", i) + [("E2", l)], QK("kT", i))
        dve_tt(qT[:, :, sl], qT[:, :, sl], E1[:, l, :].rearrange(r4, h=4), ALU.mult, QK("qT", i) + [("E1", l)], QK("qT", i))

    def p1_t(i):
        l = i % 2
        bT = bank()
        pbf = ps[bT][:].bitcast(BF16)
        for h in range(4):
            S.add("pe", lambda e, h=h, l=l, pbf=pbf: e.transpose(pbf[:, h * 128:(h + 1) * 128],
                                                                  scm[:, l, h * 128:(h + 1) * 128], ident_bf[:]),
                  reads=[("scm", l), "ident"], writes=[("ps", bT)])
        S.add("act", lambda e, i=i, pbf=pbf: e.activation(out=ktok[:, i, :], in_=pbf[:, 0:512], func=AF.Copy),
              reads=[("ps", bT)], writes=[("ktok", i, 0), ("ktok", i, 1)])

    def pass1_gen():
        for step in range(8 + 2):
            if step < 8:
                p1_a(step)
                yield
            if 0 <= step - 1 < 8:
                p1_b(step - 1)
                yield
            if 0 <= step - 2 < 8:
                p1_t(step - 2)
                yield

    def pass1_state_gen():
        for i in range(8):
            state_update(i, lambda h, i=i: decs[:, i, h:h + 1], [("decs", i)])
            yield

    def blocks_gen(n0, n1):
        for n in range(n0, n1):
            yield from gla_proj_gen(n)

    run_interleaved(pass1_gen(), blocks_gen(4, 8), 3, 4)
    run_interleaved(pass1_state_gen(), blocks_gen(8, 9), 1, 1)

    dve_copy(xchg[:, 0:1024], Sst[:].rearrange("p h e -> p (h e)"), ["Sst"], ["xchg"])
    dve_copy(xchg[:, 1024:1028], dtot[:], ["dtot", "xchg"], ["xchg"])
    load("sp", inb.ap(), xchg[:], "inb", ["inb"], reads=["xchg"])
    S.dma("pool", lambda e: e.collective_compute("AllGather", ALU.bypass, replica_groups=[[0, 1, 2, 3], [4, 5, 6, 7]],
                                                 ins=[inb.ap().opt()], outs=[outb.ap().opt()]),
          key="cc", reads=["inb"], writes=["outb"], inc=1)
    S.add("dve", lambda e: e.memset(Sst[:].rearrange("p h e -> p (h e)"), 0.0), reads=["xchg"], writes=["Sst"])
    Sflat = Sst[:].rearrange("p h e -> p (h e)")

    def combine_gen():
        for m in range(4):
            load("sp", gbuf[:], outb.ap()[m * 128:(m + 1) * 128, :], "gbuf", ["gbuf"], reads=["outb"])
            for h in range(4):
                dve_stt(tbuf[:, h * 256:(h + 1) * 256], Sst[:, h, :], gbuf[:, 1024 + h:1025 + h], gbuf[:, h * 256:(h + 1) * 256],
                        ALU.mult, ALU.add, ["Sst", "gbuf"] + TB, TB)
            dve_tt(tbuf, tbuf, Sflat, ALU.subtract, TB + ["Sst"], TB)
            dve_stt(Sflat, tbuf, wmask[:, m:m + 1], Sflat, ALU.mult, ALU.add, TB + ["Sst", "wmask"], ["Sst"])
            yield

    def pass2_gen():
        for i in range(8):
            l = i % 2
            b = bank()
            for h in range(4):
                mm(ps[b][:, h * 128:(h + 1) * 128], kT[:, h, i * 128:(i + 1) * 128], qT[:, h, i * 128:(i + 1) * 128],
                   True, True, QK("kT", i) + QK("qT", i), [("ps", b)])
            dve_tt(scm[:, l, :], ps[b][:, :], tri4_f, ALU.mult, [("ps", b), "cst"], [("scm", l)])
            yield
            vr = [("vtok", i, j) for j in range(4)]
            S.add("act", lambda e: e.activation(out=Sbf[:], in_=Sst[:].rearrange("p h e -> p (h e)"), func=AF.Copy),
                  reads=["Sst"], writes=["Sbf"])
            ob = []
            for hp in range(2):
                b = bank()
                ob.append(b)
                for hh in range(2):
                    h = hp * 2 + hh
                    o_ap = ps[b][:, hh * 256:(hh + 1) * 256]
                    mm(o_ap, scm[:, l, h * 128:(h + 1) * 128], vtok[:, i, h * 256:(h + 1) * 256], True, False,
                       [("scm", l)] + vr, [("ps", b)])
                    mm(o_ap, qT[:, h, i * 128:(i + 1) * 128], Sbf[:, h * 256:(h + 1) * 256], False, True,
                       QK("qT", i) + ["Sbf"], [("ps", b)])
            state_update(i, lambda h, i=i: decs[:, i, h:h + 1], [("decs", i)])
            for h in range(4):
                b = ob[h // 2]
                act(junk[:], ps[b][:, (h % 2) * 256:(h % 2 + 1) * 256], AF.Square, [("ps", b)], ["junk", ("ssq", l, h)],
                    accum_out=ssq[:, l, h:h + 1])
            act(rstd[:, l, :], ssq[:, l, :], AF.Ln, [("ssq", l, h) for h in range(4)], [("rstd", l)],
                scale=1.0 / 256.0, bias=RMS_EPS)
            act(rstd[:, l, :], rstd[:, l, :], AF.Exp, [("rstd", l)], [("rstd", l)], scale=-0.5)
            for h in range(4):
                b = ob[h // 2]
                dve_stt(ytok[:, l, h * 256:(h + 1) * 256], ps[b][:, (h % 2) * 256:(h % 2 + 1) * 256], rstd[:, l, h:h + 1],
                        gr[:, i, h * 256:(h + 1) * 256], ALU.mult, ALU.mult,
                        [("ps", b), ("rstd", l), ("gr", i, h)], [("ytok", l)])
            yield
            if i >= 1:
                gla_transposes(i - 1)
                yield
        gla_transposes(7)
        yield

    def gla_transposes(i):
        l = i % 2
        b = bank()
        pbf = ps[b][:].bitcast(BF16)
        for fb in range(8):
            S.add("pe", lambda e, fb=fb, l=l, pbf=pbf: e.transpose(pbf[:, fb * 128:(fb + 1) * 128],
                                                                    ytok[:, l, fb * 128:(fb + 1) * 128], ident_bf[:]),
                  reads=[("ytok", l), "ident"], writes=[("ps", b)])
        S.add("act", lambda e, i=i, pbf=pbf: e.activation(out=yT[:, 8:16, i * 128:(i + 1) * 128],
                                                           in_=pbf.rearrange("p (f t) -> p f t", f=8), func=AF.Copy),
              reads=[("ps", b)], writes=[("yT", "g", i)])

    pending_rms = []

    def conv_gen(j0, j1):
        for j in range(j0, j1):
            n = WIDX[("c", j)]
            s = n % 2
            w = wsl[s]
            wr = [("wsl", s, 0), ("wsl", s, 1), ("wsl", s, 2)]
            bh = bank()
            mm_group(ps[bh][:, 0:2], [(w[:, kc, 128:256], halo[:, kc, :]) for kc in range(16)], wr + ["halo"], bh)
            bh2 = bank()
            mm_group(ps[bh2][:, 0:2], [(w[:, kc, 256:384], halo[:, kc, :]) for kc in range(16)], wr + ["halo"], bh2)
            dve_copy(hbuf[:, 0:2], ps[bh2][:, 0:2], [("ps", bh2)], [("hbuf", 0)])
            dve_tt(hbuf[:, 0:2], ps[bh][:, 0:2], hbuf[:, 0:2], ALU.mult, [("ps", bh), ("hbuf", 0)], [("hbuf", 0)])
            yield
            for half in range(2):
                hs = slice(2 + half * 512, 2 + (half + 1) * 512)
                ts = slice(half * 512, (half + 1) * 512)
                hk = ("hbuf", 1 + half)
                xr = [("xT", half)]
                bu = bank()
                mm_group(ps[bu][:, :], [(w[:, kc, 256:384], xT[:, kc, ts]) for kc in range(16)], wr + xr, bu)
                act(hbuf[:, hs], ps[bu][:, :], AF.Copy, [("ps", bu)], [hk])
                if pending_rms:
                    pending_rms.pop(0)()
                yield
                bc = bank()
                mm_group(ps[bc][:, :], [(w[:, kc, 128:256], xT[:, kc, ts]) for kc in range(16)], wr + xr, bc)
                dve_tt(hbuf[:, hs], ps[bc][:, :], hbuf[:, hs], ALU.mult, [("ps", bc), hk], [hk])
                yield
                bb = bank()
                mm_group(ps[bb][:, :], [(w[:, kc, 0:128], xT[:, kc, ts]) for kc in range(16)], wr + xr, bb)
                hprev = [("hbuf", 0), ("hbuf", 1)] if half == 0 else [("hbuf", 1), ("hbuf", 2)]
                yk = ("ybuf", half)
                dve_ts(ybuf[:, ts], hbuf[:, hs], convw[:, 3 * j + 2:3 * j + 3], None, ALU.mult, ALU.bypass,
                       [hk, "convw"], [yk])
                dve_stt(ybuf[:, ts], hbuf[:, 1 + half * 512:1 + (half + 1) * 512], convw[:, 3 * j + 1:3 * j + 2], ybuf[:, ts],
                        ALU.mult, ALU.add, hprev + [yk, "convw"], [yk])
                dve_stt(ybuf[:, ts], hbuf[:, half * 512:(half + 1) * 512], convw[:, 3 * j:3 * j + 1], ybuf[:, ts],
                        ALU.mult, ALU.add, hprev + [yk, "convw"], [yk])
                dve_tt(ybuf[:, ts], ps[bb][:, :], ybuf[:, ts], ALU.mult, [("ps", bb), yk], [yk])
                sk = ("sqbuf", half)
                act(sqbuf[:, ts], ybuf[:, ts], AF.Square, [yk], [sk])

                def rms_stage(j=j, ts=ts, yk=yk, sk=sk, half=half):
                    br = bank()
                    mm(ps[br][:, :], ones_f, sqbuf[:, ts], True, True, [sk, "cst"], [("ps", br)])
                    act(sqbuf[:, ts], ps[br][:, :], AF.Ln, [("ps", br)], [sk], scale=1.0 / 128.0, bias=RMS_EPS)
                    act(sqbuf[:, ts], sqbuf[:, ts], AF.Exp, [sk], [sk], scale=-0.5)
                    dve_stt(yT[:, j, ts], ybuf[:, ts], convg[:, j:j + 1], sqbuf[:, ts], ALU.mult, ALU.mult,
                            [yk, sk, "convg"], [("yT", "c", j, half)])
                pending_rms.append(rms_stage)
                yield
            load_wblock(n + 2)
        while pending_rms:
            pending_rms.pop(0)()
            yield

    for i in range(1, 4):
        run_interleaved(blocks_gen(WIDX[("r", i)], WIDX[("r", i)] + 1), None)
        run_interleaved(conv_gen(i - 1, i), None)
    run_interleaved(conv_gen(3, 4), combine_gen(), 2, 1)
    A.reset(m_persist0)
    wo = [A.alloc("wo%d" % i, [128, 16, 512], BF16) for i in range(2)]
    assert A.offs["wo0"] == A.offs["xT"]
    x1 = A.alloc("x1", [128, 8, D], F32)
    x1T = A.alloc("x1T", [128, 16, NT], BF16)

    def fence(keys):
        S.add("pool", lambda e: e.memset(fz[:], 0.0), writes=list(keys) + ["fz"])

    outs = []
    g2, c2 = pass2_gen(), conv_gen(4, 8)
    g_done = c_done = False
    while not (g_done and c_done):
        if not g_done:
            try:
                next(g2)
            except StopIteration:
                g_done = True
        for _ in range(2):
            if not c_done:
                try:
                    next(c2)
                except StopIteration:
                    c_done = True
                    fence([("xT", 0), ("xT", 1), ("wo", 0), ("wo", 1)])
                    for q in range(2):
                        load("pool", wo[q][:], w_out[:, q * 512:(q + 1) * 512].rearrange("(k p) c -> p k c", p=128),
                             ("wo", q), [("wo", q)])
    if debug:
        outs.append(load("sp", dbg["yT"], yT[:].rearrange("p f t -> p (f t)"), "dbgyT", [],
                         reads=[("yT", "g", i) for i in range(8)] + [("yT", "c", j, h) for j in range(8) for h in range(2)]))
    S.fence_all("pool", lambda e: e.memset(fz[:], 0.0), "FB")

    lng = A.alloc("ln1g", [128, D], F32)
    lnb = A.alloc("ln1b", [128, D], F32)
    x1bf = [A.alloc("x1bf%d" % i, [128, D], BF16) for i in range(3)]
    x1bf.append(A.alloc("x1bfpad", [128, D], BF16))
    stats = A.alloc("stats", [128, 8, 24], F32)
    mv = A.alloc("mv", [128, 8, 2], F32)
    rs1 = A.alloc("rs1", [128, 8, 1], F32)
    rl = [A.alloc("rl0", [128, 512], F32)]
    wu = [A.alias("wu%d" % i, [128, 16, 512], BF16, A.offs["wo%d" % i]) for i in range(2)]
    wd = [A.alias("wd0", [128, 4, D], BF16, A.offs["ln1g"]), A.alias("wd1", [128, 4, D], BF16, A.offs["x1bf0"])]
    hT = [A.alias("hT%d" % i, [128, 4, NT], BF16, A.offs["yT"] + i * 8192) for i in range(2)]
    lng2 = A.alias("ln2g", [128, D], F32, A.offs["yT"] + 16384)
    lnb2 = A.alias("ln2b", [128, D], F32, A.offs["yT"] + 24576)
    assert A.offs["ln1b"] == A.offs["ln1g"] + 8192 and A.offs["x1bfpad"] == A.offs["x1bf0"] + 12288

    def fence(keys):
        S.add("pool", lambda e: e.memset(fz[:], 0.0), writes=list(keys) + ["fz"])

    for i in range(8):
        load("sp", x1[:, i, :], x_d[i * 128:(i + 1) * 128, :], ("x1", i), [("x1", i, q) for q in range(4)], reads=["FB"])
    load("sp", lng[:], ln1g_d, "lng", ["lng"], reads=["FB"])
    load("sp", lnb[:], ln1b_d, "lnb", ["lnb"], reads=["FB"])
    yT_reads = [("yT", "g", i) for i in range(8)] + [("yT", "c", j, h) for j in range(8) for h in range(2)]

    def ln_stats(i, c):
        S.add("dve", lambda e: e.bn_stats(out=stats[:, i, c * 6:(c + 1) * 6], in_=x1[:, i, c * 512:(c + 1) * 512]),
              reads=[("x1", i, c)], writes=[("stats", i, c)])

    def ln_front(i, norm_on_dve=False):
        xr = [("x1", i, q) for q in range(4)]
        S.add("dve", lambda e: e.bn_aggr(out=mv[:, i, :], in_=stats[:, i, :]),
              reads=[("stats", i, c) for c in range(4)], writes=[("mv", i)])
        act(rs1[:, i, :], mv[:, i, 1:2], AF.Ln, [("mv", i)], [("rs1", i)], bias=LN_EPS)
        act(rs1[:, i, :], rs1[:, i, :], AF.Exp, [("rs1", i)], [("rs1", i)], scale=-0.5)
        if norm_on_dve:
            dve_ts(x1[:, i, :], x1[:, i, :], mv[:, i, 0:1], rs1[:, i, 0:1], ALU.subtract, ALU.mult,
                   xr + [("mv", i), ("rs1", i)], xr)
            return
        dve_stt(mv[:, i, 1:2], mv[:, i, 0:1], -1.0, rs1[:, i, 0:1], ALU.mult, ALU.mult, [("mv", i), ("rs1", i)], [("mv", i)])
        act(x1[:, i, :], x1[:, i, :], AF.Identity, xr + [("mv", i), ("rs1", i)], xr, scale=rs1[:, i, 0:1], bias=mv[:, i, 1:2])

    def ln_back(i, g_ap, b_ap, gk, bk_, on_dve=False):
        xr = [("x1", i, q) for q in range(4)]
        dve_tt(x1[:, i, :], x1[:, i, :], g_ap, ALU.mult, xr + [gk], xr)
        if on_dve:
            dve_tt(x1[:, i, :], x1[:, i, :], b_ap, ALU.add, xr + [bk_], xr)
        else:
            S.add("pool", lambda e: e.tensor_tensor(out=x1[:, i, :], in0=x1[:, i, :], in1=b_ap, op=ALU.add),
                  reads=xr + [bk_], writes=xr)

    def layer_norm_tile(i, g_ap, b_ap, gk, bk_):
        xr = [("x1", i, q) for q in range(4)]
        for c in range(4):
            S.add("dve", lambda e, c=c: e.bn_stats(out=stats[:, i, c * 6:(c + 1) * 6], in_=x1[:, i, c * 512:(c + 1) * 512]),
                  reads=xr, writes=[("stats", i, c)])
        S.add("dve", lambda e: e.bn_aggr(out=mv[:, i, :], in_=stats[:, i, :]),
              reads=[("stats", i, c) for c in range(4)], writes=[("mv", i)])
        act(rs1[:, i, :], mv[:, i, 1:2], AF.Ln, [("mv", i)], [("rs1", i)], bias=LN_EPS)
        act(rs1[:, i, :], rs1[:, i, :], AF.Exp, [("rs1", i)], [("rs1", i)], scale=-0.5)
        dve_ts(x1[:, i, :], x1[:, i, :], mv[:, i, 0:1], rs1[:, i, 0:1], ALU.subtract, ALU.mult,
               xr + [("mv", i), ("rs1", i)], xr)
        dve_tt(x1[:, i, :], x1[:, i, :], g_ap, ALU.mult, xr + [gk], xr)
        dve_tt(x1[:, i, :], x1[:, i, :], b_ap, ALU.add, xr + [bk_], xr)

    def transposes_c(i):
        xb = x1bf[i % 4]
        for half in range(2):
            bt = bank()
            pbf = ps[bt][:].bitcast(BF16)
            for fb in range(8):
                f = half * 8 + fb
                S.add("pe", lambda e, fb=fb, f=f, pbf=pbf, xb=xb: e.transpose(pbf[:, fb * 128:(fb + 1) * 128],
                                                                               xb[:, f * 128:(f + 1) * 128], ident_bf[:]),
                      reads=[("x1bf", i % 4), "ident"], writes=[("ps", bt)])
            S.add("act", lambda e, i=i, half=half, pbf=pbf: e.activation(
                out=x1T[:, half * 8:(half + 1) * 8, i * 128:(i + 1) * 128],
                in_=pbf.rearrange("p (f t) -> p f t", f=8), func=AF.Copy),
                reads=[("ps", bt)], writes=[("x1T", i, half)])

    def mix_group(q, i):
        b = bank()
        mm_group(ps[b][:, :], [(yT[:, fc, i * 128:(i + 1) * 128], wo[q % 2][:, fc, :]) for fc in range(16)],
                 [("wo", q % 2)] + yT_reads, b)
        dve_stt(x1[:, i, q * 512:(q + 1) * 512], x1[:, i, q * 512:(q + 1) * 512], ALPHA, ps[b][:, :], ALU.mult, ALU.add,
                [("x1", i, q), ("ps", b)], [("x1", i, q)])
        ln_stats(i, q)

    for q in range(2):
        for i in range(8):
            mix_group(q, i)
        load("pool", wo[q][:], w_out[:, (q + 2) * 512:(q + 3) * 512].rearrange("(k p) c -> p k c", p=128),
             ("wo", q), [("wo", q)])
    for i in range(4):
        mix_group(2, i)
    def ln1_back(i):
        ln_back(i, lng[:], lnb[:], "lng", "lnb")
        act(x1bf[i % 4][:], x1[:, i, :], AF.Copy, [("x1", i, qq) for qq in range(4)], [("x1bf", i % 4)])

    NPART = 16
    HT_KEYS = [("hT", sl_, c, h) for sl_ in range(2) for c in range(4) for h in range(2)]

    def load_wu(p):
        for fb in range(4):
            load("pool", wu[p % 2][:, :, fb * 128:(fb + 1) * 128],
                 w_up[:, p * 512 + fb * 128:p * 512 + (fb + 1) * 128].rearrange("(k p) c -> p k c", p=128),
                 ("wu", p % 2, fb), [("wu", p % 2, fb)])

    def load_wd(p):
        load("pool", wd[p % 2][:], w_down[p * 512:(p + 1) * 512, :].rearrange("(c p) m -> p c m", p=128),
             ("wd", p % 2), [("wd", p % 2)])

    def up_bank(p, fb, half):
        b = bank()
        xr_ = [("x1T", i, hh) for i in range(4 * half, 4 * half + 4) for hh in range(2)]
        mm_group(ps[b][:, :], [(wu[p % 2][:, kc, fb * 128:(fb + 1) * 128], x1T[:, kc, half * 512:(half + 1) * 512])
                               for kc in range(16)], [("wu", p % 2, fb)] + xr_, b)
        act(rl[0][:], ps[b][:, :], AF.Relu, [("ps", b)], [("rl", 0)])
        act(hT[p % 2][:, fb, half * 512:(half + 1) * 512], rl[0][:], AF.Square, [("rl", 0)], [("hT", p % 2, fb, half)])

    def down_tile(p, i, parts=None):
        parts = parts or (p,)
        last = parts[-1] == NPART - 1
        for mb in range(4):
            b = bank()
            pairs, rd = [], []
            for pp in parts:
                pairs += [(hT[pp % 2][:, c, i * 128:(i + 1) * 128], wd[pp % 2][:, c, mb * 512:(mb + 1) * 512]) for c in range(4)]
                rd += [("wd", pp % 2)] + [("hT", pp % 2, c, i // 4) for c in range(4)]
            mm_group(ps[b][:, :], pairs, rd, b)
            xs = x1[:, i, mb * 512:(mb + 1) * 512]
            if parts[0] == 0:
                dve_stt(xs, xs, ALPHA, ps[b][:, :], ALU.mult, ALU.add, [("x1", i, mb), ("ps", b)], [("x1", i, mb)])
            else:
                dve_tt(xs, ps[b][:, :], xs, ALU.add, [("x1", i, mb), ("ps", b)], [("x1", i, mb)])
            if last:
                ln_stats(i, mb)

    for i in range(8):
        mix_group(3, i)
        if i + 4 < 8:
            mix_group(2, i + 4)
        ln_front(i)
        if i >= 1:
            ln1_back(i - 1)
        if i >= 4:
            transposes_c(i - 4)
        if i == 3:
            fence([("wo", 0)] + [("wu", 0, fb) for fb in range(4)])
            load_wu(0)
    ln1_back(7)
    if debug:
        outs.append(load("sp", dbg["x1"], x1[:].rearrange("p i d -> p (i d)"), "dbgx1", [],
                         reads=[("x1", i, q) for i in range(8) for q in range(4)]))
    fence(yT_reads + HT_KEYS + ["lng2", "lnb2"])
    fence([("wo", 1)] + [("wu", 1, fb) for fb in range(4)])
    fence(["lng", "lnb", ("wd", 0)])
    load_wd(0)
    load_wu(1)
    transposes_c(4)
    transposes_c(5)
    up_bank(0, 0, 0)
    transposes_c(6)
    up_bank(0, 1, 0)
    transposes_c(7)
    up_bank(0, 2, 0)
    fence([("x1bf", 0), ("x1bf", 1), ("x1bf", 2), ("x1bf", 3), ("wd", 1)])
    load_wd(1)
    load("sp", lng2[:], ln2g_d, "lng2", ["lng2"], reads=[("wd", 1)])
    load("sp", lnb2[:], ln2b_d, "lnb2", ["lnb2"], reads=[("wd", 1)])
    up_bank(0, 3, 0)
    for fb in range(4):
        up_bank(0, fb, 1)

    def ln2_back(i):
        ln_back(i, lng2[:], lnb2[:], "lng2", "lnb2", on_dve=(i >= 6))
        outs.append(load("sp", out_d[i * 128:(i + 1) * 128, :], x1[:, i, :], ("out", i), [],
                         reads=[("x1", i, q) for q in range(4)]))

    for p in range(NPART - 2):
        for i in range(8):
            down_tile(p, i)
            up_bank(p + 1, i // 2, i % 2)
        load_wu(p + 2)
        load_wd(p + 2)
    for fb in range(4):
        for half in range(2):
            up_bank(NPART - 1, fb, half)
    for i in range(8):
        down_tile(None, i, parts=(NPART - 2, NPART - 1))
        ln_front(i)
        if i >= 1:
            ln2_back(i - 1)
    ln2_back(7)
    S.flush(outs)
    return nc


def _consts():
    c = np.zeros((128, 1024), np.float32)
    c[:, 0:128] = np.eye(128, dtype=np.float32)
    s = np.arange(128)
    tri = (s[:, None] <= s[None, :]).astype(np.float32)
    for h in range(4):
        c[:, 128 + h * 128:128 + (h + 1) * 128] = tri
    c[:, 640:768] = (s[:, None] > s[None, :]).astype(np.float32)
    c[:, 768:896] = 1.0
    return c


def make_in_maps(x, w_in, conv_w, conv_norm_g, w_gate_up, gate_bias, gla_norm_g, w_out,
                 ln1_g, ln1_b, w_ff_up, w_ff_down, ln2_g, ln2_b):
    f = np.float32
    x = np.asarray(x, f)
    B, SEQ, _ = x.shape
    shared = {
        "w_in": np.ascontiguousarray(np.asarray(w_in, f)[0]),
        "w_out": np.ascontiguousarray(np.asarray(w_out, f)[0]),
        "w_up": np.ascontiguousarray(np.asarray(w_ff_up, f)[0]),
        "w_down": np.ascontiguousarray(np.asarray(w_ff_down, f)[0]),
        "wg1": np.ascontiguousarray(np.concatenate([np.asarray(w_gate_up, f)[0], np.asarray(gate_bias, f)[0][None, :]], 0)),
        "convw": np.ascontiguousarray(np.asarray(conv_w, f)[0].reshape(3, 8, 128).transpose(2, 1, 0).reshape(128, 24)),
        "convg": np.ascontiguousarray(np.asarray(conv_norm_g, f)[0].reshape(8, 128).T),
        "glag": np.ascontiguousarray(np.broadcast_to(np.asarray(gla_norm_g, f)[0][None, :], (128, 1024))),
        "ln1g": np.ascontiguousarray(np.broadcast_to(np.asarray(ln1_g, f)[0][None, :], (128, D))),
        "ln1b": np.ascontiguousarray(np.broadcast_to(np.asarray(ln1_b, f)[0][None, :], (128, D))),
        "ln2g": np.ascontiguousarray(np.broadcast_to(np.asarray(ln2_g, f)[0][None, :], (128, D))),
        "ln2b": np.ascontiguousarray(np.broadcast_to(np.asarray(ln2_b, f)[0][None, :], (128, D))),
        "cst": _consts(),
    }
    in_maps = []
    nq = SEQ // NT
    for c in range(8):
        b, j = c // nq, c % nq
        s0 = j * NT
        hal = np.zeros((2, D), f)
        if s0 > 0:
            hal[:] = x[b, s0 - 2:s0]
        m = dict(shared)
        m["x"] = np.ascontiguousarray(x[b, s0:s0 + NT])
        m["xT"] = np.ascontiguousarray(x[b, s0:s0 + NT].T)
        m["xTh"] = np.ascontiguousarray(hal.T)
        wm = np.zeros((128, 4), f)
        wm[:, :j] = 1.0
        m["wmask"] = wm
        in_maps.append(m)
    return in_maps


def kernel(**inputs):
    in_maps = make_in_maps(**inputs)
    nc = build_program()
    res = run_bass_kernel_spmd(nc, in_maps, core_ids=list(range(8)))
    x = inputs["x"]
    B, SEQ, _ = x.shape
    out = np.empty((B, SEQ, D), np.float32)
    nq = SEQ // NT
    for c in range(8):
        b, j = c // nq, c % nq
        out[b, j * NT:(j + 1) * NT] = np.asarray(res.results[c]["out"], np.float32)
    return out
```

```python
import numpy as np
import concourse.bass as bass
import concourse.mybir as mybir
from concourse.bass_utils import run_bass_kernel_spmd

F32 = mybir.dt.float32
BF16 = mybir.dt.bfloat16
ALU = mybir.AluOpType
AF = mybir.ActivationFunctionType

ENGS = ("pe", "act", "dve", "pool", "sp")

D = 2048
NT = 1024
NPRE = 3072
DFF = 8192
COL_B, COL_C, COL_U, COL_Q, COL_K, COL_V, COL_R, COL_Z = 0, 1024, 2048, 3072, 3584, 4096, 5120, 6144
DIN = 6160
ALPHA = 2.0 ** 0.25
LN_EPS = 1e-5
RMS_EPS = 1e-6
QSCALE = 128.0 ** -0.5
GI = 1.0 / 16.0


class Op:
    __slots__ = ("eng", "fn", "deps", "dma_sem", "dma_val", "milestone", "signals", "inc", "seq")
    _ctr = [0]

    def __init__(self, eng, fn):
        self.eng = eng
        self.fn = fn
        self.deps = ()
        self.dma_sem = None
        self.dma_val = 0
        self.milestone = 0
        self.signals = False
        Op._ctr[0] += 1
        self.seq = Op._ctr[0]


class Sched:
    def __init__(self, nc):
        self.nc = nc
        self.ops = {e: [] for e in ENGS}
        self.last_writer = {}
        self.readers = {}
        self.dma_cnt = {}
        self.dma_semh = {}
        self.eng_semh = None
        self.eng_cnt = {e: 0 for e in ENGS}
        self.nflush = 0

    def _deps(self, op, reads, writes):
        deps = {}
        lw = self.last_writer
        for r in reads:
            w = lw.get(r)
            if w is not None:
                deps[id(w)] = w
        for r in writes:
            w = lw.get(r)
            if w is not None:
                deps[id(w)] = w
            for rd in self.readers.get(r, ()):
                deps[id(rd)] = rd
        deps.pop(id(op), None)
        if op.eng == "pe" and op.dma_sem is None:
            deps = {k: d for k, d in deps.items() if not (d.eng == "pe" and d.dma_sem is None)}
        keep, young = [], {}
        for d in deps.values():
            if d.dma_sem is not None:
                keep.append(d)
            elif d.eng not in young or d.seq > young[d.eng].seq:
                young[d.eng] = d
        op.deps = tuple(keep) + tuple(young.values())
        for d in op.deps:
            d.signals = True
        for r in reads:
            self.readers.setdefault(r, []).append(op)
        for r in writes:
            lw[r] = op
            self.readers[r] = []

    def add(self, eng, fn, reads=(), writes=()):
        op = Op(eng, fn)
        self._deps(op, reads, writes)
        self.ops[eng].append(op)
        return op

    def dma(self, eng, fn, key, reads=(), writes=(), inc=16):
        op = Op(eng, fn)
        op.inc = inc
        self.dma_cnt[key] = self.dma_cnt.get(key, 0) + inc
        op.dma_sem = key
        op.dma_val = self.dma_cnt[key]
        self._deps(op, reads, writes)
        self.ops[eng].append(op)
        return op

    def fence_all(self, eng, fn, key):
        op = Op(eng, fn)
        deps = {}
        for e in ENGS:
            if self.ops[e]:
                d = self.ops[e][-1]
                deps[id(d)] = d
        w = self.last_writer.get(key)
        if w is not None:
            deps[id(w)] = w
        op.deps = tuple(deps.values())
        for d in op.deps:
            d.signals = True
        self.last_writer[key] = op
        self.readers[key] = []
        self.ops[eng].append(op)
        return op

    def flush(self, final_wait_ops=()):
        nc = self.nc
        if self.eng_semh is None:
            self.eng_semh = {e: nc.alloc_semaphore(name="ms_" + e) for e in ENGS}
        for k in self.dma_cnt:
            if k not in self.dma_semh:
                self.dma_semh[k] = nc.alloc_semaphore(name="dq_%d" % len(self.dma_semh))
        for e in ENGS:
            c = self.eng_cnt[e]
            for op in self.ops[e]:
                if op.dma_sem is None and op.signals and op.milestone == 0:
                    c += 1
                    op.milestone = c
            self.eng_cnt[e] = c
        ops_snapshot = {e: self.ops[e] for e in ENGS}
        esem, dsem = self.eng_semh, self.dma_semh

        def stream(e):
            def body(eng):
                waited = {}
                for op in ops_snapshot[e]:
                    need = {}
                    for d in op.deps:
                        if d.dma_sem is not None:
                            k = ("d", d.dma_sem)
                            v = d.dma_val
                        else:
                            k = ("e", d.eng)
                            v = d.milestone
                            if v <= 0:
                                continue
                        if v > need.get(k, 0):
                            need[k] = v
                    for k, v in need.items():
                        if waited.get(k, 0) >= v:
                            continue
                        waited[k] = v
                        eng.wait_ge(dsem[k[1]] if k[0] == "d" else esem[k[1]], v)
                    inst = op.fn(eng)
                    if op.dma_sem is not None:
                        inst.then_inc(dsem[op.dma_sem], op.inc)
                    elif op.signals:
                        inst.then_inc(esem[e], 1)
                if e == "sp":
                    for op in final_wait_ops:
                        eng.wait_ge(dsem[op.dma_sem], op.dma_val)
            return body

        with nc.Block() as block:
            block.tensor(stream("pe"))
            block.scalar(stream("act"))
            block.vector(stream("dve"))
            block.gpsimd(stream("pool"))
            block.sync(stream("sp"))
        for e in ENGS:
            for op in self.ops[e]:
                if op.dma_sem is None and not op.signals:
                    op.milestone = -1
        self.ops = {e: [] for e in ENGS}
        self.nflush += 1


class Arena:
    def __init__(self, nc, base=16512, top=229344):
        self.nc = nc
        self.cur = base
        self.top = top
        self.n = 0
        self.peak = base
        self.offs = {}

    def alloc(self, name, shape, dtype):
        esz = 4 if dtype == F32 else 2
        nbytes = esz
        for s in shape[1:]:
            nbytes *= s
        off = (self.cur + 63) // 64 * 64
        assert off + nbytes <= self.top, (name, off, nbytes, self.top)
        self.cur = off + nbytes
        self.peak = max(self.peak, self.cur)
        self.n += 1
        self.offs[name] = off
        return self.nc.alloc_sbuf_tensor_at("%s_%d" % (name, self.n), list(shape), dtype, offset=off)

    def alias(self, name, shape, dtype, off):
        self.n += 1
        return self.nc.alloc_sbuf_tensor_at("%s_%d" % (name, self.n), list(shape), dtype, offset=off)

    def alloc_top(self, name, shape, dtype):
        esz = 4 if dtype == F32 else 2
        nbytes = esz
        for s in shape[1:]:
            nbytes *= s
        off = (self.top - nbytes) // 64 * 64
        assert off >= self.cur, (name, off, self.cur)
        self.top = off
        self.n += 1
        self.offs[name] = off
        return self.nc.alloc_sbuf_tensor_at("%s_%d" % (name, self.n), list(shape), dtype, offset=off)

    def mark(self):
        return self.cur

    def reset(self, m):
        self.cur = m


def build_program(debug=False):
    nc = bass.Bass("TRN2", target_bir_lowering=False)

    def dram_in(name, shape):
        return nc.dram_tensor(name, list(shape), F32, kind="ExternalInput").ap()

    xT_d = dram_in("xT", [D, NT])
    xTh_d = dram_in("xTh", [D, 2])
    wmask_d = dram_in("wmask", [128, 4])
    x_d = dram_in("x", [NT, D])
    w_in = dram_in("w_in", [D, DIN])
    w_out = dram_in("w_out", [D, D])
    w_up = dram_in("w_up", [D, DFF])
    w_down = dram_in("w_down", [DFF, D])
    wg1_d = dram_in("wg1", [17, 512])
    convw_d = dram_in("convw", [128, 24])
    convg_d = dram_in("convg", [128, 8])
    glag_d = dram_in("glag", [128, 1024])
    ln1g_d = dram_in("ln1g", [128, D])
    ln1b_d = dram_in("ln1b", [128, D])
    ln2g_d = dram_in("ln2g", [128, D])
    ln2b_d = dram_in("ln2b", [128, D])
    cst_d = dram_in("cst", [128, 1024])
    out_d = nc.dram_tensor("out", [NT, D], F32, kind="ExternalOutput").ap()
    dbg = {}
    if debug:
        dbg["yT"] = nc.dram_tensor("dbg_yT", [128, 16 * NT], BF16, kind="ExternalOutput").ap()
        dbg["x1"] = nc.dram_tensor("dbg_x1", [128, 8 * D], F32, kind="ExternalOutput").ap()

    S = Sched(nc)
    A = Arena(nc)
    ps = [nc.alloc_psum_tensor("psb%d" % i, [128, 512], F32) for i in range(8)]
    bank_ctr = [0]

    def bank():
        b = bank_ctr[0]
        bank_ctr[0] = (b + 1) % 8
        return b

    def mm(out_ap, lhsT, rhs, start, stop, reads, writes):
        S.add("pe", lambda e: e.matmul(out_ap, lhsT, rhs, start=start, stop=stop), reads=reads, writes=writes)

    def mm_group(out_ap, pairs, reads, b):
        n = len(pairs)
        for i, (l, r) in enumerate(pairs):
            mm(out_ap, l, r, i == 0, i == n - 1, reads, [("ps", b)])

    def act(out, in_, func, reads, writes, **kw):
        S.add("act", lambda e: e.activation(out=out, in_=in_, func=func, **kw), reads=reads, writes=writes)

    def dve_tt(out, in0, in1, op, reads, writes):
        S.add("dve", lambda e: e.tensor_tensor(out=out, in0=in0, in1=in1, op=op), reads=reads, writes=writes)

    def dve_stt(out, in0, scalar, in1, op0, op1, reads, writes):
        S.add("dve", lambda e: e.scalar_tensor_tensor(out=out, in0=in0, scalar=scalar, in1=in1, op0=op0, op1=op1),
              reads=reads, writes=writes)

    def dve_ts(out, in0, s1, s2, op0, op1, reads, writes):
        S.add("dve", lambda e: e.tensor_scalar(out=out, in0=in0, scalar1=s1, scalar2=s2, op0=op0, op1=op1),
              reads=reads, writes=writes)

    def dve_copy(out, in_, reads, writes):
        S.add("dve", lambda e: e.tensor_copy(out=out, in_=in_), reads=reads, writes=writes)

    def load(eng, out, in_, key, writes, reads=(), **kw):
        return S.dma(eng, lambda e: e.dma_start(out=out, in_=in_, **kw), key=key, reads=reads, writes=writes)

    cst = A.alloc("cst", [128, 1024], F32)
    ident_bf = A.alloc("ident", [128, 128], BF16)
    tri_f = cst[:, 128:256]
    tri4_f = cst[:, 128:640]
    ustr_f = cst[:, 640:768]
    ones_f = cst[:, 768:896]
    wg1 = A.alloc("wg1", [17, 512], F32)
    convw = A.alloc("convw", [128, 24], F32)
    convg = A.alloc("convg", [128, 8], F32)
    Sst = A.alloc("Sst", [128, 4, 256], F32)
    halo = A.alloc("halo", [128, 16, 2], BF16)
    Wz = A.alloc("Wz", [128, 16, 16], BF16)
    fz = A.alloc("fz", [128, 8], F32)
    m_persist0 = A.mark()
    xT = A.alloc("xT", [128, 16, NT], BF16)

    load("sp", cst[:], cst_d, "cst", ["cst"])
    load("sp", wg1[:], wg1_d, "wg1", ["wg1"])
    load("sp", convw[:], convw_d, "convw", ["convw"])
    load("sp", convg[:], convg_d, "convg", ["convg"])
    dve_copy(ident_bf[:], cst[:, 0:128], ["cst"], ["ident"])
    S.add("dve", lambda e: e.memset(Sst[:].rearrange("p h e -> p (h e)"), 0.0), writes=["Sst"])

    def run_interleaved(g1, g2, n1=1, n2=1):
        d1 = g1 is None
        d2 = g2 is None
        while not (d1 and d2):
            for _ in range(n1):
                if not d1:
                    try:
                        next(g1)
                    except StopIteration:
                        d1 = True
            for _ in range(n2):
                if not d2:
                    try:
                        next(g2)
                    except StopIteration:
                        d2 = True

    top0 = A.top
    yT = A.alloc_top("yT", [128, 16, NT], BF16)
    glag = A.alloc("glag", [128, 1024], F32)
    wsl = [A.alloc("wsl%d" % i, [128, 16, 384], BF16) for i in range(2)]
    qT = A.alloc("qT", [128, 4, NT], BF16)
    kT = A.alloc("kT", [128, 4, NT], BF16)
    ktok = A.alloc("ktok", [128, 8, 512], BF16)
    vtok = A.alloc("vtok", [128, 8, 1024], BF16)
    gr = A.alloc("gr", [128, 8, 1024], BF16)
    zl1m = A.alloc("zl1m", [17, NT], F32)
    silu_t = [A.alloc("silu%d" % i, [128, 256], F32) for i in range(2)]
    Lb = A.alloc("Lb", [128, 2, 512], F32)
    E1 = A.alloc("E1", [128, 2, 512], F32)
    E2 = A.alloc("E2", [128, 2, 512], F32)
    E3 = E2
    scm = A.alloc("scm", [128, 2, 512], BF16)
    Sbf = A.alloc("Sbf", [128, 1024], BF16)
    ytok = A.alloc("ytok", [128, 2, 1024], BF16)
    ssq = A.alloc("ssq", [128, 2, 4], F32)
    rstd = A.alloc("rstd", [128, 2, 4], F32)
    decs = A.alloc("decs", [128, 8, 4], F32)
    dtot = A.alloc("dtot", [128, 4], F32)
    wmask = A.alloc("wmask", [128, 4], F32)
    xchg = A.alloc("xchg", [128, 1028], F32)
    gbuf = A.alloc("gbuf", [128, 1028], F32)
    tbuf = Lb[:].rearrange("p l c -> p (l c)")
    TB = [("Lb", 0), ("Lb", 1)]
    junk = A.alloc("junk", [128, 256], BF16)
    hbuf = A.alloc("hbuf", [128, NT + 2], F32)
    ybuf = A.alloc("ybuf", [128, NT], F32)
    sqbuf = A.alloc("sqbuf", [128, NT], F32)
    inb = nc.dram_tensor("xch_in", [128, 1028], F32)
    outb = nc.dram_tensor("xch_out", [512, 1028], F32)

    def load_xT(half, after=()):
        load("pool", xT[:, :, half * 512:(half + 1) * 512], xT_d[:, half * 512:(half + 1) * 512].rearrange("(k p) c -> p k c", p=128),
             ("xT", half), [("xT", half)], reads=list(after))
    load("sp", glag[:], glag_d, "glag", ["glag"])
    load("sp", wmask[:], wmask_d, "wmask", ["wmask"])
    S.add("dve", lambda e: e.memset(zl1m[:], 1.0), writes=["zl1m"])
    S.add("dve", lambda e: e.memset(dtot[:], 1.0), writes=["dtot"])

    wblocks = [("k", 0, COL_K, 256), ("k", 1, COL_K + 256, 256), ("q", 0, COL_Q, 256), ("q", 1, COL_Q + 256, 256)]
    wblocks += [("v", i, COL_V + 256 * i, 256) for i in range(4)]
    wblocks += [("r", 0, COL_R, 256)]
    for i in range(1, 4):
        wblocks += [("r", i, COL_R + 256 * i, 256), ("c", i - 1, None, 384)]
    wblocks += [("c", j, None, 384) for j in range(3, 8)]
    WIDX = {(b[0], b[1]): n for n, b in enumerate(wblocks)}

    def load_wblock(n):
        if n >= len(wblocks):
            return
        kind, idx, c0, ncol = wblocks[n]
        s = n % 2
        if kind == "c":
            for g, cb in enumerate((COL_B, COL_C, COL_U)):
                src = w_in[:, cb + idx * 128:cb + (idx + 1) * 128].rearrange("(k p) c -> p k c", p=128)
                load("pool", wsl[s][:, :, g * 128:(g + 1) * 128], src, ("wsl", s, g), [("wsl", s, g)])
        else:
            src = w_in[:, c0:c0 + ncol].rearrange("(k p) c -> p k c", p=128)
            load("pool", wsl[s][:, :, 0:ncol], src, ("wsl", s, 0), [("wsl", s, 0), ("wsl", s, 1), ("wsl", s, 2)])

    load_xT(0)
    load_wblock(0)
    load("pool", Wz[:], w_in[:, COL_Z:COL_Z + 16].rearrange("(k p) c -> p k c", p=128), "Wz", ["Wz"])
    load_xT(1, after=[("wsl", 0, 0)])
    load_wblock(1)
    load("pool", halo[:], xTh_d.rearrange("(k p) c -> p k c", p=128), "halo", ["halo"])

    def gla_proj_gen(n):
        kind, idx, c0, ncol = wblocks[n]
        s = n % 2
        wr = [("wsl", s, 0), ("wsl", s, 1), ("wsl", s, 2)]
        w = wsl[s]
        if kind in ("q", "k"):
            dstT = qT if kind == "q" else kT
            for half in range(2):
                for hh in range(2):
                    h = idx * 2 + hh
                    b = bank()
                    mm_group(ps[b][:, :], [(w[:, kc, hh * 128:(hh + 1) * 128], xT[:, kc, half * 512:(half + 1) * 512])
                                           for kc in range(16)], wr + [("xT", half)], b)
                    if kind == "q":
                        act(dstT[:, h, half * 512:(half + 1) * 512], ps[b][:, :], AF.Copy, [("ps", b)],
                            [("qT", h, half)], scale=QSCALE)
                    else:
                        act(dstT[:, h, half * 512:(half + 1) * 512], ps[b][:, :], AF.Copy, [("ps", b)],
                            [("kT", h, half)])
                    yield
        if kind in ("v", "r"):
            for i in range(8):
                b = bank()
                mm_group(ps[b][:, 0:256], [(xT[:, kc, i * 128:(i + 1) * 128], w[:, kc, 0:256]) for kc in range(16)],
                         wr + [("xT", i // 4)], b)
                if kind == "k":
                    dve_copy(ktok[:, i, idx * 256:(idx + 1) * 256], ps[b][:, 0:256], [("ps", b)], [("ktok", i, idx)])
                elif kind == "v":
                    dve_copy(vtok[:, i, idx * 256:(idx + 1) * 256], ps[b][:, 0:256], [("ps", b)], [("vtok", i, idx)])
                else:
                    st = silu_t[i % 2]
                    act(st[:], ps[b][:, 0:256], AF.Silu, [("ps", b)], [("silu", i % 2)])
                    dve_tt(gr[:, i, idx * 256:(idx + 1) * 256], st[:], glag[:, idx * 256:(idx + 1) * 256], ALU.mult,
                           [("silu", i % 2), "glag"], [("gr", i, idx)])
                yield
        load_wblock(n + 2)

    run_interleaved(gla_proj_gen(0), None)
    for half in range(2):
        b = bank()
        mm_group(ps[b][0:16, :], [(Wz[:, kc, :], xT[:, kc, half * 512:(half + 1) * 512]) for kc in range(16)],
                 ["Wz", ("xT", half)], b)
        dve_copy(zl1m[0:16, half * 512:(half + 1) * 512], ps[b][0:16, :], [("ps", b)], ["zl1m"])

    for n in range(1, 4):
        run_interleaved(gla_proj_gen(n), None)

    r4 = "p (h t) -> p h t"
    QK = lambda nm, i: [(nm, h, i // 4) for h in range(4)]

    def state_update(i, dec_ap_fn, bank_pair_reads):
        vr = [("vtok", i, j) for j in range(4)]
        for hp in range(2):
            b = bank()
            for hh in range(2):
                h = hp * 2 + hh
                mm(ps[b][:, hh * 256:(hh + 1) * 256], ktok[:, i, h * 128:(h + 1) * 128],
                   vtok[:, i, h * 256:(h + 1) * 256], True, True, [("ktok", i, 0), ("ktok", i, 1)] + vr, [("ps", b)])
            for hh in range(2):
                h = hp * 2 + hh
                dve_stt(Sst[:, h, :], Sst[:, h, :], dec_ap_fn(h), ps[b][:, hh * 256:(hh + 1) * 256], ALU.mult, ALU.add,
                        ["Sst", ("ps", b)] + bank_pair_reads, ["Sst"])

    def p1_a(i):
        l = i % 2
        b = bank()
        mm(ps[b][:, :], zl1m[0:17, i * 128:(i + 1) * 128], wg1[0:17, :], True, True, ["zl1m", "wg1"], [("ps", b)])
        act(Lb[:, l, :], ps[b][:, :], AF.Exp, [("ps", b)], [("Lb", l)], scale=-1.0)
        act(Lb[:, l, :], Lb[:, l, :], AF.Ln, [("Lb", l)], [("Lb", l)], bias=1.0)

    def p1_b(i):
        l = i % 2
        b = bank()
        for h in range(4):
            mm(ps[b][:, h * 128:(h + 1) * 128], Lb[:, l, h * 128:(h + 1) * 128], tri_f, True, True,
               [("Lb", l), "cst"], [("ps", b)])
        act(E1[:, l, :], ps[b][:, :], AF.Exp, [("ps", b)], [("E1", l)], scale=-GI)
        act(E2[:, l, :], ps[b][:, :], AF.Exp, [("ps", b)], [("E2", l)], scale=GI)
        sl = slice(i * 128, (i + 1) * 128)
        S.add("dve", lambda e, i=i, l=l: e.tensor_copy(out=decs[:, i, :], in_=E1[:, l, :].rearrange(r4, h=4)[:, :, 127]),
              reads=[("E1", l)], writes=[("decs", i)])
        dve_tt(dtot[:], dtot[:], decs[:, i, :], ALU.mult, ["dtot", ("decs", i)], ["dtot"])
        for h in range(4):
            dve_stt(scm[:, l, h * 128:(h + 1) * 128], kT[:, h, sl], decs[:, i, h:h + 1], E2[:, l, h * 128:(h + 1) * 128],
                    ALU.mult, ALU.mult, QK("kT", i) + [("decs", i), ("E2", l)], [("scm", l)])
        dve_tt(kT[:, :, sl], kT[:, :, sl], E2[:, l, :].rearrange(r4, h=4), ALU.mult, QK("kT", i) + [("E2", l)], QK("kT", i))
        dve_tt(qT[:, :, sl], qT[:, :, sl], E1[:, l, :].rearrange(r4, h=4), ALU.mult, QK("qT", i) + [("E1", l)], QK("qT", i))

    def p1_t(i):
        l = i % 2
        bT = bank()
        pbf = ps[bT][:].bitcast(BF16)
        for h in range(4):
            S.add("pe", lambda e, h=h, l=l, pbf=pbf: e.transpose(pbf[:, h * 128:(h + 1) * 128],
                                                                  scm[:, l, h * 128:(h + 1) * 128], ident_bf[:]),
                  reads=[("scm", l), "ident"], writes=[("ps", bT)])
        S.add("act", lambda e, i=i, pbf=pbf: e.activation(out=ktok[:, i, :], in_=pbf[:, 0:512], func=AF.Copy),
              reads=[("ps", bT)], writes=[("ktok", i, 0), ("ktok", i, 1)])

    def pass1_gen():
        for step in range(8 + 2):
            if step < 8:
                p1_a(step)
                yield
            if 0 <= step - 1 < 8:
                p1_b(step - 1)
                yield
            if 0 <= step - 2 < 8:
                p1_t(step - 2)
                yield

    def pass1_state_gen():
        for i in range(8):
            state_update(i, lambda h, i=i: decs[:, i, h:h + 1], [("decs", i)])
            yield

    def blocks_gen(n0, n1):
        for n in range(n0, n1):
            yield from gla_proj_gen(n)

    run_interleaved(pass1_gen(), blocks_gen(4, 8), 3, 4)
    run_interleaved(pass1_state_gen(), blocks_gen(8, 9), 1, 1)

    dve_copy(xchg[:, 0:1024], Sst[:].rearrange("p h e -> p (h e)"), ["Sst"], ["xchg"])
    dve_copy(xchg[:, 1024:1028], dtot[:], ["dtot", "xchg"], ["xchg"])
    load("sp", inb.ap(), xchg[:], "inb", ["inb"], reads=["xchg"])
    S.dma("pool", lambda e: e.collective_compute("AllGather", ALU.bypass, replica_groups=[[0, 1, 2, 3], [4, 5, 6, 7]],
                                                 ins=[inb.ap().opt()], outs=[outb.ap().opt()]),
          key="cc", reads=["inb"], writes=["outb"], inc=1)
    S.add("dve", lambda e: e.memset(Sst[:].rearrange("p h e -> p (h e)"), 0.0), reads=["xchg"], writes=["Sst"])
    Sflat = Sst[:].rearrange("p h e -> p (h e)")

    def combine_gen():
        for m in range(4):
            load("sp", gbuf[:], outb.ap()[m * 128:(m + 1) * 128, :], "gbuf", ["gbuf"], reads=["outb"])
            for h in range(4):
                dve_stt(tbuf[:, h * 256:(h + 1) * 256], Sst[:, h, :], gbuf[:, 1024 + h:1025 + h], gbuf[:, h * 256:(h + 1) * 256],
                        ALU.mult, ALU.add, ["Sst", "gbuf"] + TB, TB)
            dve_tt(tbuf, tbuf, Sflat, ALU.subtract, TB + ["Sst"], TB)
            dve_stt(Sflat, tbuf, wmask[:, m:m + 1], Sflat, ALU.mult, ALU.add, TB + ["Sst", "wmask"], ["Sst"])
            yield

    def pass2_gen():
        for i in range(8):
            l = i % 2
            b = bank()
            for h in range(4):
                mm(ps[b][:, h * 128:(h + 1) * 128], kT[:, h, i * 128:(i + 1) * 128], qT[:, h, i * 128:(i + 1) * 128],
                   True, True, QK("kT", i) + QK("qT", i), [("ps", b)])
            dve_tt(scm[:, l, :], ps[b][:, :], tri4_f, ALU.mult, [("ps", b), "cst"], [("scm", l)])
            yield
            vr = [("vtok", i, j) for j in range(4)]
            S.add("act", lambda e: e.activation(out=Sbf[:], in_=Sst[:].rearrange("p h e -> p (h e)"), func=AF.Copy),
                  reads=["Sst"], writes=["Sbf"])
            ob = []
            for hp in range(2):
                b = bank()
                ob.append(b)
                for hh in range(2):
                    h = hp * 2 + hh
                    o_ap = ps[b][:, hh * 256:(hh + 1) * 256]
                    mm(o_ap, scm[:, l, h * 128:(h + 1) * 128], vtok[:, i, h * 256:(h + 1) * 256], True, False,
                       [("scm", l)] + vr, [("ps", b)])
                    mm(o_ap, qT[:, h, i * 128:(i + 1) * 128], Sbf[:, h * 256:(h + 1) * 256], False, True,
                       QK("qT", i) + ["Sbf"], [("ps", b)])
            state_update(i, lambda h, i=i: decs[:, i, h:h + 1], [("decs", i)])
            for h in range(4):
                b = ob[h // 2]
                act(junk[:], ps[b][:, (h % 2) * 256:(h % 2 + 1) * 256], AF.Square, [("ps", b)], ["junk", ("ssq", l, h)],
                    accum_out=ssq[:, l, h:h + 1])
            act(rstd[:, l, :], ssq[:, l, :], AF.Ln, [("ssq", l, h) for h in range(4)], [("rstd", l)],
                scale=1.0 / 256.0, bias=RMS_EPS)
            act(rstd[:, l, :], rstd[:, l, :], AF.Exp, [("rstd", l)], [("rstd", l)], scale=-0.5)
            for h in range(4):
                b = ob[h // 2]
                dve_stt(ytok[:, l, h * 256:(h + 1) * 256], ps[b][:, (h % 2) * 256:(h % 2 + 1) * 256], rstd[:, l, h:h + 1],
                        gr[:, i, h * 256:(h + 1) * 256], ALU.mult, ALU.mult,
                        [("ps", b), ("rstd", l), ("gr", i, h)], [("ytok", l)])
            yield
            if i >= 1:
                gla_transposes(i - 1)
                yield
        gla_transposes(7)
        yield

    def gla_transposes(i):
        l = i % 2
        b = bank()
        pbf = ps[b][:].bitcast(BF16)
        for fb in range(8):
            S.add("pe", lambda e, fb=fb, l=l, pbf=pbf: e.transpose(pbf[:, fb * 128:(fb + 1) * 128],
                                                                    ytok[:, l, fb * 128:(fb + 1) * 128], ident_bf[:]),
                  reads=[("ytok", l), "ident"], writes=[("ps", b)])
        S.add("act", lambda e, i=i, pbf=pbf: e.activation(out=yT[:, 8:16, i * 128:(i + 1) * 128],
                                                           in_=pbf.rearrange("p (f t) -> p f t", f=8), func=AF.Copy),
              reads=[("ps", b)], writes=[("yT", "g", i)])

    pending_rms = []

    def conv_gen(j0, j1):
        for j in range(j0, j1):
            n = WIDX[("c", j)]
            s = n % 2
            w = wsl[s]
            wr = [("wsl", s, 0), ("wsl", s, 1), ("wsl", s, 2)]
            bh = bank()
            mm_group(ps[bh][:, 0:2], [(w[:, kc, 128:256], halo[:, kc, :]) for kc in range(16)], wr + ["halo"], bh)
            bh2 = bank()
            mm_group(ps[bh2][:, 0:2], [(w[:, kc, 256:384], halo[:, kc, :]) for kc in range(16)], wr + ["halo"], bh2)
            dve_copy(hbuf[:, 0:2], ps[bh2][:, 0:2], [("ps", bh2)], [("hbuf", 0)])
            dve_tt(hbuf[:, 0:2], ps[bh][:, 0:2], hbuf[:, 0:2], ALU.mult, [("ps", bh), ("hbuf", 0)], [("hbuf", 0)])
            yield
            for half in range(2):
                hs = slice(2 + half * 512, 2 + (half + 1) * 512)
                ts = slice(half * 512, (half + 1) * 512)
                hk = ("hbuf", 1 + half)
                xr = [("xT", half)]
                bu = bank()
                mm_group(ps[bu][:, :], [(w[:, kc, 256:384], xT[:, kc, ts]) for kc in range(16)], wr + xr, bu)
                act(hbuf[:, hs], ps[bu][:, :], AF.Copy, [("ps", bu)], [hk])
                if pending_rms:
                    pending_rms.pop(0)()
                yield
                bc = bank()
                mm_group(ps[bc][:, :], [(w[:, kc, 128:256], xT[:, kc, ts]) for kc in range(16)], wr + xr, bc)
                dve_tt(hbuf[:, hs], ps[bc][:, :], hbuf[:, hs], ALU.mult, [("ps", bc), hk], [hk])
                yield
                bb = bank()
                mm_group(ps[bb][:, :], [(w[:, kc, 0:128], xT[:, kc, ts]) for kc in range(16)], wr + xr, bb)
                hprev = [("hbuf", 0), ("hbuf", 1)] if half == 0 else [("hbuf", 1), ("hbuf", 2)]
                yk = ("ybuf", half)
                dve_ts(ybuf[:, ts], hbuf[:, hs], convw[:, 3 * j + 2:3 * j + 3], None, ALU.mult, ALU.bypass,
                       [hk, "convw"], [yk])
                dve_stt(ybuf[:, ts], hbuf[:, 1 + half * 512:1 + (half + 1) * 512], convw[:, 3 * j + 1:3 * j + 2], ybuf[:, ts],
                        ALU.mult, ALU.add, hprev + [yk, "convw"], [yk])
                dve_stt(ybuf[:, ts], hbuf[:, half * 512:(half + 1) * 512], convw[:, 3 * j:3 * j + 1], ybuf[:, ts],
                        ALU.mult, ALU.add, hprev + [yk, "convw"], [yk])
                dve_tt(ybuf[:, ts], ps[bb][:, :], ybuf[:, ts], ALU.mult, [("ps", bb), yk], [yk])
                sk = ("sqbuf", half)
                act(sqbuf[:, ts], ybuf[:, ts], AF.Square, [yk], [sk])

                def rms_stage(j=j, ts=ts, yk=yk, sk=sk, half=half):
                    br = bank()
                    mm(ps[br][:, :], ones_f, sqbuf[:, ts], True, True, [sk, "cst"], [("ps", br)])
                    act(sqbuf[:, ts], ps[br][:, :], AF.Ln, [("ps", br)], [sk], scale=1.0 / 128.0, bias=RMS_EPS)
                    act(sqbuf[:, ts], sqbuf[:, ts], AF.Exp, [sk], [sk], scale=-0.5)
                    dve_stt(yT[:, j, ts], ybuf[:, ts], convg[:, j:j + 1], sqbuf[:, ts], ALU.mult, ALU.mult,
                            [yk, sk, "convg"], [("yT", "c", j, half)])
                pending_rms.append(rms_stage)
                yield
            load_wblock(n + 2)
        while pending_rms:
            pending_rms.pop(0)()
            yield

    for i in range(1, 4):
        run_interleaved(blocks_gen(WIDX[("r", i)], WIDX[("r", i)] + 1), None)
        run_interleaved(conv_gen(i - 1, i), None)
    run_interleaved(conv_gen(3, 4), combine_gen(), 2, 1)
    A.reset(m_persist0)
    wo = [A.alloc("wo%d" % i, [128, 16, 512], BF16) for i in range(2)]
    assert A.offs["wo0"] == A.offs["xT"]
    x1 = A.alloc("x1", [128, 8, D], F32)
    x1T = A.alloc("x1T", [128, 16, NT], BF16)

    def fence(keys):
        S.add("pool", lambda e: e.memset(fz[:], 0.0), writes=list(keys) + ["fz"])

    outs = []
    g2, c2 = pass2_gen(), conv_gen(4, 8)
    g_done = c_done = False
    while not (g_done and c_done):
        if not g_done:
            try:
                next(g2)
            except StopIteration:
                g_done = True
        for _ in range(2):
            if not c_done:
                try:
                    next(c2)
                except StopIteration:
                    c_done = True
                    fence([("xT", 0), ("xT", 1), ("wo", 0), ("wo", 1)])
                    for q in range(2):
                        load("pool", wo[q][:], w_out[:, q * 512:(q + 1) * 512].rearrange("(k p) c -> p k c", p=128),
                             ("wo", q), [("wo", q)])
    if debug:
        outs.append(load("sp", dbg["yT"], yT[:].rearrange("p f t -> p (f t)"), "dbgyT", [],
                         reads=[("yT", "g", i) for i in range(8)] + [("yT", "c", j, h) for j in range(8) for h in range(2)]))
    S.fence_all("pool", lambda e: e.memset(fz[:], 0.0), "FB")

    lng = A.alloc("ln1g", [128, D], F32)
    lnb = A.alloc("ln1b", [128, D], F32)
    x1bf = [A.alloc("x1bf%d" % i, [128, D], BF16) for i in range(3)]
    A.alloc("x1bfpad", [128, D], BF16)
    stats = A.alloc("stats", [128, 8, 24], F32)
    mv = A.alloc("mv", [128, 8, 2], F32)
    rs1 = A.alloc("rs1", [128, 8, 1], F32)
    rl = [A.alloc("rl0", [128, 512], F32)]
    wu = [A.alias("wu%d" % i, [128, 16, 512], BF16, A.offs["wo%d" % i]) for i in range(2)]
    wd = [A.alias("wd0", [128, 4, D], BF16, A.offs["ln1g"]), A.alias("wd1", [128, 4, D], BF16, A.offs["x1bf0"])]
    hT = [A.alias("hT%d" % i, [128, 4, NT], BF16, A.offs["yT"] + i * 8192) for i in range(2)]
    lng2 = A.alias("ln2g", [128, D], F32, A.offs["yT"] + 16384)
    lnb2 = A.alias("ln2b", [128, D], F32, A.offs["yT"] + 24576)
    assert A.offs["ln1b"] == A.offs["ln1g"] + 8192 and A.offs["x1bfpad"] == A.offs["x1bf0"] + 12288

    def fence(keys):
        S.add("pool", lambda e: e.memset(fz[:], 0.0), writes=list(keys) + ["fz"])

    for i in range(8):
        load("sp", x1[:, i, :], x_d[i * 128:(i + 1) * 128, :], ("x1", i), [("x1", i, q) for q in range(4)], reads=["FB"])
    load("sp", lng[:], ln1g_d, "lng", ["lng"], reads=["FB"])
    load("sp", lnb[:], ln1b_d, "lnb", ["lnb"], reads=["FB"])
    yT_reads = [("yT", "g", i) for i in range(8)] + [("yT", "c", j, h) for j in range(8) for h in range(2)]

    def ln_stats(i, c):
        S.add("dve", lambda e: e.bn_stats(out=stats[:, i, c * 6:(c + 1) * 6], in_=x1[:, i, c * 512:(c + 1) * 512]),
              reads=[("x1", i, c)], writes=[("stats", i, c)])

    def ln_front(i, norm_on_dve=False):
        xr = [("x1", i, q) for q in range(4)]
        S.add("dve", lambda e: e.bn_aggr(out=mv[:, i, :], in_=stats[:, i, :]),
              reads=[("stats", i, c) for c in range(4)], writes=[("mv", i)])
        act(rs1[:, i, :], mv[:, i, 1:2], AF.Ln, [("mv", i)], [("rs1", i)], bias=LN_EPS)
        act(rs1[:, i, :], rs1[:, i, :], AF.Exp, [("rs1", i)], [("rs1", i)], scale=-0.5)
        if norm_on_dve:
            dve_ts(x1[:, i, :], x1[:, i, :], mv[:, i, 0:1], rs1[:, i, 0:1], ALU.subtract, ALU.mult,
                   xr + [("mv", i), ("rs1", i)], xr)
            return
        dve_stt(mv[:, i, 1:2], mv[:, i, 0:1], -1.0, rs1[:, i, 0:1], ALU.mult, ALU.mult, [("mv", i), ("rs1", i)], [("mv", i)])
        act(x1[:, i, :], x1[:, i, :], AF.Identity, xr + [("mv", i), ("rs1", i)], xr, scale=rs1[:, i, 0:1], bias=mv[:, i, 1:2])

    def ln_back(i, g_ap, b_ap, gk, bk_, on_dve=False):
        xr = [("x1", i, q) for q in range(4)]
        dve_tt(x1[:, i, :], x1[:, i, :], g_ap, ALU.mult, xr + [gk], xr)
        if on_dve:
            dve_tt(x1[:, i, :], x1[:, i, :], b_ap, ALU.add, xr + [bk_], xr)
        else:
            S.add("pool", lambda e: e.tensor_tensor(out=x1[:, i, :], in0=x1[:, i, :], in1=b_ap, op=ALU.add),
                  reads=xr + [bk_], writes=xr)

    def layer_norm_tile(i, g_ap, b_ap, gk, bk_):
        xr = [("x1", i, q) for q in range(4)]
        for c in range(4):
            S.add("dve", lambda e, c=c: e.bn_stats(out=stats[:, i, c * 6:(c + 1) * 6], in_=x1[:, i, c * 512:(c + 1) * 512]),
                  reads=xr, writes=[("stats", i, c)])
        S.add("dve", lambda e: e.bn_aggr(out=mv[:, i, :], in_=stats[:, i, :]),
              reads=[("stats", i, c) for c in range(4)], writes=[("mv", i)])
        act(rs1[:, i, :], mv[:, i, 1:2], AF.Ln, [("mv", i)], [("rs1", i)], bias=LN_EPS)
        act(rs1[:, i, :], rs1[:, i, :], AF.Exp, [("rs1", i)], [("rs1", i)], scale=-0.5)
        dve_ts(x1[:, i, :], x1[:, i, :], mv[:, i, 0:1], rs1[:, i, 0:1], ALU.subtract, ALU.mult,
               xr + [("mv", i), ("rs1", i)], xr)
        dve_tt(x1[:, i, :], x1[:, i, :], g_ap, ALU.mult, xr + [gk], xr)
        dve_tt(x1[:, i, :], x1[:, i, :], b_ap, ALU.add, xr + [bk_], xr)

    def transposes_c(i):
        xb = x1bf[i % 3]
        for half in range(2):
            bt = bank()
            pbf = ps[bt][:].bitcast(BF16)
            for fb in range(8):
                f = half * 8 + fb
                S.add("pe", lambda e, fb=fb, f=f, pbf=pbf, xb=xb: e.transpose(pbf[:, fb * 128:(fb + 1) * 128],
                                                                               xb[:, f * 128:(f + 1) * 128], ident_bf[:]),
                      reads=[("x1bf", i % 3), "ident"], writes=[("ps", bt)])
            S.add("act", lambda e, i=i, half=half, pbf=pbf: e.activation(
                out=x1T[:, half * 8:(half + 1) * 8, i * 128:(i + 1) * 128],
                in_=pbf.rearrange("p (f t) -> p f t", f=8), func=AF.Copy),
                reads=[("ps", bt)], writes=[("x1T", i, half)])

    def mix_group(q, i):
        b = bank()
        mm_group(ps[b][:, :], [(yT[:, fc, i * 128:(i + 1) * 128], wo[q % 2][:, fc, :]) for fc in range(16)],
                 [("wo", q % 2)] + yT_reads, b)
        dve_stt(x1[:, i, q * 512:(q + 1) * 512], x1[:, i, q * 512:(q + 1) * 512], ALPHA, ps[b][:, :], ALU.mult, ALU.add,
                [("x1", i, q), ("ps", b)], [("x1", i, q)])
        ln_stats(i, q)

    for q in range(2):
        for i in range(8):
            mix_group(q, i)
        load("pool", wo[q][:], w_out[:, (q + 2) * 512:(q + 3) * 512].rearrange("(k p) c -> p k c", p=128),
             ("wo", q), [("wo", q)])
    for i in range(4):
        mix_group(2, i)
    def ln1_back(i):
        ln_back(i, lng[:], lnb[:], "lng", "lnb")
        act(x1bf[i % 3][:], x1[:, i, :], AF.Copy, [("x1", i, qq) for qq in range(4)], [("x1bf", i % 3)])

    NPART = 16
    HT_KEYS = [("hT", sl_, c, h) for sl_ in range(2) for c in range(4) for h in range(2)]

    def load_wu(p):
        for fb in range(4):
            load("pool", wu[p % 2][:, :, fb * 128:(fb + 1) * 128],
                 w_up[:, p * 512 + fb * 128:p * 512 + (fb + 1) * 128].rearrange("(k p) c -> p k c", p=128),
                 ("wu", p % 2, fb), [("wu", p % 2, fb)])

    def load_wd(p):
        load("pool", wd[p % 2][:], w_down[p * 512:(p + 1) * 512, :].rearrange("(c p) m -> p c m", p=128),
             ("wd", p % 2), [("wd", p % 2)])

    def up_bank(p, fb, half):
        b = bank()
        xr_ = [("x1T", i, hh) for i in range(4 * half, 4 * half + 4) for hh in range(2)]
        mm_group(ps[b][:, :], [(wu[p % 2][:, kc, fb * 128:(fb + 1) * 128], x1T[:, kc, half * 512:(half + 1) * 512])
                               for kc in range(16)], [("wu", p % 2, fb)] + xr_, b)
        act(rl[0][:], ps[b][:, :], AF.Relu, [("ps", b)], [("rl", 0)])
        act(hT[p % 2][:, fb, half * 512:(half + 1) * 512], rl[0][:], AF.Square, [("rl", 0)], [("hT", p % 2, fb, half)])

    def down_tile(p, i, parts=None):
        parts = parts or (p,)
        last = parts[-1] == NPART - 1
        for mb in range(4):
            b = bank()
            pairs, rd = [], []
            for pp in parts:
                pairs += [(hT[pp % 2][:, c, i * 128:(i + 1) * 128], wd[pp % 2][:, c, mb * 512:(mb + 1) * 512]) for c in range(4)]
                rd += [("wd", pp % 2)] + [("hT", pp % 2, c, i // 4) for c in range(4)]
            mm_group(ps[b][:, :], pairs, rd, b)
            xs = x1[:, i, mb * 512:(mb + 1) * 512]
            if parts[0] == 0:
                dve_stt(xs, xs, ALPHA, ps[b][:, :], ALU.mult, ALU.add, [("x1", i, mb), ("ps", b)], [("x1", i, mb)])
            else:
                dve_tt(xs, ps[b][:, :], xs, ALU.add, [("x1", i, mb), ("ps", b)], [("x1", i, mb)])
            if last:
                ln_stats(i, mb)

    for i in range(8):
        mix_group(3, i)
        if i + 4 < 8:
            mix_group(2, i + 4)
        ln_front(i)
        if i >= 1:
            ln1_back(i - 1)
        if i >= 3:
            transposes_c(i - 3)
        if i == 3:
            fence([("wo", 0)] + [("wu", 0, fb) for fb in range(4)])
            load_wu(0)
    ln1_back(7)
    if debug:
        outs.append(load("sp", dbg["x1"], x1[:].rearrange("p i d -> p (i d)"), "dbgx1", [],
                         reads=[("x1", i, q) for i in range(8) for q in range(4)]))
    fence(yT_reads + HT_KEYS + ["lng2", "lnb2"])
    fence([("wo", 1)] + [("wu", 1, fb) for fb in range(4)])
    fence(["lng", "lnb", ("wd", 0)])
    load_wd(0)
    load_wu(1)
    up_bank(0, 0, 0)
    up_bank(0, 1, 0)
    transposes_c(5)
    up_bank(0, 2, 0)
    transposes_c(6)
    transposes_c(7)
    fence([("x1bf", 0), ("x1bf", 1), ("x1bf", 2), ("wd", 1)])
    load_wd(1)
    load("sp", lng2[:], ln2g_d, "lng2", ["lng2"], reads=[("wd", 1)])
    load("sp", lnb2[:], ln2b_d, "lnb2", ["lnb2"], reads=[("wd", 1)])
    up_bank(0, 3, 0)
    for fb in range(4):
        up_bank(0, fb, 1)

    def ln2_back(i):
        ln_back(i, lng2[:], lnb2[:], "lng2", "lnb2", on_dve=(i >= 6))
        outs.append(load("sp", out_d[i * 128:(i + 1) * 128, :], x1[:, i, :], ("out", i), [],
                         reads=[("x1", i, q) for q in range(4)]))

    for p in range(NPART - 2):
        for i in range(8):
            down_tile(p, i)
            up_bank(p + 1, i // 2, i % 2)
        load_wu(p + 2)
        load_wd(p + 2)
    for fb in range(4):
        for half in range(2):
            up_bank(NPART - 1, fb, half)
    for i in range(8):
        down_tile(None, i, parts=(NPART - 2, NPART - 1))
        ln_front(i)
        if i >= 1:
            ln2_back(i - 1)
    ln2_back(7)
    S.flush(outs)
    return nc


def _consts():
    c = np.zeros((128, 1024), np.float32)
    c[:, 0:128] = np.eye(128, dtype=np.float32)
    s = np.arange(128)
    tri = (s[:, None] <= s[None, :]).astype(np.float32)
    for h in range(4):
        c[:, 128 + h * 128:128 + (h + 1) * 128] = tri
    c[:, 640:768] = (s[:, None] > s[None, :]).astype(np.float32)
    c[:, 768:896] = 1.0
    return c


def make_in_maps(x, w_in, conv_w, conv_norm_g, w_gate_up, gate_bias, gla_norm_g, w_out,
                 ln1_g, ln1_b, w_ff_up, w_ff_down, ln2_g, ln2_b):
    f = np.float32
    x = np.asarray(x, f)
    B, SEQ, _ = x.shape
    shared = {
        "w_in": np.ascontiguousarray(np.asarray(w_in, f)[0]),
        "w_out": np.ascontiguousarray(np.asarray(w_out, f)[0]),
        "w_up": np.ascontiguousarray(np.asarray(w_ff_up, f)[0]),
        "w_down": np.ascontiguousarray(np.asarray(w_ff_down, f)[0]),
        "wg1": np.ascontiguousarray(np.concatenate([np.asarray(w_gate_up, f)[0], np.asarray(gate_bias, f)[0][None, :]], 0)),
        "convw": np.ascontiguousarray(np.asarray(conv_w, f)[0].reshape(3, 8, 128).transpose(2, 1, 0).reshape(128, 24)),
        "convg": np.ascontiguousarray(np.asarray(conv_norm_g, f)[0].reshape(8, 128).T),
        "glag": np.ascontiguousarray(np.broadcast_to(np.asarray(gla_norm_g, f)[0][None, :], (128, 1024))),
        "ln1g": np.ascontiguousarray(np.broadcast_to(np.asarray(ln1_g, f)[0][None, :], (128, D))),
        "ln1b": np.ascontiguousarray(np.broadcast_to(np.asarray(ln1_b, f)[0][None, :], (128, D))),
        "ln2g": np.ascontiguousarray(np.broadcast_to(np.asarray(ln2_g, f)[0][None, :], (128, D))),
        "ln2b": np.ascontiguousarray(np.broadcast_to(np.asarray(ln2_b, f)[0][None, :], (128, D))),
        "cst": _consts(),
    }
    in_maps = []
    nq = SEQ // NT
    for c in range(8):
        b, j = c // nq, c % nq
        s0 = j * NT
        hal = np.zeros((2, D), f)
        if s0 > 0:
            hal[:] = x[b, s0 - 2:s0]
        m = dict(shared)
        m["x"] = np.ascontiguousarray(x[b, s0:s0 + NT])
        m["xT"] = np.ascontiguousarray(x[b, s0:s0 + NT].T)
        m["xTh"] = np.ascontiguousarray(hal.T)
        wm = np.zeros((128, 4), f)
        wm[:, :j] = 1.0
        m["wmask"] = wm
        in_maps.append(m)
    return in_maps


def kernel(**inputs):
    in_maps = make_in_maps(**inputs)
    nc = build_program()
    res = run_bass_kernel_spmd(nc, in_maps, core_ids=list(range(8)))
    x = inputs["x"]
    B, SEQ, _ = x.shape
    out = np.empty((B, SEQ, D), np.float32)
    nq = SEQ // NT
    for c in range(8):
        b, j = c // nq, c % nq
        out[b, j * NT:(j + 1) * NT] = np.asarray(res.results[c]["out"], np.float32)
    return out
```

```python
import numpy as np
import concourse.bass as bass
import concourse.mybir as mybir
from concourse.bass_utils import run_bass_kernel_spmd

F32 = mybir.dt.float32
BF16 = mybir.dt.bfloat16
ALU = mybir.AluOpType
AF = mybir.ActivationFunctionType

ENGS = ("pe", "act", "dve", "pool", "sp")

D = 2048
NT = 1024
NPRE = 3072
DFF = 8192
COL_B, COL_C, COL_U, COL_Q, COL_K, COL_V, COL_R, COL_Z = 0, 1024, 2048, 3072, 3584, 4096, 5120, 6144
DIN = 6160
ALPHA = 2.0 ** 0.25
LN_EPS = 1e-5
RMS_EPS = 1e-6
QSCALE = 128.0 ** -0.5
GI = 1.0 / 16.0


class Op:
    __slots__ = ("eng", "fn", "deps", "dma_sem", "dma_val", "milestone", "signals", "inc", "seq")
    _ctr = [0]

    def __init__(self, eng, fn):
        self.eng = eng
        self.fn = fn
        self.deps = ()
        self.dma_sem = None
        self.dma_val = 0
        self.milestone = 0
        self.signals = False
        Op._ctr[0] += 1
        self.seq = Op._ctr[0]


class Sched:
    def __init__(self, nc):
        self.nc = nc
        self.ops = {e: [] for e in ENGS}
        self.last_writer = {}
        self.readers = {}
        self.dma_cnt = {}
        self.dma_semh = {}
        self.eng_semh = None
        self.eng_cnt = {e: 0 for e in ENGS}
        self.nflush = 0

    def _deps(self, op, reads, writes):
        deps = {}
        lw = self.last_writer
        for r in reads:
            w = lw.get(r)
            if w is not None:
                deps[id(w)] = w
        for r in writes:
            w = lw.get(r)
            if w is not None:
                deps[id(w)] = w
            for rd in self.readers.get(r, ()):
                deps[id(rd)] = rd
        deps.pop(id(op), None)
        if op.eng == "pe" and op.dma_sem is None:
            deps = {k: d for k, d in deps.items() if not (d.eng == "pe" and d.dma_sem is None)}
        keep, young = [], {}
        for d in deps.values():
            if d.dma_sem is not None:
                keep.append(d)
            elif d.eng not in young or d.seq > young[d.eng].seq:
                young[d.eng] = d
        op.deps = tuple(keep) + tuple(young.values())
        for d in op.deps:
            d.signals = True
        for r in reads:
            self.readers.setdefault(r, []).append(op)
        for r in writes:
            lw[r] = op
            self.readers[r] = []

    def add(self, eng, fn, reads=(), writes=()):
        op = Op(eng, fn)
        self._deps(op, reads, writes)
        self.ops[eng].append(op)
        return op

    def dma(self, eng, fn, key, reads=(), writes=(), inc=16):
        op = Op(eng, fn)
        op.inc = inc
        self.dma_cnt[key] = self.dma_cnt.get(key, 0) + inc
        op.dma_sem = key
        op.dma_val = self.dma_cnt[key]
        self._deps(op, reads, writes)
        self.ops[eng].append(op)
        return op

    def fence_all(self, eng, fn, key):
        op = Op(eng, fn)
        deps = {}
        for e in ENGS:
            if self.ops[e]:
                d = self.ops[e][-1]
                deps[id(d)] = d
        w = self.last_writer.get(key)
        if w is not None:
            deps[id(w)] = w
        op.deps = tuple(deps.values())
        for d in op.deps:
            d.signals = True
        self.last_writer[key] = op
        self.readers[key] = []
        self.ops[eng].append(op)
        return op

    def flush(self, final_wait_ops=()):
        nc = self.nc
        if self.eng_semh is None:
            self.eng_semh = {e: nc.alloc_semaphore(name="ms_" + e) for e in ENGS}
        for k in self.dma_cnt:
            if k not in self.dma_semh:
                self.dma_semh[k] = nc.alloc_semaphore(name="dq_%d" % len(self.dma_semh))
        for e in ENGS:
            c = self.eng_cnt[e]
            for op in self.ops[e]:
                if op.dma_sem is None and op.signals and op.milestone == 0:
                    c += 1
                    op.milestone = c
            self.eng_cnt[e] = c
        ops_snapshot = {e: self.ops[e] for e in ENGS}
        esem, dsem = self.eng_semh, self.dma_semh

        def stream(e):
            def body(eng):
                waited = {}
                for op in ops_snapshot[e]:
                    need = {}
                    for d in op.deps:
                        if d.dma_sem is not None:
                            k = ("d", d.dma_sem)
                            v = d.dma_val
                        else:
                            k = ("e", d.eng)
                            v = d.milestone
                            if v <= 0:
                                continue
                        if v > need.get(k, 0):
                            need[k] = v
                    for k, v in need.items():
                        if waited.get(k, 0) >= v:
                            continue
                        waited[k] = v
                        eng.wait_ge(dsem[k[1]] if k[0] == "d" else esem[k[1]], v)
                    inst = op.fn(eng)
                    if op.dma_sem is not None:
                        inst.then_inc(dsem[op.dma_sem], op.inc)
                    elif op.signals:
                        inst.then_inc(esem[e], 1)
                if e == "sp":
                    for op in final_wait_ops:
                        eng.wait_ge(dsem[op.dma_sem], op.dma_val)
            return body

        with nc.Block() as block:
            block.tensor(stream("pe"))
            block.scalar(stream("act"))
            block.vector(stream("dve"))
            block.gpsimd(stream("pool"))
            block.sync(stream("sp"))
        for e in ENGS:
            for op in self.ops[e]:
                if op.dma_sem is None and not op.signals:
                    op.milestone = -1
        self.ops = {e: [] for e in ENGS}
        self.nflush += 1


class Arena:
    def __init__(self, nc, base=16512, top=229344):
        self.nc = nc
        self.cur = base
        self.top = top
        self.n = 0
        self.peak = base
        self.offs = {}

    def alloc(self, name, shape, dtype):
        esz = 4 if dtype == F32 else 2
        nbytes = esz
        for s in shape[1:]:
            nbytes *= s
        off = (self.cur + 63) // 64 * 64
        assert off + nbytes <= self.top, (name, off, nbytes, self.top)
        self.cur = off + nbytes
        self.peak = max(self.peak, self.cur)
        self.n += 1
        self.offs[name] = off
        return self.nc.alloc_sbuf_tensor_at("%s_%d" % (name, self.n), list(shape), dtype, offset=off)

    def alias(self, name, shape, dtype, off):
        self.n += 1
        return self.nc.alloc_sbuf_tensor_at("%s_%d" % (name, self.n), list(shape), dtype, offset=off)

    def alloc_top(self, name, shape, dtype):
        esz = 4 if dtype == F32 else 2
        nbytes = esz
        for s in shape[1:]:
            nbytes *= s
        off = (self.top - nbytes) // 64 * 64
        assert off >= self.cur, (name, off, self.cur)
        self.top = off
        self.n += 1
        self.offs[name] = off
        return self.nc.alloc_sbuf_tensor_at("%s_%d" % (name, self.n), list(shape), dtype, offset=off)

    def mark(self):
        return self.cur

    def reset(self, m):
        self.cur = m


def build_program(debug=False):
    nc = bass.Bass("TRN2", target_bir_lowering=False)

    def dram_in(name, shape):
        return nc.dram_tensor(name, list(shape), F32, kind="ExternalInput").ap()

    xT_d = dram_in("xT", [D, NT])
    xTh_d = dram_in("xTh", [D, 2])
    wmask_d = dram_in("wmask", [128, 4])
    x_d = dram_in("x", [NT, D])
    w_in = dram_in("w_in", [D, DIN])
    w_out = dram_in("w_out", [D, D])
    w_up = dram_in("w_up", [D, DFF])
    w_down = dram_in("w_down", [DFF, D])
    wg1_d = dram_in("wg1", [17, 512])
    convw_d = dram_in("convw", [128, 24])
    convg_d = dram_in("convg", [128, 8])
    glag_d = dram_in("glag", [128, 1024])
    ln1g_d = dram_in("ln1g", [128, D])
    ln1b_d = dram_in("ln1b", [128, D])
    ln2g_d = dram_in("ln2g", [128, D])
    ln2b_d = dram_in("ln2b", [128, D])
    cst_d = dram_in("cst", [128, 1024])
    out_d = nc.dram_tensor("out", [NT, D], F32, kind="ExternalOutput").ap()
    dbg = {}
    if debug:
        dbg["yT"] = nc.dram_tensor("dbg_yT", [128, 16 * NT], BF16, kind="ExternalOutput").ap()
        dbg["x1"] = nc.dram_tensor("dbg_x1", [128, 8 * D], F32, kind="ExternalOutput").ap()

    S = Sched(nc)
    A = Arena(nc)
    ps = [nc.alloc_psum_tensor("psb%d" % i, [128, 512], F32) for i in range(8)]
    bank_ctr = [0]

    def bank():
        b = bank_ctr[0]
        bank_ctr[0] = (b + 1) % 8
        return b

    def mm(out_ap, lhsT, rhs, start, stop, reads, writes):
        S.add("pe", lambda e: e.matmul(out_ap, lhsT, rhs, start=start, stop=stop), reads=reads, writes=writes)

    def mm_group(out_ap, pairs, reads, b):
        n = len(pairs)
        for i, (l, r) in enumerate(pairs):
            mm(out_ap, l, r, i == 0, i == n - 1, reads, [("ps", b)])

    def act(out, in_, func, reads, writes, **kw):
        S.add("act", lambda e: e.activation(out=out, in_=in_, func=func, **kw), reads=reads, writes=writes)

    def dve_tt(out, in0, in1, op, reads, writes):
        S.add("dve", lambda e: e.tensor_tensor(out=out, in0=in0, in1=in1, op=op), reads=reads, writes=writes)

    def dve_stt(out, in0, scalar, in1, op0, op1, reads, writes):
        S.add("dve", lambda e: e.scalar_tensor_tensor(out=out, in0=in0, scalar=scalar, in1=in1, op0=op0, op1=op1),
              reads=reads, writes=writes)

    def dve_ts(out, in0, s1, s2, op0, op1, reads, writes):
        S.add("dve", lambda e: e.tensor_scalar(out=out, in0=in0, scalar1=s1, scalar2=s2, op0=op0, op1=op1),
              reads=reads, writes=writes)

    def dve_copy(out, in_, reads, writes):
        S.add("dve", lambda e: e.tensor_copy(out=out, in_=in_), reads=reads, writes=writes)

    def load(eng, out, in_, key, writes, reads=(), **kw):
        return S.dma(eng, lambda e: e.dma_start(out=out, in_=in_, **kw), key=key, reads=reads, writes=writes)

    cst = A.alloc("cst", [128, 1024], F32)
    ident_bf = A.alloc("ident", [128, 128], BF16)
    tri_f = cst[:, 128:256]
    tri4_f = cst[:, 128:640]
    ustr_f = cst[:, 640:768]
    ones_f = cst[:, 768:896]
    wg1 = A.alloc("wg1", [17, 512], F32)
    convw = A.alloc("convw", [128, 24], F32)
    convg = A.alloc("convg", [128, 8], F32)
    Sst = A.alloc("Sst", [128, 4, 256], F32)
    halo = A.alloc("halo", [128, 16, 2], BF16)
    Wz = A.alloc("Wz", [128, 16, 16], BF16)
    fz = A.alloc("fz", [128, 8], F32)
    m_persist0 = A.mark()
    xT = A.alloc("xT", [128, 16, NT], BF16)

    load("sp", cst[:], cst_d, "cst", ["cst"])
    load("sp", wg1[:], wg1_d, "wg1", ["wg1"])
    load("sp", convw[:], convw_d, "convw", ["convw"])
    load("sp", convg[:], convg_d, "convg", ["convg"])
    dve_copy(ident_bf[:], cst[:, 0:128], ["cst"], ["ident"])
    S.add("dve", lambda e: e.memset(Sst[:].rearrange("p h e -> p (h e)"), 0.0), writes=["Sst"])

    def run_interleaved(g1, g2, n1=1, n2=1):
        d1 = g1 is None
        d2 = g2 is None
        while not (d1 and d2):
            for _ in range(n1):
                if not d1:
                    try:
                        next(g1)
                    except StopIteration:
                        d1 = True
            for _ in range(n2):
                if not d2:
                    try:
                        next(g2)
                    except StopIteration:
                        d2 = True

    top0 = A.top
    yT = A.alloc_top("yT", [128, 16, NT], BF16)
    glag = A.alloc("glag", [128, 1024], F32)
    wsl = [A.alloc("wsl%d" % i, [128, 16, 384], BF16) for i in range(2)]
    qT = A.alloc("qT", [128, 4, NT], BF16)
    kT = A.alloc("kT", [128, 4, NT], BF16)
    ktok = A.alloc("ktok", [128, 8, 512], BF16)
    vtok = A.alloc("vtok", [128, 8, 1024], BF16)
    gr = A.alloc("gr", [128, 8, 1024], BF16)
    zl1m = A.alloc("zl1m", [17, NT], F32)
    silu_t = [A.alloc("silu%d" % i, [128, 256], F32) for i in range(2)]
    Lb = A.alloc("Lb", [128, 2, 512], F32)
    E1 = A.alloc("E1", [128, 2, 512], F32)
    E2 = A.alloc("E2", [128, 2, 512], F32)
    E3 = E2
    scm = A.alloc("scm", [128, 2, 512], BF16)
    Sbf = A.alloc("Sbf", [128, 1024], BF16)
    ytok = A.alloc("ytok", [128, 2, 1024], BF16)
    ssq = A.alloc("ssq", [128, 2, 4], F32)
    rstd = A.alloc("rstd", [128, 2, 4], F32)
    decs = A.alloc("decs", [128, 8, 4], F32)
    dtot = A.alloc("dtot", [128, 4], F32)
    wmask = A.alloc("wmask", [128, 4], F32)
    xchg = A.alloc("xchg", [128, 1028], F32)
    gbuf = A.alloc("gbuf", [128, 1028], F32)
    tbuf = Lb[:].rearrange("p l c -> p (l c)")
    TB = [("Lb", 0), ("Lb", 1)]
    junk = A.alloc("junk", [128, 256], BF16)
    hbuf = A.alloc("hbuf", [128, NT + 2], F32)
    ybuf = A.alloc("ybuf", [128, NT], F32)
    sqbuf = A.alloc("sqbuf", [128, NT], F32)
    inb = nc.dram_tensor("xch_in", [128, 1028], F32)
    outb = nc.dram_tensor("xch_out", [512, 1028], F32)

    def load_xT(half, after=()):
        load("pool", xT[:, :, half * 512:(half + 1) * 512], xT_d[:, half * 512:(half + 1) * 512].rearrange("(k p) c -> p k c", p=128),
             ("xT", half), [("xT", half)], reads=list(after))
    load("sp", glag[:], glag_d, "glag", ["glag"])
    load("sp", wmask[:], wmask_d, "wmask", ["wmask"])
    S.add("dve", lambda e: e.memset(zl1m[:], 1.0), writes=["zl1m"])
    S.add("dve", lambda e: e.memset(dtot[:], 1.0), writes=["dtot"])

    wblocks = [("k", 0, COL_K, 256), ("k", 1, COL_K + 256, 256), ("q", 0, COL_Q, 256), ("q", 1, COL_Q + 256, 256)]
    wblocks += [("v", i, COL_V + 256 * i, 256) for i in range(4)]
    wblocks += [("r", 0, COL_R, 256)]
    for i in range(1, 4):
        wblocks += [("r", i, COL_R + 256 * i, 256), ("c", i - 1, None, 384)]
    wblocks += [("c", j, None, 384) for j in range(3, 8)]
    WIDX = {(b[0], b[1]): n for n, b in enumerate(wblocks)}

    def load_wblock(n):
        if n >= len(wblocks):
            return
        kind, idx, c0, ncol = wblocks[n]
        s = n % 2
        if kind == "c":
            for g, cb in enumerate((COL_B, COL_C, COL_U)):
                src = w_in[:, cb + idx * 128:cb + (idx + 1) * 128].rearrange("(k p) c -> p k c", p=128)
                load("pool", wsl[s][:, :, g * 128:(g + 1) * 128], src, ("wsl", s, g), [("wsl", s, g)])
        else:
            src = w_in[:, c0:c0 + ncol].rearrange("(k p) c -> p k c", p=128)
            load("pool", wsl[s][:, :, 0:ncol], src, ("wsl", s, 0), [("wsl", s, 0), ("wsl", s, 1), ("wsl", s, 2)])

    load_xT(0)
    load_wblock(0)
    load("pool", Wz[:], w_in[:, COL_Z:COL_Z + 16].rearrange("(k p) c -> p k c", p=128), "Wz", ["Wz"])
    load_xT(1, after=[("wsl", 0, 0)])
    load_wblock(1)
    load("pool", halo[:], xTh_d.rearrange("(k p) c -> p k c", p=128), "halo", ["halo"])

    def gla_proj_gen(n):
        kind, idx, c0, ncol = wblocks[n]
        s = n % 2
        wr = [("wsl", s, 0), ("wsl", s, 1), ("wsl", s, 2)]
        w = wsl[s]
        if kind in ("q", "k"):
            dstT = qT if kind == "q" else kT
            for half in range(2):
                for hh in range(2):
                    h = idx * 2 + hh
                    b = bank()
                    mm_group(ps[b][:, :], [(w[:, kc, hh * 128:(hh + 1) * 128], xT[:, kc, half * 512:(half + 1) * 512])
                                           for kc in range(16)], wr + [("xT", half)], b)
                    if kind == "q":
                        act(dstT[:, h, half * 512:(half + 1) * 512], ps[b][:, :], AF.Copy, [("ps", b)],
                            [("qT", h, half)], scale=QSCALE)
                    else:
                        act(dstT[:, h, half * 512:(half + 1) * 512], ps[b][:, :], AF.Copy, [("ps", b)],
                            [("kT", h, half)])
                    yield
        if kind in ("v", "r"):
            for i in range(8):
                b = bank()
                mm_group(ps[b][:, 0:256], [(xT[:, kc, i * 128:(i + 1) * 128], w[:, kc, 0:256]) for kc in range(16)],
                         wr + [("xT", i // 4)], b)
                if kind == "k":
                    dve_copy(ktok[:, i, idx * 256:(idx + 1) * 256], ps[b][:, 0:256], [("ps", b)], [("ktok", i, idx)])
                elif kind == "v":
                    dve_copy(vtok[:, i, idx * 256:(idx + 1) * 256], ps[b][:, 0:256], [("ps", b)], [("vtok", i, idx)])
                else:
                    st = silu_t[i % 2]
                    act(st[:], ps[b][:, 0:256], AF.Silu, [("ps", b)], [("silu", i % 2)])
                    dve_tt(gr[:, i, idx * 256:(idx + 1) * 256], st[:], glag[:, idx * 256:(idx + 1) * 256], ALU.mult,
                           [("silu", i % 2), "glag"], [("gr", i, idx)])
                yield
        load_wblock(n + 2)

    run_interleaved(gla_proj_gen(0), None)
    for half in range(2):
        b = bank()
        mm_group(ps[b][0:16, :], [(Wz[:, kc, :], xT[:, kc, half * 512:(half + 1) * 512]) for kc in range(16)],
                 ["Wz", ("xT", half)], b)
        dve_copy(zl1m[0:16, half * 512:(half + 1) * 512], ps[b][0:16, :], [("ps", b)], ["zl1m"])

    for n in range(1, 4):
        run_interleaved(gla_proj_gen(n), None)

    r4 = "p (h t) -> p h t"
    QK = lambda nm, i: [(nm, h, i // 4) for h in range(4)]

    def state_update(i, dec_ap_fn, bank_pair_reads):
        vr = [("vtok", i, j) for j in range(4)]
        for hp in range(2):
            b = bank()
            for hh in range(2):
                h = hp * 2 + hh
                mm(ps[b][:, hh * 256:(hh + 1) * 256], ktok[:, i, h * 128:(h + 1) * 128],
                   vtok[:, i, h * 256:(h + 1) * 256], True, True, [("ktok", i, 0), ("ktok", i, 1)] + vr, [("ps", b)])
            for hh in range(2):
                h = hp * 2 + hh
                dve_stt(Sst[:, h, :], Sst[:, h, :], dec_ap_fn(h), ps[b][:, hh * 256:(hh + 1) * 256], ALU.mult, ALU.add,
                        ["Sst", ("ps", b)] + bank_pair_reads, ["Sst"])

    def p1_a(i):
        l = i % 2
        b = bank()
        mm(ps[b][:, :], zl1m[0:17, i * 128:(i + 1) * 128], wg1[0:17, :], True, True, ["zl1m", "wg1"], [("ps", b)])
        act(Lb[:, l, :], ps[b][:, :], AF.Exp, [("ps", b)], [("Lb", l)], scale=-1.0)
        act(Lb[:, l, :], Lb[:, l, :], AF.Ln, [("Lb", l)], [("Lb", l)], bias=1.0)

    def p1_b(i):
        l = i % 2
        b = bank()
        for h in range(4):
            mm(ps[b][:, h * 128:(h + 1) * 128], Lb[:, l, h * 128:(h + 1) * 128], tri_f, True, True,
               [("Lb", l), "cst"], [("ps", b)])
        act(E1[:, l, :], ps[b][:, :], AF.Exp, [("ps", b)], [("E1", l)], scale=-GI)
        act(E2[:, l, :], ps[b][:, :], AF.Exp, [("ps", b)], [("E2", l)], scale=GI)
        sl = slice(i * 128, (i + 1) * 128)
        S.add("dve", lambda e, i=i, l=l: e.tensor_copy(out=decs[:, i, :], in_=E1[:, l, :].rearrange(r4, h=4)[:, :, 127]),
              reads=[("E1", l)], writes=[("decs", i)])
        dve_tt(dtot[:], dtot[:], decs[:, i, :], ALU.mult, ["dtot", ("decs", i)], ["dtot"])
        for h in range(4):
            dve_stt(scm[:, l, h * 128:(h + 1) * 128], kT[:, h, sl], decs[:, i, h:h + 1], E2[:, l, h * 128:(h + 1) * 128],
                    ALU.mult, ALU.mult, QK("kT", i) + [("decs", i), ("E2", l)], [("scm", l)])
        dve_tt(kT[:, :, sl], kT[:, :, sl], E2[:, l, :].rearrange(r4, h=4), ALU.mult, QK("kT", i) + [("E2", l)], QK("kT", i))
        dve_tt(qT[:, :, sl], qT[:, :, sl], E1[:, l, :].rearrange(r4, h=4), ALU.mult, QK("qT", i) + [("E1", l)], QK("qT", i))

    def p1_t(i):
        l = i % 2
        bT = bank()
        pbf = ps[bT][:].bitcast(BF16)
        for h in range(4):
            S.add("pe", lambda e, h=h, l=l, pbf=pbf: e.transpose(pbf[:, h * 128:(h + 1) * 128],
                                                                  scm[:, l, h * 128:(h + 1) * 128], ident_bf[:]),
                  reads=[("scm", l), "ident"], writes=[("ps", bT)])
        S.add("act", lambda e, i=i, pbf=pbf: e.activation(out=ktok[:, i, :], in_=pbf[:, 0:512], func=AF.Copy),
              reads=[("ps", bT)], writes=[("ktok", i, 0), ("ktok", i, 1)])

    def pass1_gen():
        for step in range(8 + 2):
            if step < 8:
                p1_a(step)
                yield
            if 0 <= step - 1 < 8:
                p1_b(step - 1)
                yield
            if 0 <= step - 2 < 8:
                p1_t(step - 2)
                yield

    def pass1_state_gen():
        for i in range(8):
            state_update(i, lambda h, i=i: decs[:, i, h:h + 1], [("decs", i)])
            yield

    def blocks_gen(n0, n1):
        for n in range(n0, n1):
            yield from gla_proj_gen(n)

    run_interleaved(pass1_gen(), blocks_gen(4, 8), 3, 4)
    run_interleaved(pass1_state_gen(), blocks_gen(8, 9), 1, 1)

    dve_copy(xchg[:, 0:1024], Sst[:].rearrange("p h e -> p (h e)"), ["Sst"], ["xchg"])
    dve_copy(xchg[:, 1024:1028], dtot[:], ["dtot", "xchg"], ["xchg"])
    load("sp", inb.ap(), xchg[:], "inb", ["inb"], reads=["xchg"])
    S.dma("pool", lambda e: e.collective_compute("AllGather", ALU.bypass, replica_groups=[[0, 1, 2, 3], [4, 5, 6, 7]],
                                                 ins=[inb.ap().opt()], outs=[outb.ap().opt()]),
          key="cc", reads=["inb"], writes=["outb"], inc=1)
    S.add("dve", lambda e: e.memset(Sst[:].rearrange("p h e -> p (h e)"), 0.0), reads=["xchg"], writes=["Sst"])
    Sflat = Sst[:].rearrange("p h e -> p (h e)")

    def combine_gen():
        for m in range(4):
            load("sp", gbuf[:], outb.ap()[m * 128:(m + 1) * 128, :], "gbuf", ["gbuf"], reads=["outb"])
            for h in range(4):
                dve_stt(tbuf[:, h * 256:(h + 1) * 256], Sst[:, h, :], gbuf[:, 1024 + h:1025 + h], gbuf[:, h * 256:(h + 1) * 256],
                        ALU.mult, ALU.add, ["Sst", "gbuf"] + TB, TB)
            dve_tt(tbuf, tbuf, Sflat, ALU.subtract, TB + ["Sst"], TB)
            dve_stt(Sflat, tbuf, wmask[:, m:m + 1], Sflat, ALU.mult, ALU.add, TB + ["Sst", "wmask"], ["Sst"])
            yield

    def pass2_gen():
        for i in range(8):
            l = i % 2
            b = bank()
            for h in range(4):
                mm(ps[b][:, h * 128:(h + 1) * 128], kT[:, h, i * 128:(i + 1) * 128], qT[:, h, i * 128:(i + 1) * 128],
                   True, True, QK("kT", i) + QK("qT", i), [("ps", b)])
            dve_tt(scm[:, l, :], ps[b][:, :], tri4_f, ALU.mult, [("ps", b), "cst"], [("scm", l)])
            yield
            vr = [("vtok", i, j) for j in range(4)]
            S.add("act", lambda e: e.activation(out=Sbf[:], in_=Sst[:].rearrange("p h e -> p (h e)"), func=AF.Copy),
                  reads=["Sst"], writes=["Sbf"])
            ob = []
            for hp in range(2):
                b = bank()
                ob.append(b)
                for hh in range(2):
                    h = hp * 2 + hh
                    o_ap = ps[b][:, hh * 256:(hh + 1) * 256]
                    mm(o_ap, scm[:, l, h * 128:(h + 1) * 128], vtok[:, i, h * 256:(h + 1) * 256], True, False,
                       [("scm", l)] + vr, [("ps", b)])
                    mm(o_ap, qT[:, h, i * 128:(i + 1) * 128], Sbf[:, h * 256:(h + 1) * 256], False, True,
                       QK("qT", i) + ["Sbf"], [("ps", b)])
            state_update(i, lambda h, i=i: decs[:, i, h:h + 1], [("decs", i)])
            for h in range(4):
                b = ob[h // 2]
                act(junk[:], ps[b][:, (h % 2) * 256:(h % 2 + 1) * 256], AF.Square, [("ps", b)], ["junk", ("ssq", l, h)],
                    accum_out=ssq[:, l, h:h + 1])
            act(rstd[:, l, :], ssq[:, l, :], AF.Ln, [("ssq", l, h) for h in range(4)], [("rstd", l)],
                scale=1.0 / 256.0, bias=RMS_EPS)
            act(rstd[:, l, :], rstd[:, l, :], AF.Exp, [("rstd", l)], [("rstd", l)], scale=-0.5)
            for h in range(4):
                b = ob[h // 2]
                dve_stt(ytok[:, l, h * 256:(h + 1) * 256], ps[b][:, (h % 2) * 256:(h % 2 + 1) * 256], rstd[:, l, h:h + 1],
                        gr[:, i, h * 256:(h + 1) * 256], ALU.mult, ALU.mult,
                        [("ps", b), ("rstd", l), ("gr", i, h)], [("ytok", l)])
            yield
            if i >= 1:
                gla_transposes(i - 1)
                yield
        gla_transposes(7)
        yield

    def gla_transposes(i):
        l = i % 2
        b = bank()
        pbf = ps[b][:].bitcast(BF16)
        for fb in range(8):
            S.add("pe", lambda e, fb=fb, l=l, pbf=pbf: e.transpose(pbf[:, fb * 128:(fb + 1) * 128],
                                                                    ytok[:, l, fb * 128:(fb + 1) * 128], ident_bf[:]),
                  reads=[("ytok", l), "ident"], writes=[("ps", b)])
        S.add("act", lambda e, i=i, pbf=pbf: e.activation(out=yT[:, 8:16, i * 128:(i + 1) * 128],
                                                           in_=pbf.rearrange("p (f t) -> p f t", f=8), func=AF.Copy),
              reads=[("ps", b)], writes=[("yT", "g", i)])

    pending_rms = []

    def conv_gen(j0, j1):
        for j in range(j0, j1):
            n = WIDX[("c", j)]
            s = n % 2
            w = wsl[s]
            wr = [("wsl", s, 0), ("wsl", s, 1), ("wsl", s, 2)]
            bh = bank()
            mm_group(ps[bh][:, 0:2], [(w[:, kc, 128:256], halo[:, kc, :]) for kc in range(16)], wr + ["halo"], bh)
            bh2 = bank()
            mm_group(ps[bh2][:, 0:2], [(w[:, kc, 256:384], halo[:, kc, :]) for kc in range(16)], wr + ["halo"], bh2)
            dve_copy(hbuf[:, 0:2], ps[bh2][:, 0:2], [("ps", bh2)], [("hbuf", 0)])
            dve_tt(hbuf[:, 0:2], ps[bh][:, 0:2], hbuf[:, 0:2], ALU.mult, [("ps", bh), ("hbuf", 0)], [("hbuf", 0)])
            yield
            for half in range(2):
                hs = slice(2 + half * 512, 2 + (half + 1) * 512)
                ts = slice(half * 512, (half + 1) * 512)
                hk = ("hbuf", 1 + half)
                xr = [("xT", half)]
                bu = bank()
                mm_group(ps[bu][:, :], [(w[:, kc, 256:384], xT[:, kc, ts]) for kc in range(16)], wr + xr, bu)
                act(hbuf[:, hs], ps[bu][:, :], AF.Copy, [("ps", bu)], [hk])
                if pending_rms:
                    pending_rms.pop(0)()
                yield
                bc = bank()
                mm_group(ps[bc][:, :], [(w[:, kc, 128:256], xT[:, kc, ts]) for kc in range(16)], wr + xr, bc)
                dve_tt(hbuf[:, hs], ps[bc][:, :], hbuf[:, hs], ALU.mult, [("ps", bc), hk], [hk])
                yield
                bb = bank()
                mm_group(ps[bb][:, :], [(w[:, kc, 0:128], xT[:, kc, ts]) for kc in range(16)], wr + xr, bb)
                hprev = [("hbuf", 0), ("hbuf", 1)] if half == 0 else [("hbuf", 1), ("hbuf", 2)]
                yk = ("ybuf", half)
                dve_ts(ybuf[:, ts], hbuf[:, hs], convw[:, 3 * j + 2:3 * j + 3], None, ALU.mult, ALU.bypass,
                       [hk, "convw"], [yk])
                dve_stt(ybuf[:, ts], hbuf[:, 1 + half * 512:1 + (half + 1) * 512], convw[:, 3 * j + 1:3 * j + 2], ybuf[:, ts],
                        ALU.mult, ALU.add, hprev + [yk, "convw"], [yk])
                dve_stt(ybuf[:, ts], hbuf[:, half * 512:(half + 1) * 512], convw[:, 3 * j:3 * j + 1], ybuf[:, ts],
                        ALU.mult, ALU.add, hprev + [yk, "convw"], [yk])
                dve_tt(ybuf[:, ts], ps[bb][:, :], ybuf[:, ts], ALU.mult, [("ps", bb), yk], [yk])
                sk = ("sqbuf", half)
                act(sqbuf[:, ts], ybuf[:, ts], AF.Square, [yk], [sk])

                def rms_stage(j=j, ts=ts, yk=yk, sk=sk, half=half):
                    br = bank()
                    mm(ps[br][:, :], ones_f, sqbuf[:, ts], True, True, [sk, "cst"], [("ps", br)])
                    act(sqbuf[:, ts], ps[br][:, :], AF.Ln, [("ps", br)], [sk], scale=1.0 / 128.0, bias=RMS_EPS)
                    act(sqbuf[:, ts], sqbuf[:, ts], AF.Exp, [sk], [sk], scale=-0.5)
                    dve_stt(yT[:, j, ts], ybuf[:, ts], convg[:, j:j + 1], sqbuf[:, ts], ALU.mult, ALU.mult,
                            [yk, sk, "convg"], [("yT", "c", j, half)])
                pending_rms.append(rms_stage)
                yield
            load_wblock(n + 2)
        while pending_rms:
            pending_rms.pop(0)()
            yield

    for i in range(1, 4):
        run_interleaved(blocks_gen(WIDX[("r", i)], WIDX[("r", i)] + 1), None)
        run_interleaved(conv_gen(i - 1, i), None)
    run_interleaved(conv_gen(3, 4), combine_gen(), 2, 1)
    A.reset(m_persist0)
    wo = [A.alloc("wo%d" % i, [128, 16, 512], BF16) for i in range(2)]
    assert A.offs["wo0"] == A.offs["xT"]
    x1 = A.alloc("x1", [128, 8, D], F32)
    x1T = A.alloc("x1T", [128, 16, NT], BF16)

    def fence(keys):
        S.add("pool", lambda e: e.memset(fz[:], 0.0), writes=list(keys) + ["fz"])

    outs = []
    g2, c2 = pass2_gen(), conv_gen(4, 8)
    g_done = c_done = False
    while not (g_done and c_done):
        if not g_done:
            try:
                next(g2)
            except StopIteration:
                g_done = True
        for _ in range(2):
            if not c_done:
                try:
                    next(c2)
                except StopIteration:
                    c_done = True
                    fence([("xT", 0), ("xT", 1), ("wo", 0), ("wo", 1)])
                    for q in range(2):
                        load("pool", wo[q][:], w_out[:, q * 512:(q + 1) * 512].rearrange("(k p) c -> p k c", p=128),
                             ("wo", q), [("wo", q)])
    if debug:
        outs.append(load("sp", dbg["yT"], yT[:].rearrange("p f t -> p (f t)"), "dbgyT", [],
                         reads=[("yT", "g", i) for i in range(8)] + [("yT", "c", j, h) for j in range(8) for h in range(2)]))
    S.fence_all("pool", lambda e: e.memset(fz[:], 0.0), "FB")

    lng = A.alloc("ln1g", [128, D], F32)
    lnb = A.alloc("ln1b", [128, D], F32)
    x1bf = [A.alloc("x1bf%d" % i, [128, D], BF16) for i in range(3)]
    A.alloc("x1bfpad", [128, D], BF16)
    stats = A.alloc("stats", [128, 8, 24], F32)
    mv = A.alloc("mv", [128, 8, 2], F32)
    rs1 = A.alloc("rs1", [128, 8, 1], F32)
    rl = [A.alloc("rl0", [128, 512], F32)]
    wu = [A.alias("wu%d" % i, [128, 16, 512], BF16, A.offs["wo%d" % i]) for i in range(2)]
    wd = [A.alias("wd0", [128, 4, D], BF16, A.offs["ln1g"]), A.alias("wd1", [128, 4, D], BF16, A.offs["x1bf0"])]
    hT = [A.alias("hT%d" % i, [128, 4, NT], BF16, A.offs["yT"] + i * 8192) for i in range(2)]
    lng2 = A.alias("ln2g", [128, D], F32, A.offs["yT"] + 16384)
    lnb2 = A.alias("ln2b", [128, D], F32, A.offs["yT"] + 24576)
    assert A.offs["ln1b"] == A.offs["ln1g"] + 8192 and A.offs["x1bfpad"] == A.offs["x1bf0"] + 12288

    def fence(keys):
        S.add("pool", lambda e: e.memset(fz[:], 0.0), writes=list(keys) + ["fz"])

    for i in range(8):
        load("sp", x1[:, i, :], x_d[i * 128:(i + 1) * 128, :], ("x1", i), [("x1", i, q) for q in range(4)], reads=["FB"])
    load("sp", lng[:], ln1g_d, "lng", ["lng"], reads=["FB"])
    load("sp", lnb[:], ln1b_d, "lnb", ["lnb"], reads=["FB"])
    yT_reads = [("yT", "g", i) for i in range(8)] + [("yT", "c", j, h) for j in range(8) for h in range(2)]

    def ln_stats(i, c):
        S.add("dve", lambda e: e.bn_stats(out=stats[:, i, c * 6:(c + 1) * 6], in_=x1[:, i, c * 512:(c + 1) * 512]),
              reads=[("x1", i, c)], writes=[("stats", i, c)])

    def ln_front(i, norm_on_dve=False):
        xr = [("x1", i, q) for q in range(4)]
        S.add("dve", lambda e: e.bn_aggr(out=mv[:, i, :], in_=stats[:, i, :]),
              reads=[("stats", i, c) for c in range(4)], writes=[("mv", i)])
        act(rs1[:, i, :], mv[:, i, 1:2], AF.Ln, [("mv", i)], [("rs1", i)], bias=LN_EPS)
        act(rs1[:, i, :], rs1[:, i, :], AF.Exp, [("rs1", i)], [("rs1", i)], scale=-0.5)
        if norm_on_dve:
            dve_ts(x1[:, i, :], x1[:, i, :], mv[:, i, 0:1], rs1[:, i, 0:1], ALU.subtract, ALU.mult,
                   xr + [("mv", i), ("rs1", i)], xr)
            return
        dve_stt(mv[:, i, 1:2], mv[:, i, 0:1], -1.0, rs1[:, i, 0:1], ALU.mult, ALU.mult, [("mv", i), ("rs1", i)], [("mv", i)])
        act(x1[:, i, :], x1[:, i, :], AF.Identity, xr + [("mv", i), ("rs1", i)], xr, scale=rs1[:, i, 0:1], bias=mv[:, i, 1:2])

    def ln_back(i, g_ap, b_ap, gk, bk_, on_dve=False):
        xr = [("x1", i, q) for q in range(4)]
        dve_tt(x1[:, i, :], x1[:, i, :], g_ap, ALU.mult, xr + [gk], xr)
        if on_dve:
            dve_tt(x1[:, i, :], x1[:, i, :], b_ap, ALU.add, xr + [bk_], xr)
        else:
            S.add("pool", lambda e: e.tensor_tensor(out=x1[:, i, :], in0=x1[:, i, :], in1=b_ap, op=ALU.add),
                  reads=xr + [bk_], writes=xr)

    def layer_norm_tile(i, g_ap, b_ap, gk, bk_):
        xr = [("x1", i, q) for q in range(4)]
        for c in range(4):
            S.add("dve", lambda e, c=c: e.bn_stats(out=stats[:, i, c * 6:(c + 1) * 6], in_=x1[:, i, c * 512:(c + 1) * 512]),
                  reads=xr, writes=[("stats", i, c)])
        S.add("dve", lambda e: e.bn_aggr(out=mv[:, i, :], in_=stats[:, i, :]),
              reads=[("stats", i, c) for c in range(4)], writes=[("mv", i)])
        act(rs1[:, i, :], mv[:, i, 1:2], AF.Ln, [("mv", i)], [("rs1", i)], bias=LN_EPS)
        act(rs1[:, i, :], rs1[:, i, :], AF.Exp, [("rs1", i)], [("rs1", i)], scale=-0.5)
        dve_ts(x1[:, i, :], x1[:, i, :], mv[:, i, 0:1], rs1[:, i, 0:1], ALU.subtract, ALU.mult,
               xr + [("mv", i), ("rs1", i)], xr)
        dve_tt(x1[:, i, :], x1[:, i, :], g_ap, ALU.mult, xr + [gk], xr)
        dve_tt(x1[:, i, :], x1[:, i, :], b_ap, ALU.add, xr + [bk_], xr)

    def transposes_c(i):
        xb = x1bf[i % 3]
        for half in range(2):
            bt = bank()
            pbf = ps[bt][:].bitcast(BF16)
            for fb in range(8):
                f = half * 8 + fb
                S.add("pe", lambda e, fb=fb, f=f, pbf=pbf, xb=xb: e.transpose(pbf[:, fb * 128:(fb + 1) * 128],
                                                                               xb[:, f * 128:(f + 1) * 128], ident_bf[:]),
                      reads=[("x1bf", i % 3), "ident"], writes=[("ps", bt)])
            S.add("act", lambda e, i=i, half=half, pbf=pbf: e.activation(
                out=x1T[:, half * 8:(half + 1) * 8, i * 128:(i + 1) * 128],
                in_=pbf.rearrange("p (f t) -> p f t", f=8), func=AF.Copy),
                reads=[("ps", bt)], writes=[("x1T", i, half)])

    def mix_group(q, i):
        b = bank()
        mm_group(ps[b][:, :], [(yT[:, fc, i * 128:(i + 1) * 128], wo[q % 2][:, fc, :]) for fc in range(16)],
                 [("wo", q % 2)] + yT_reads, b)
        dve_stt(x1[:, i, q * 512:(q + 1) * 512], x1[:, i, q * 512:(q + 1) * 512], ALPHA, ps[b][:, :], ALU.mult, ALU.add,
                [("x1", i, q), ("ps", b)], [("x1", i, q)])
        ln_stats(i, q)

    for q in range(2):
        for i in range(8):
            mix_group(q, i)
        load("pool", wo[q][:], w_out[:, (q + 2) * 512:(q + 3) * 512].rearrange("(k p) c -> p k c", p=128),
             ("wo", q), [("wo", q)])
    for i in range(4):
        mix_group(2, i)
    def ln1_back(i):
        ln_back(i, lng[:], lnb[:], "lng", "lnb")
        act(x1bf[i % 3][:], x1[:, i, :], AF.Copy, [("x1", i, qq) for qq in range(4)], [("x1bf", i % 3)])

    NPART = 16
    HT_KEYS = [("hT", sl_, c, h) for sl_ in range(2) for c in range(4) for h in range(2)]

    def load_wu(p):
        load("pool", wu[p % 2][:], w_up[:, p * 512:(p + 1) * 512].rearrange("(k p) c -> p k c", p=128),
             ("wu", p % 2), [("wu", p % 2, fb) for fb in range(4)])

    def load_wd(p):
        load("pool", wd[p % 2][:], w_down[p * 512:(p + 1) * 512, :].rearrange("(c p) m -> p c m", p=128),
             ("wd", p % 2), [("wd", p % 2)])

    def up_bank(p, fb, half):
        b = bank()
        xr_ = [("x1T", i, hh) for i in range(4 * half, 4 * half + 4) for hh in range(2)]
        mm_group(ps[b][:, :], [(wu[p % 2][:, kc, fb * 128:(fb + 1) * 128], x1T[:, kc, half * 512:(half + 1) * 512])
                               for kc in range(16)], [("wu", p % 2, fb)] + xr_, b)
        act(rl[0][:], ps[b][:, :], AF.Relu, [("ps", b)], [("rl", 0)])
        act(hT[p % 2][:, fb, half * 512:(half + 1) * 512], rl[0][:], AF.Square, [("rl", 0)], [("hT", p % 2, fb, half)])

    def down_tile(p, i, parts=None):
        parts = parts or (p,)
        last = parts[-1] == NPART - 1
        for mb in range(4):
            b = bank()
            pairs, rd = [], []
            for pp in parts:
                pairs += [(hT[pp % 2][:, c, i * 128:(i + 1) * 128], wd[pp % 2][:, c, mb * 512:(mb + 1) * 512]) for c in range(4)]
                rd += [("wd", pp % 2)] + [("hT", pp % 2, c, i // 4) for c in range(4)]
            mm_group(ps[b][:, :], pairs, rd, b)
            xs = x1[:, i, mb * 512:(mb + 1) * 512]
            if parts[0] == 0:
                dve_stt(xs, xs, ALPHA, ps[b][:, :], ALU.mult, ALU.add, [("x1", i, mb), ("ps", b)], [("x1", i, mb)])
            else:
                dve_tt(xs, ps[b][:, :], xs, ALU.add, [("x1", i, mb), ("ps", b)], [("x1", i, mb)])
            if last:
                ln_stats(i, mb)

    for i in range(8):
        mix_group(3, i)
        if i + 4 < 8:
            mix_group(2, i + 4)
        ln_front(i)
        if i >= 1:
            ln1_back(i - 1)
        if i >= 3:
            transposes_c(i - 3)
        if i == 3:
            fence([("wo", 0)] + [("wu", 0, fb) for fb in range(4)])
            load_wu(0)
    ln1_back(7)
    if debug:
        outs.append(load("sp", dbg["x1"], x1[:].rearrange("p i d -> p (i d)"), "dbgx1", [],
                         reads=[("x1", i, q) for i in range(8) for q in range(4)]))
    fence(yT_reads + HT_KEYS + ["lng2", "lnb2"])
    fence([("wo", 1)] + [("wu", 1, fb) for fb in range(4)])
    fence(["lng", "lnb", ("wd", 0)])
    load_wd(0)
    load_wu(1)
    transposes_c(5)
    up_bank(0, 0, 0)
    transposes_c(6)
    up_bank(0, 1, 0)
    transposes_c(7)
    up_bank(0, 2, 0)
    fence([("x1bf", 0), ("x1bf", 1), ("x1bf", 2), ("wd", 1)])
    load_wd(1)
    load("sp", lng2[:], ln2g_d, "lng2", ["lng2"], reads=[("wd", 1)])
    load("sp", lnb2[:], ln2b_d, "lnb2", ["lnb2"], reads=[("wd", 1)])
    up_bank(0, 3, 0)
    for fb in range(4):
        up_bank(0, fb, 1)

    def ln2_back(i):
        ln_back(i, lng2[:], lnb2[:], "lng2", "lnb2", on_dve=(i >= 6))
        outs.append(load("sp", out_d[i * 128:(i + 1) * 128, :], x1[:, i, :], ("out", i), [],
                         reads=[("x1", i, q) for q in range(4)]))

    for p in range(NPART - 2):
        for i in range(8):
            down_tile(p, i)
            up_bank(p + 1, i // 2, i % 2)
        load_wu(p + 2)
        load_wd(p + 2)
    for fb in range(4):
        for half in range(2):
            up_bank(NPART - 1, fb, half)
    for i in range(8):
        down_tile(None, i, parts=(NPART - 2, NPART - 1))
        ln_front(i)
        if i >= 1:
            ln2_back(i - 1)
    ln2_back(7)
    S.flush(outs)
    return nc


def _consts():
    c = np.zeros((128, 1024), np.float32)
    c[:, 0:128] = np.eye(128, dtype=np.float32)
    s = np.arange(128)
    tri = (s[:, None] <= s[None, :]).astype(np.float32)
    for h in range(4):
        c[:, 128 + h * 128:128 + (h + 1) * 128] = tri
    c[:, 640:768] = (s[:, None] > s[None, :]).astype(np.float32)
    c[:, 768:896] = 1.0
    return c


def make_in_maps(x, w_in, conv_w, conv_norm_g, w_gate_up, gate_bias, gla_norm_g, w_out,
                 ln1_g, ln1_b, w_ff_up, w_ff_down, ln2_g, ln2_b):
    f = np.float32
    x = np.asarray(x, f)
    B, SEQ, _ = x.shape
    shared = {
        "w_in": np.ascontiguousarray(np.asarray(w_in, f)[0]),
        "w_out": np.ascontiguousarray(np.asarray(w_out, f)[0]),
        "w_up": np.ascontiguousarray(np.asarray(w_ff_up, f)[0]),
        "w_down": np.ascontiguousarray(np.asarray(w_ff_down, f)[0]),
        "wg1": np.ascontiguousarray(np.concatenate([np.asarray(w_gate_up, f)[0], np.asarray(gate_bias, f)[0][None, :]], 0)),
        "convw": np.ascontiguousarray(np.asarray(conv_w, f)[0].reshape(3, 8, 128).transpose(2, 1, 0).reshape(128, 24)),
        "convg": np.ascontiguousarray(np.asarray(conv_norm_g, f)[0].reshape(8, 128).T),
        "glag": np.ascontiguousarray(np.broadcast_to(np.asarray(gla_norm_g, f)[0][None, :], (128, 1024))),
        "ln1g": np.ascontiguousarray(np.broadcast_to(np.asarray(ln1_g, f)[0][None, :], (128, D))),
        "ln1b": np.ascontiguousarray(np.broadcast_to(np.asarray(ln1_b, f)[0][None, :], (128, D))),
        "ln2g": np.ascontiguousarray(np.broadcast_to(np.asarray(ln2_g, f)[0][None, :], (128, D))),
        "ln2b": np.ascontiguousarray(np.broadcast_to(np.asarray(ln2_b, f)[0][None, :], (128, D))),
        "cst": _consts(),
    }
    in_maps = []
    nq = SEQ // NT
    for c in range(8):
        b, j = c // nq, c % nq
        s0 = j * NT
        hal = np.zeros((2, D), f)
        if s0 > 0:
            hal[:] = x[b, s0 - 2:s0]
        m = dict(shared)
        m["x"] = np.ascontiguousarray(x[b, s0:s0 + NT])
        m["xT"] = np.ascontiguousarray(x[b, s0:s0 + NT].T)
        m["xTh"] = np.ascontiguousarray(hal.T)
        wm = np.zeros((128, 4), f)
        wm[:, :j] = 1.0
        m["wmask"] = wm
        in_maps.append(m)
    return in_maps


def kernel(**inputs):
    in_maps = make_in_maps(**inputs)
    nc = build_program()
    res = run_bass_kernel_spmd(nc, in_maps, core_ids=list(range(8)))
    x = inputs["x"]
    B, SEQ, _ = x.shape
    out = np.empty((B, SEQ, D), np.float32)
    nq = SEQ // NT
    for c in range(8):
        b, j = c // nq, c % nq
        out[b, j * NT:(j + 1) * NT] = np.asarray(res.results[c]["out"], np.float32)
    return out
```

```python
import numpy as np
import concourse.bass as bass
import concourse.mybir as mybir
from concourse.bass_utils import run_bass_kernel_spmd

F32 = mybir.dt.float32
BF16 = mybir.dt.bfloat16
ALU = mybir.AluOpType
AF = mybir.ActivationFunctionType

ENGS = ("pe", "act", "dve", "pool", "sp")

D = 2048
NT = 1024
NPRE = 3072
DFF = 8192
COL_B, COL_C, COL_U, COL_Q, COL_K, COL_V, COL_R, COL_Z = 0, 1024, 2048, 3072, 3584, 4096, 5120, 6144
DIN = 6160
ALPHA = 2.0 ** 0.25
LN_EPS = 1e-5
RMS_EPS = 1e-6
QSCALE = 128.0 ** -0.5
GI = 1.0 / 16.0


class Op:
    __slots__ = ("eng", "fn", "deps", "dma_sem", "dma_val", "milestone", "signals", "inc", "seq")
    _ctr = [0]

    def __init__(self, eng, fn):
        self.eng = eng
        self.fn = fn
        self.deps = ()
        self.dma_sem = None
        self.dma_val = 0
        self.milestone = 0
        self.signals = False
        Op._ctr[0] += 1
        self.seq = Op._ctr[0]


class Sched:
    def __init__(self, nc):
        self.nc = nc
        self.ops = {e: [] for e in ENGS}
        self.last_writer = {}
        self.readers = {}
        self.dma_cnt = {}
        self.dma_semh = {}
        self.eng_semh = None
        self.eng_cnt = {e: 0 for e in ENGS}
        self.nflush = 0

    def _deps(self, op, reads, writes):
        deps = {}
        lw = self.last_writer
        for r in reads:
            w = lw.get(r)
            if w is not None:
                deps[id(w)] = w
        for r in writes:
            w = lw.get(r)
            if w is not None:
                deps[id(w)] = w
            for rd in self.readers.get(r, ()):
                deps[id(rd)] = rd
        deps.pop(id(op), None)
        if op.eng == "pe" and op.dma_sem is None:
            deps = {k: d for k, d in deps.items() if not (d.eng == "pe" and d.dma_sem is None)}
        keep, young = [], {}
        for d in deps.values():
            if d.dma_sem is not None:
                keep.append(d)
            elif d.eng not in young or d.seq > young[d.eng].seq:
                young[d.eng] = d
        op.deps = tuple(keep) + tuple(young.values())
        for d in op.deps:
            d.signals = True
        for r in reads:
            self.readers.setdefault(r, []).append(op)
        for r in writes:
            lw[r] = op
            self.readers[r] = []

    def add(self, eng, fn, reads=(), writes=()):
        op = Op(eng, fn)
        self._deps(op, reads, writes)
        self.ops[eng].append(op)
        return op

    def dma(self, eng, fn, key, reads=(), writes=(), inc=16):
        op = Op(eng, fn)
        op.inc = inc
        self.dma_cnt[key] = self.dma_cnt.get(key, 0) + inc
        op.dma_sem = key
        op.dma_val = self.dma_cnt[key]
        self._deps(op, reads, writes)
        self.ops[eng].append(op)
        return op

    def fence_all(self, eng, fn, key):
        op = Op(eng, fn)
        deps = {}
        for e in ENGS:
            if self.ops[e]:
                d = self.ops[e][-1]
                deps[id(d)] = d
        w = self.last_writer.get(key)
        if w is not None:
            deps[id(w)] = w
        op.deps = tuple(deps.values())
        for d in op.deps:
            d.signals = True
        self.last_writer[key] = op
        self.readers[key] = []
        self.ops[eng].append(op)
        return op

    def flush(self, final_wait_ops=()):
        nc = self.nc
        if self.eng_semh is None:
            self.eng_semh = {e: nc.alloc_semaphore(name="ms_" + e) for e in ENGS}
        for k in self.dma_cnt:
            if k not in self.dma_semh:
                self.dma_semh[k] = nc.alloc_semaphore(name="dq_%d" % len(self.dma_semh))
        for e in ENGS:
            c = self.eng_cnt[e]
            for op in self.ops[e]:
                if op.dma_sem is None and op.signals and op.milestone == 0:
                    c += 1
                    op.milestone = c
            self.eng_cnt[e] = c
        ops_snapshot = {e: self.ops[e] for e in ENGS}
        esem, dsem = self.eng_semh, self.dma_semh

        def stream(e):
            def body(eng):
                waited = {}
                for op in ops_snapshot[e]:
                    need = {}
                    for d in op.deps:
                        if d.dma_sem is not None:
                            k = ("d", d.dma_sem)
                            v = d.dma_val
                        else:
                            k = ("e", d.eng)
                            v = d.milestone
                            if v <= 0:
                                continue
                        if v > need.get(k, 0):
                            need[k] = v
                    for k, v in need.items():
                        if waited.get(k, 0) >= v:
                            continue
                        waited[k] = v
                        eng.wait_ge(dsem[k[1]] if k[0] == "d" else esem[k[1]], v)
                    inst = op.fn(eng)
                    if op.dma_sem is not None:
                        inst.then_inc(dsem[op.dma_sem], op.inc)
                    elif op.signals:
                        inst.then_inc(esem[e], 1)
                if e == "sp":
                    for op in final_wait_ops:
                        eng.wait_ge(dsem[op.dma_sem], op.dma_val)
            return body

        with nc.Block() as block:
            block.tensor(stream("pe"))
            block.scalar(stream("act"))
            block.vector(stream("dve"))
            block.gpsimd(stream("pool"))
            block.sync(stream("sp"))
        for e in ENGS:
            for op in self.ops[e]:
                if op.dma_sem is None and not op.signals:
                    op.milestone = -1
        self.ops = {e: [] for e in ENGS}
        self.nflush += 1


class Arena:
    def __init__(self, nc, base=16512, top=229344):
        self.nc = nc
        self.cur = base
        self.top = top
        self.n = 0
        self.peak = base
        self.offs = {}

    def alloc(self, name, shape, dtype):
        esz = 4 if dtype == F32 else 2
        nbytes = esz
        for s in shape[1:]:
            nbytes *= s
        off = (self.cur + 63) // 64 * 64
        assert off + nbytes <= self.top, (name, off, nbytes, self.top)
        self.cur = off + nbytes
        self.peak = max(self.peak, self.cur)
        self.n += 1
        self.offs[name] = off
        return self.nc.alloc_sbuf_tensor_at("%s_%d" % (name, self.n), list(shape), dtype, offset=off)

    def alias(self, name, shape, dtype, off):
        self.n += 1
        return self.nc.alloc_sbuf_tensor_at("%s_%d" % (name, self.n), list(shape), dtype, offset=off)

    def alloc_top(self, name, shape, dtype):
        esz = 4 if dtype == F32 else 2
        nbytes = esz
        for s in shape[1:]:
            nbytes *= s
        off = (self.top - nbytes) // 64 * 64
        assert off >= self.cur, (name, off, self.cur)
        self.top = off
        self.n += 1
        self.offs[name] = off
        return self.nc.alloc_sbuf_tensor_at("%s_%d" % (name, self.n), list(shape), dtype, offset=off)

    def mark(self):
        return self.cur

    def reset(self, m):
        self.cur = m


def build_program(debug=False):
    nc = bass.Bass("TRN2", target_bir_lowering=False)

    def dram_in(name, shape):
        return nc.dram_tensor(name, list(shape), F32, kind="ExternalInput").ap()

    xT_d = dram_in("xT", [D, NT])
    xTh_d = dram_in("xTh", [D, 2])
    wmask_d = dram_in("wmask", [128, 4])
    x_d = dram_in("x", [NT, D])
    w_in = dram_in("w_in", [D, DIN])
    w_out = dram_in("w_out", [D, D])
    w_up = dram_in("w_up", [D, DFF])
    w_down = dram_in("w_down", [DFF, D])
    wg1_d = dram_in("wg1", [17, 512])
    convw_d = dram_in("convw", [128, 24])
    convg_d = dram_in("convg", [128, 8])
    glag_d = dram_in("glag", [128, 1024])
    ln1g_d = dram_in("ln1g", [128, D])
    ln1b_d = dram_in("ln1b", [128, D])
    ln2g_d = dram_in("ln2g", [128, D])
    ln2b_d = dram_in("ln2b", [128, D])
    cst_d = dram_in("cst", [128, 1024])
    out_d = nc.dram_tensor("out", [NT, D], F32, kind="ExternalOutput").ap()
    dbg = {}
    if debug:
        dbg["yT"] = nc.dram_tensor("dbg_yT", [128, 16 * NT], BF16, kind="ExternalOutput").ap()
        dbg["x1"] = nc.dram_tensor("dbg_x1", [128, 8 * D], F32, kind="ExternalOutput").ap()

    S = Sched(nc)
    A = Arena(nc)
    ps = [nc.alloc_psum_tensor("psb%d" % i, [128, 512], F32) for i in range(8)]
    bank_ctr = [0]

    def bank():
        b = bank_ctr[0]
        bank_ctr[0] = (b + 1) % 8
        return b

    def mm(out_ap, lhsT, rhs, start, stop, reads, writes):
        S.add("pe", lambda e: e.matmul(out_ap, lhsT, rhs, start=start, stop=stop), reads=reads, writes=writes)

    def mm_group(out_ap, pairs, reads, b):
        n = len(pairs)
        for i, (l, r) in enumerate(pairs):
            mm(out_ap, l, r, i == 0, i == n - 1, reads, [("ps", b)])

    def act(out, in_, func, reads, writes, **kw):
        S.add("act", lambda e: e.activation(out=out, in_=in_, func=func, **kw), reads=reads, writes=writes)

    def dve_tt(out, in0, in1, op, reads, writes):
        S.add("dve", lambda e: e.tensor_tensor(out=out, in0=in0, in1=in1, op=op), reads=reads, writes=writes)

    def dve_stt(out, in0, scalar, in1, op0, op1, reads, writes):
        S.add("dve", lambda e: e.scalar_tensor_tensor(out=out, in0=in0, scalar=scalar, in1=in1, op0=op0, op1=op1),
              reads=reads, writes=writes)

    def dve_ts(out, in0, s1, s2, op0, op1, reads, writes):
        S.add("dve", lambda e: e.tensor_scalar(out=out, in0=in0, scalar1=s1, scalar2=s2, op0=op0, op1=op1),
              reads=reads, writes=writes)

    def dve_copy(out, in_, reads, writes):
        S.add("dve", lambda e: e.tensor_copy(out=out, in_=in_), reads=reads, writes=writes)

    def load(eng, out, in_, key, writes, reads=(), **kw):
        return S.dma(eng, lambda e: e.dma_start(out=out, in_=in_, **kw), key=key, reads=reads, writes=writes)

    cst = A.alloc("cst", [128, 1024], F32)
    ident_bf = A.alloc("ident", [128, 128], BF16)
    tri_f = cst[:, 128:256]
    tri4_f = cst[:, 128:640]
    ustr_f = cst[:, 640:768]
    ones_f = cst[:, 768:896]
    wg1 = A.alloc("wg1", [17, 512], F32)
    convw = A.alloc("convw", [128, 24], F32)
    convg = A.alloc("convg", [128, 8], F32)
    Sst = A.alloc("Sst", [128, 4, 256], F32)
    halo = A.alloc("halo", [128, 16, 2], BF16)
    Wz = A.alloc("Wz", [128, 16, 16], BF16)
    fz = A.alloc("fz", [128, 8], F32)
    m_persist0 = A.mark()
    xT = A.alloc("xT", [128, 16, NT], BF16)

    S.add("dve", lambda e: e.memset(Sst[:].rearrange("p h e -> p (h e)"), 0.0), writes=["Sst"])

    def run_interleaved(g1, g2, n1=1, n2=1):
        d1 = g1 is None
        d2 = g2 is None
        while not (d1 and d2):
            for _ in range(n1):
                if not d1:
                    try:
                        next(g1)
                    except StopIteration:
                        d1 = True
            for _ in range(n2):
                if not d2:
                    try:
                        next(g2)
                    except StopIteration:
                        d2 = True

    top0 = A.top
    yT = A.alloc_top("yT", [128, 16, NT], BF16)
    glag = A.alloc("glag", [128, 1024], F32)
    wsl = [A.alloc("wsl%d" % i, [128, 16, 384], BF16) for i in range(2)]
    qT = A.alloc("qT", [128, 4, NT], BF16)
    kT = A.alloc("kT", [128, 4, NT], BF16)
    ktok = A.alloc("ktok", [128, 8, 512], BF16)
    vtok = A.alloc("vtok", [128, 8, 1024], BF16)
    gr = A.alloc("gr", [128, 8, 1024], BF16)
    zl1m = A.alloc("zl1m", [17, NT], F32)
    silu_t = [A.alloc("silu%d" % i, [128, 256], F32) for i in range(2)]
    Lb = A.alloc("Lb", [128, 2, 512], F32)
    E1 = A.alloc("E1", [128, 2, 512], F32)
    E2 = A.alloc("E2", [128, 2, 512], F32)
    E3 = E2
    scm = A.alloc("scm", [128, 2, 512], BF16)
    Sbf = A.alloc("Sbf", [128, 1024], BF16)
    ytok = A.alloc("ytok", [128, 2, 1024], BF16)
    ssq = A.alloc("ssq", [128, 2, 4], F32)
    rstd = A.alloc("rstd", [128, 2, 4], F32)
    decs = A.alloc("decs", [128, 8, 4], F32)
    dtot = A.alloc("dtot", [128, 4], F32)
    wmask = A.alloc("wmask", [128, 4], F32)
    xchg = A.alloc("xchg", [128, 1028], F32)
    gbuf = A.alloc("gbuf", [128, 1028], F32)
    tbuf = Lb[:].rearrange("p l c -> p (l c)")
    TB = [("Lb", 0), ("Lb", 1)]
    junk = A.alloc("junk", [128, 256], BF16)
    hbuf = A.alloc("hbuf", [128, NT + 2], F32)
    ybuf = A.alloc("ybuf", [128, NT], F32)
    sqbuf = A.alloc("sqbuf", [128, NT], F32)
    inb = nc.dram_tensor("xch_in", [128, 1028], F32)
    outb = nc.dram_tensor("xch_out", [512, 1028], F32)

    def load_xT(half, after=()):
        load("pool", xT[:, :, half * 512:(half + 1) * 512], xT_d[:, half * 512:(half + 1) * 512].rearrange("(k p) c -> p k c", p=128),
             ("xT", half), [("xT", half)], reads=list(after))
    S.add("dve", lambda e: e.memset(zl1m[:], 1.0), writes=["zl1m"])
    S.add("dve", lambda e: e.memset(dtot[:], 1.0), writes=["dtot"])

    wblocks = [("k", 0, COL_K, 256), ("k", 1, COL_K + 256, 256), ("q", 0, COL_Q, 256), ("q", 1, COL_Q + 256, 256)]
    wblocks += [("v", i, COL_V + 256 * i, 256) for i in range(4)]
    wblocks += [("r", 0, COL_R, 256)]
    for i in range(1, 4):
        wblocks += [("r", i, COL_R + 256 * i, 256), ("c", i - 1, None, 384)]
    wblocks += [("c", j, None, 384) for j in range(3, 8)]
    WIDX = {(b[0], b[1]): n for n, b in enumerate(wblocks)}

    def load_wblock(n):
        if n >= len(wblocks):
            return
        kind, idx, c0, ncol = wblocks[n]
        s = n % 2
        if kind == "c":
            for g, cb in enumerate((COL_B, COL_C, COL_U)):
                src = w_in[:, cb + idx * 128:cb + (idx + 1) * 128].rearrange("(k p) c -> p k c", p=128)
                load("pool", wsl[s][:, :, g * 128:(g + 1) * 128], src, ("wsl", s, g), [("wsl", s, g)])
        else:
            src = w_in[:, c0:c0 + ncol].rearrange("(k p) c -> p k c", p=128)
            load("pool", wsl[s][:, :, 0:ncol], src, ("wsl", s, 0), [("wsl", s, 0), ("wsl", s, 1), ("wsl", s, 2)])

    load_xT(0)
    load_wblock(0)
    first = [("wsl", 0, 0)]
    load("sp", cst[:], cst_d, "cst", ["cst"], reads=first)
    load("sp", wg1[:], wg1_d, "wg1", ["wg1"], reads=first)
    load("sp", convw[:], convw_d, "convw", ["convw"], reads=first)
    load("sp", convg[:], convg_d, "convg", ["convg"], reads=first)
    dve_copy(ident_bf[:], cst[:, 0:128], ["cst"], ["ident"])
    load("sp", glag[:], glag_d, "glag", ["glag"], reads=first)
    load("sp", wmask[:], wmask_d, "wmask", ["wmask"], reads=first)
    load("pool", Wz[:], w_in[:, COL_Z:COL_Z + 16].rearrange("(k p) c -> p k c", p=128), "Wz", ["Wz"])
    load_xT(1, after=[("wsl", 0, 0)])
    load_wblock(1)
    load("pool", halo[:], xTh_d.rearrange("(k p) c -> p k c", p=128), "halo", ["halo"])

    def gla_proj_gen(n):
        kind, idx, c0, ncol = wblocks[n]
        s = n % 2
        wr = [("wsl", s, 0), ("wsl", s, 1), ("wsl", s, 2)]
        w = wsl[s]
        if kind in ("q", "k"):
            dstT = qT if kind == "q" else kT
            for half in range(2):
                for hh in range(2):
                    h = idx * 2 + hh
                    b = bank()
                    mm_group(ps[b][:, :], [(w[:, kc, hh * 128:(hh + 1) * 128], xT[:, kc, half * 512:(half + 1) * 512])
                                           for kc in range(16)], wr + [("xT", half)], b)
                    if kind == "q":
                        act(dstT[:, h, half * 512:(half + 1) * 512], ps[b][:, :], AF.Copy, [("ps", b)],
                            [("qT", h, half)], scale=QSCALE)
                    else:
                        act(dstT[:, h, half * 512:(half + 1) * 512], ps[b][:, :], AF.Copy, [("ps", b)],
                            [("kT", h, half)])
                    yield
        if kind in ("v", "r"):
            for i in range(8):
                b = bank()
                mm_group(ps[b][:, 0:256], [(xT[:, kc, i * 128:(i + 1) * 128], w[:, kc, 0:256]) for kc in range(16)],
                         wr + [("xT", i // 4)], b)
                if kind == "k":
                    dve_copy(ktok[:, i, idx * 256:(idx + 1) * 256], ps[b][:, 0:256], [("ps", b)], [("ktok", i, idx)])
                elif kind == "v":
                    dve_copy(vtok[:, i, idx * 256:(idx + 1) * 256], ps[b][:, 0:256], [("ps", b)], [("vtok", i, idx)])
                else:
                    st = silu_t[i % 2]
                    act(st[:], ps[b][:, 0:256], AF.Silu, [("ps", b)], [("silu", i % 2)])
                    dve_tt(gr[:, i, idx * 256:(idx + 1) * 256], st[:], glag[:, idx * 256:(idx + 1) * 256], ALU.mult,
                           [("silu", i % 2), "glag"], [("gr", i, idx)])
                yield
        load_wblock(n + 2)

    def zlow_half(half):
        b = bank()
        mm_group(ps[b][0:16, :], [(Wz[:, kc, :], xT[:, kc, half * 512:(half + 1) * 512]) for kc in range(16)],
                 ["Wz", ("xT", half)], b)
        dve_copy(zl1m[0:16, half * 512:(half + 1) * 512], ps[b][0:16, :], [("ps", b)], ["zl1m"])

    g0 = gla_proj_gen(0)
    next(g0)
    next(g0)
    zlow_half(0)
    run_interleaved(g0, None)
    zlow_half(1)

    for n in range(1, 4):
        run_interleaved(gla_proj_gen(n), None)

    r4 = "p (h t) -> p h t"
    QK = lambda nm, i: [(nm, h, i // 4) for h in range(4)]

    def state_update(i, dec_ap_fn, bank_pair_reads):
        vr = [("vtok", i, j) for j in range(4)]
        for hp in range(2):
            b = bank()
            for hh in range(2):
                h = hp * 2 + hh
                mm(ps[b][:, hh * 256:(hh + 1) * 256], ktok[:, i, h * 128:(h + 1) * 128],
                   vtok[:, i, h * 256:(h + 1) * 256], True, True, [("ktok", i, 0), ("ktok", i, 1)] + vr, [("ps", b)])
            for hh in range(2):
                h = hp * 2 + hh
                dve_stt(Sst[:, h, :], Sst[:, h, :], dec_ap_fn(h), ps[b][:, hh * 256:(hh + 1) * 256], ALU.mult, ALU.add,
                        ["Sst", ("ps", b)] + bank_pair_reads, ["Sst"])

    def p1_a(i):
        l = i % 2
        b = bank()
        mm(ps[b][:, :], zl1m[0:17, i * 128:(i + 1) * 128], wg1[0:17, :], True, True, ["zl1m", "wg1"], [("ps", b)])
        act(Lb[:, l, :], ps[b][:, :], AF.Exp, [("ps", b)], [("Lb", l)], scale=-1.0)
        act(Lb[:, l, :], Lb[:, l, :], AF.Ln, [("Lb", l)], [("Lb", l)], bias=1.0)

    def p1_b(i):
        l = i % 2
        b = bank()
        for h in range(4):
            mm(ps[b][:, h * 128:(h + 1) * 128], Lb[:, l, h * 128:(h + 1) * 128], tri_f, True, True,
               [("Lb", l), "cst"], [("ps", b)])
        act(E1[:, l, :], ps[b][:, :], AF.Exp, [("ps", b)], [("E1", l)], scale=-GI)
        act(E2[:, l, :], ps[b][:, :], AF.Exp, [("ps", b)], [("E2", l)], scale=GI)
        sl = slice(i * 128, (i + 1) * 128)
        S.add("dve", lambda e, i=i, l=l: e.tensor_copy(out=decs[:, i, :], in_=E1[:, l, :].rearrange(r4, h=4)[:, :, 127]),
              reads=[("E1", l)], writes=[("decs", i)])
        dve_tt(dtot[:], dtot[:], decs[:, i, :], ALU.mult, ["dtot", ("decs", i)], ["dtot"])
        for h in range(4):
            dve_stt(scm[:, l, h * 128:(h + 1) * 128], kT[:, h, sl], decs[:, i, h:h + 1], E2[:, l, h * 128:(h + 1) * 128],
                    ALU.mult, ALU.mult, QK("kT", i) + [("decs", i), ("E2", l)], [("scm", l)])
        dve_tt(kT[:, :, sl], kT[:, :, sl], E2[:, l, :].rearrange(r4, h=4), ALU.mult, QK("kT", i) + [("E2", l)], QK("kT", i))
        dve_tt(qT[:, :, sl], qT[:, :, sl], E1[:, l, :].rearrange(r4, h=4), ALU.mult, QK("qT", i) + [("E1", l)], QK("qT", i))

    def p1_t(i):
        l = i % 2
        bT = bank()
        pbf = ps[bT][:].bitcast(BF16)
        for h in range(4):
            S.add("pe", lambda e, h=h, l=l, pbf=pbf: e.transpose(pbf[:, h * 128:(h + 1) * 128],
                                                                  scm[:, l, h * 128:(h + 1) * 128], ident_bf[:]),
                  reads=[("scm", l), "ident"], writes=[("ps", bT)])
        S.add("act", lambda e, i=i, pbf=pbf: e.activation(out=ktok[:, i, :], in_=pbf[:, 0:512], func=AF.Copy),
              reads=[("ps", bT)], writes=[("ktok", i, 0), ("ktok", i, 1)])

    def pass1_gen():
        for step in range(8 + 2):
            if step < 8:
                p1_a(step)
                yield
            if 0 <= step - 1 < 8:
                p1_b(step - 1)
                yield
            if 0 <= step - 2 < 8:
                p1_t(step - 2)
                yield

    def pass1_state_gen():
        for i in range(8):
            state_update(i, lambda h, i=i: decs[:, i, h:h + 1], [("decs", i)])
            yield

    def blocks_gen(n0, n1):
        for n in range(n0, n1):
            yield from gla_proj_gen(n)

    run_interleaved(pass1_gen(), blocks_gen(4, 8), 3, 4)
    run_interleaved(pass1_state_gen(), blocks_gen(8, 9), 1, 1)

    dve_copy(xchg[:, 0:1024], Sst[:].rearrange("p h e -> p (h e)"), ["Sst"], ["xchg"])
    dve_copy(xchg[:, 1024:1028], dtot[:], ["dtot", "xchg"], ["xchg"])
    load("sp", inb.ap(), xchg[:], "inb", ["inb"], reads=["xchg"])
    S.dma("pool", lambda e: e.collective_compute("AllGather", ALU.bypass, replica_groups=[[0, 1, 2, 3], [4, 5, 6, 7]],
                                                 ins=[inb.ap().opt()], outs=[outb.ap().opt()]),
          key="cc", reads=["inb"], writes=["outb"], inc=1)
    S.add("dve", lambda e: e.memset(Sst[:].rearrange("p h e -> p (h e)"), 0.0), reads=["xchg"], writes=["Sst"])
    Sflat = Sst[:].rearrange("p h e -> p (h e)")

    def combine_gen():
        for m in range(4):
            load("sp", gbuf[:], outb.ap()[m * 128:(m + 1) * 128, :], "gbuf", ["gbuf"], reads=["outb"])
            for h in range(4):
                dve_stt(tbuf[:, h * 256:(h + 1) * 256], Sst[:, h, :], gbuf[:, 1024 + h:1025 + h], gbuf[:, h * 256:(h + 1) * 256],
                        ALU.mult, ALU.add, ["Sst", "gbuf"] + TB, TB)
            dve_tt(tbuf, tbuf, Sflat, ALU.subtract, TB + ["Sst"], TB)
            dve_stt(Sflat, tbuf, wmask[:, m:m + 1], Sflat, ALU.mult, ALU.add, TB + ["Sst", "wmask"], ["Sst"])
            yield

    def pass2_gen():
        for i in range(8):
            l = i % 2
            b = bank()
            for h in range(4):
                mm(ps[b][:, h * 128:(h + 1) * 128], kT[:, h, i * 128:(i + 1) * 128], qT[:, h, i * 128:(i + 1) * 128],
                   True, True, QK("kT", i) + QK("qT", i), [("ps", b)])
            dve_tt(scm[:, l, :], ps[b][:, :], tri4_f, ALU.mult, [("ps", b), "cst"], [("scm", l)])
            yield
            vr = [("vtok", i, j) for j in range(4)]
            S.add("act", lambda e: e.activation(out=Sbf[:], in_=Sst[:].rearrange("p h e -> p (h e)"), func=AF.Copy),
                  reads=["Sst"], writes=["Sbf"])
            ob = []
            for hp in range(2):
                b = bank()
                ob.append(b)
                for hh in range(2):
                    h = hp * 2 + hh
                    o_ap = ps[b][:, hh * 256:(hh + 1) * 256]
                    mm(o_ap, scm[:, l, h * 128:(h + 1) * 128], vtok[:, i, h * 256:(h + 1) * 256], True, False,
                       [("scm", l)] + vr, [("ps", b)])
                    mm(o_ap, qT[:, h, i * 128:(i + 1) * 128], Sbf[:, h * 256:(h + 1) * 256], False, True,
                       QK("qT", i) + ["Sbf"], [("ps", b)])
            state_update(i, lambda h, i=i: decs[:, i, h:h + 1], [("decs", i)])
            for h in range(4):
                b = ob[h // 2]
                act(junk[:], ps[b][:, (h % 2) * 256:(h % 2 + 1) * 256], AF.Square, [("ps", b)], ["junk", ("ssq", l, h)],
                    accum_out=ssq[:, l, h:h + 1])
            act(rstd[:, l, :], ssq[:, l, :], AF.Ln, [("ssq", l, h) for h in range(4)], [("rstd", l)],
                scale=1.0 / 256.0, bias=RMS_EPS)
            act(rstd[:, l, :], rstd[:, l, :], AF.Exp, [("rstd", l)], [("rstd", l)], scale=-0.5)
            for h in range(4):
                b = ob[h // 2]
                dve_stt(ytok[:, l, h * 256:(h + 1) * 256], ps[b][:, (h % 2) * 256:(h % 2 + 1) * 256], rstd[:, l, h:h + 1],
                        gr[:, i, h * 256:(h + 1) * 256], ALU.mult, ALU.mult,
                        [("ps", b), ("rstd", l), ("gr", i, h)], [("ytok", l)])
            yield
            if i >= 1:
                gla_transposes(i - 1)
                yield
        gla_transposes(7)
        yield

    def gla_transposes(i):
        l = i % 2
        b = bank()
        pbf = ps[b][:].bitcast(BF16)
        for fb in range(8):
            S.add("pe", lambda e, fb=fb, l=l, pbf=pbf: e.transpose(pbf[:, fb * 128:(fb + 1) * 128],
                                                                    ytok[:, l, fb * 128:(fb + 1) * 128], ident_bf[:]),
                  reads=[("ytok", l), "ident"], writes=[("ps", b)])
        S.add("act", lambda e, i=i, pbf=pbf: e.activation(out=yT[:, 8:16, i * 128:(i + 1) * 128],
                                                           in_=pbf.rearrange("p (f t) -> p f t", f=8), func=AF.Copy),
              reads=[("ps", b)], writes=[("yT", "g", i)])

    pending_rms = []

    def conv_gen(j0, j1):
        for j in range(j0, j1):
            n = WIDX[("c", j)]
            s = n % 2
            w = wsl[s]
            wr = [("wsl", s, 0), ("wsl", s, 1), ("wsl", s, 2)]
            bh = bank()
            mm_group(ps[bh][:, 0:2], [(w[:, kc, 128:256], halo[:, kc, :]) for kc in range(16)], wr + ["halo"], bh)
            bh2 = bank()
            mm_group(ps[bh2][:, 0:2], [(w[:, kc, 256:384], halo[:, kc, :]) for kc in range(16)], wr + ["halo"], bh2)
            dve_copy(hbuf[:, 0:2], ps[bh2][:, 0:2], [("ps", bh2)], [("hbuf", 0)])
            dve_tt(hbuf[:, 0:2], ps[bh][:, 0:2], hbuf[:, 0:2], ALU.mult, [("ps", bh), ("hbuf", 0)], [("hbuf", 0)])
            yield
            for half in range(2):
                hs = slice(2 + half * 512, 2 + (half + 1) * 512)
                ts = slice(half * 512, (half + 1) * 512)
                hk = ("hbuf", 1 + half)
                xr = [("xT", half)]
                bu = bank()
                mm_group(ps[bu][:, :], [(w[:, kc, 256:384], xT[:, kc, ts]) for kc in range(16)], wr + xr, bu)
                act(hbuf[:, hs], ps[bu][:, :], AF.Copy, [("ps", bu)], [hk])
                if pending_rms:
                    pending_rms.pop(0)()
                yield
                bc = bank()
                mm_group(ps[bc][:, :], [(w[:, kc, 128:256], xT[:, kc, ts]) for kc in range(16)], wr + xr, bc)
                dve_tt(hbuf[:, hs], ps[bc][:, :], hbuf[:, hs], ALU.mult, [("ps", bc), hk], [hk])
                yield
                bb = bank()
                mm_group(ps[bb][:, :], [(w[:, kc, 0:128], xT[:, kc, ts]) for kc in range(16)], wr + xr, bb)
                hprev = [("hbuf", 0), ("hbuf", 1)] if half == 0 else [("hbuf", 1), ("hbuf", 2)]
                yk = ("ybuf", half)
                dve_ts(ybuf[:, ts], hbuf[:, hs], convw[:, 3 * j + 2:3 * j + 3], None, ALU.mult, ALU.bypass,
                       [hk, "convw"], [yk])
                dve_stt(ybuf[:, ts], hbuf[:, 1 + half * 512:1 + (half + 1) * 512], convw[:, 3 * j + 1:3 * j + 2], ybuf[:, ts],
                        ALU.mult, ALU.add, hprev + [yk, "convw"], [yk])
                dve_stt(ybuf[:, ts], hbuf[:, half * 512:(half + 1) * 512], convw[:, 3 * j:3 * j + 1], ybuf[:, ts],
                        ALU.mult, ALU.add, hprev + [yk, "convw"], [yk])
                dve_tt(ybuf[:, ts], ps[bb][:, :], ybuf[:, ts], ALU.mult, [("ps", bb), yk], [yk])
                sk = ("sqbuf", half)
                act(sqbuf[:, ts], ybuf[:, ts], AF.Square, [yk], [sk])

                def rms_stage(j=j, ts=ts, yk=yk, sk=sk, half=half):
                    br = bank()
                    mm(ps[br][:, :], ones_f, sqbuf[:, ts], True, True, [sk, "cst"], [("ps", br)])
                    act(sqbuf[:, ts], ps[br][:, :], AF.Ln, [("ps", br)], [sk], scale=1.0 / 128.0, bias=RMS_EPS)
                    act(sqbuf[:, ts], sqbuf[:, ts], AF.Exp, [sk], [sk], scale=-0.5)
                    dve_stt(yT[:, j, ts], ybuf[:, ts], convg[:, j:j + 1], sqbuf[:, ts], ALU.mult, ALU.mult,
                            [yk, sk, "convg"], [("yT", "c", j, half)])
                pending_rms.append(rms_stage)
                yield
            load_wblock(n + 2)
        while pending_rms:
            pending_rms.pop(0)()
            yield

    for i in range(1, 4):
        run_interleaved(blocks_gen(WIDX[("r", i)], WIDX[("r", i)] + 1), None)
        run_interleaved(conv_gen(i - 1, i), None)
    run_interleaved(conv_gen(3, 4), combine_gen(), 2, 1)
    A.reset(m_persist0)
    wo = [A.alloc("wo%d" % i, [128, 16, 512], BF16) for i in range(2)]
    assert A.offs["wo0"] == A.offs["xT"]
    x1 = A.alloc("x1", [128, 8, D], F32)
    x1T = A.alloc("x1T", [128, 16, NT], BF16)

    def fence(keys):
        S.add("pool", lambda e: e.memset(fz[:], 0.0), writes=list(keys) + ["fz"])

    outs = []
    g2, c2 = pass2_gen(), conv_gen(4, 8)
    g_done = c_done = False
    while not (g_done and c_done):
        if not g_done:
            try:
                next(g2)
            except StopIteration:
                g_done = True
        for _ in range(2):
            if not c_done:
                try:
                    next(c2)
                except StopIteration:
                    c_done = True
                    fence([("xT", 0), ("xT", 1), ("wo", 0), ("wo", 1)])
                    for q in range(2):
                        load("pool", wo[q][:], w_out[:, q * 512:(q + 1) * 512].rearrange("(k p) c -> p k c", p=128),
                             ("wo", q), [("wo", q)])
    if debug:
        outs.append(load("sp", dbg["yT"], yT[:].rearrange("p f t -> p (f t)"), "dbgyT", [],
                         reads=[("yT", "g", i) for i in range(8)] + [("yT", "c", j, h) for j in range(8) for h in range(2)]))
    S.fence_all("pool", lambda e: e.memset(fz[:], 0.0), "FB")

    lng = A.alloc("ln1g", [128, D], F32)
    lnb = A.alloc("ln1b", [128, D], F32)
    x1bf = [A.alloc("x1bf%d" % i, [128, D], BF16) for i in range(3)]
    A.alloc("x1bfpad", [128, D], BF16)
    stats = A.alloc("stats", [128, 8, 24], F32)
    mv = A.alloc("mv", [128, 8, 2], F32)
    rs1 = A.alloc("rs1", [128, 8, 1], F32)
    rl = [A.alloc("rl0", [128, 512], F32)]
    wu = [A.alias("wu%d" % i, [128, 16, 512], BF16, A.offs["wo%d" % i]) for i in range(2)]
    wd = [A.alias("wd0", [128, 4, D], BF16, A.offs["ln1g"]), A.alias("wd1", [128, 4, D], BF16, A.offs["x1bf0"])]
    hT = [A.alias("hT%d" % i, [128, 4, NT], BF16, A.offs["yT"] + i * 8192) for i in range(2)]
    lng2 = A.alias("ln2g", [128, D], F32, A.offs["yT"] + 16384)
    lnb2 = A.alias("ln2b", [128, D], F32, A.offs["yT"] + 24576)
    assert A.offs["ln1b"] == A.offs["ln1g"] + 8192 and A.offs["x1bfpad"] == A.offs["x1bf0"] + 12288

    def fence(keys):
        S.add("pool", lambda e: e.memset(fz[:], 0.0), writes=list(keys) + ["fz"])

    for i in range(8):
        load("sp", x1[:, i, :], x_d[i * 128:(i + 1) * 128, :], ("x1", i), [("x1", i, q) for q in range(4)], reads=["FB"])
    load("sp", lng[:], ln1g_d, "lng", ["lng"], reads=["FB"])
    load("sp", lnb[:], ln1b_d, "lnb", ["lnb"], reads=["FB"])
    yT_reads = [("yT", "g", i) for i in range(8)] + [("yT", "c", j, h) for j in range(8) for h in range(2)]

    def ln_stats(i, c):
        S.add("dve", lambda e: e.bn_stats(out=stats[:, i, c * 6:(c + 1) * 6], in_=x1[:, i, c * 512:(c + 1) * 512]),
              reads=[("x1", i, c)], writes=[("stats", i, c)])

    def ln_front(i, norm_on_dve=False):
        xr = [("x1", i, q) for q in range(4)]
        S.add("dve", lambda e: e.bn_aggr(out=mv[:, i, :], in_=stats[:, i, :]),
              reads=[("stats", i, c) for c in range(4)], writes=[("mv", i)])
        act(rs1[:, i, :], mv[:, i, 1:2], AF.Ln, [("mv", i)], [("rs1", i)], bias=LN_EPS)
        act(rs1[:, i, :], rs1[:, i, :], AF.Exp, [("rs1", i)], [("rs1", i)], scale=-0.5)
        if norm_on_dve:
            dve_ts(x1[:, i, :], x1[:, i, :], mv[:, i, 0:1], rs1[:, i, 0:1], ALU.subtract, ALU.mult,
                   xr + [("mv", i), ("rs1", i)], xr)
            return
        dve_stt(mv[:, i, 1:2], mv[:, i, 0:1], -1.0, rs1[:, i, 0:1], ALU.mult, ALU.mult, [("mv", i), ("rs1", i)], [("mv", i)])
        act(x1[:, i, :], x1[:, i, :], AF.Identity, xr + [("mv", i), ("rs1", i)], xr, scale=rs1[:, i, 0:1], bias=mv[:, i, 1:2])

    def ln_back(i, g_ap, b_ap, gk, bk_, on_dve=False):
        xr = [("x1", i, q) for q in range(4)]
        dve_tt(x1[:, i, :], x1[:, i, :], g_ap, ALU.mult, xr + [gk], xr)
        if on_dve:
            dve_tt(x1[:, i, :], x1[:, i, :], b_ap, ALU.add, xr + [bk_], xr)
        else:
            S.add("pool", lambda e: e.tensor_tensor(out=x1[:, i, :], in0=x1[:, i, :], in1=b_ap, op=ALU.add),
                  reads=xr + [bk_], writes=xr)

    def layer_norm_tile(i, g_ap, b_ap, gk, bk_):
        xr = [("x1", i, q) for q in range(4)]
        for c in range(4):
            S.add("dve", lambda e, c=c: e.bn_stats(out=stats[:, i, c * 6:(c + 1) * 6], in_=x1[:, i, c * 512:(c + 1) * 512]),
                  reads=xr, writes=[("stats", i, c)])
        S.add("dve", lambda e: e.bn_aggr(out=mv[:, i, :], in_=stats[:, i, :]),
              reads=[("stats", i, c) for c in range(4)], writes=[("mv", i)])
        act(rs1[:, i, :], mv[:, i, 1:2], AF.Ln, [("mv", i)], [("rs1", i)], bias=LN_EPS)
        act(rs1[:, i, :], rs1[:, i, :], AF.Exp, [("rs1", i)], [("rs1", i)], scale=-0.5)
        dve_ts(x1[:, i, :], x1[:, i, :], mv[:, i, 0:1], rs1[:, i, 0:1], ALU.subtract, ALU.mult,
               xr + [("mv", i), ("rs1", i)], xr)
        dve_tt(x1[:, i, :], x1[:, i, :], g_ap, ALU.mult, xr + [gk], xr)
        dve_tt(x1[:, i, :], x1[:, i, :], b_ap, ALU.add, xr + [bk_], xr)

    def transposes_c(i):
        xb = x1bf[i % 3]
        for half in range(2):
            bt = bank()
            pbf = ps[bt][:].bitcast(BF16)
            for fb in range(8):
                f = half * 8 + fb
                S.add("pe", lambda e, fb=fb, f=f, pbf=pbf, xb=xb: e.transpose(pbf[:, fb * 128:(fb + 1) * 128],
                                                                               xb[:, f * 128:(f + 1) * 128], ident_bf[:]),
                      reads=[("x1bf", i % 3), "ident"], writes=[("ps", bt)])
            S.add("act", lambda e, i=i, half=half, pbf=pbf: e.activation(
                out=x1T[:, half * 8:(half + 1) * 8, i * 128:(i + 1) * 128],
                in_=pbf.rearrange("p (f t) -> p f t", f=8), func=AF.Copy),
                reads=[("ps", bt)], writes=[("x1T", i, half)])

    def mix_group(q, i):
        b = bank()
        mm_group(ps[b][:, :], [(yT[:, fc, i * 128:(i + 1) * 128], wo[q % 2][:, fc, :]) for fc in range(16)],
                 [("wo", q % 2)] + yT_reads, b)
        dve_stt(x1[:, i, q * 512:(q + 1) * 512], x1[:, i, q * 512:(q + 1) * 512], ALPHA, ps[b][:, :], ALU.mult, ALU.add,
                [("x1", i, q), ("ps", b)], [("x1", i, q)])
        ln_stats(i, q)

    for q in range(2):
        for i in range(8):
            mix_group(q, i)
        load("pool", wo[q][:], w_out[:, (q + 2) * 512:(q + 3) * 512].rearrange("(k p) c -> p k c", p=128),
             ("wo", q), [("wo", q)])
    for i in range(4):
        mix_group(2, i)
    def ln1_back(i):
        ln_back(i, lng[:], lnb[:], "lng", "lnb")
        act(x1bf[i % 3][:], x1[:, i, :], AF.Copy, [("x1", i, qq) for qq in range(4)], [("x1bf", i % 3)])

    NPART = 16
    HT_KEYS = [("hT", sl_, c, h) for sl_ in range(2) for c in range(4) for h in range(2)]

    def load_wu(p):
        for fb in range(4):
            load("pool", wu[p % 2][:, :, fb * 128:(fb + 1) * 128],
                 w_up[:, p * 512 + fb * 128:p * 512 + (fb + 1) * 128].rearrange("(k p) c -> p k c", p=128),
                 ("wu", p % 2, fb), [("wu", p % 2, fb)])

    def load_wd(p):
        load("pool", wd[p % 2][:], w_down[p * 512:(p + 1) * 512, :].rearrange("(c p) m -> p c m", p=128),
             ("wd", p % 2), [("wd", p % 2)])

    def up_bank(p, fb, half):
        b = bank()
        xr_ = [("x1T", i, hh) for i in range(4 * half, 4 * half + 4) for hh in range(2)]
        mm_group(ps[b][:, :], [(wu[p % 2][:, kc, fb * 128:(fb + 1) * 128], x1T[:, kc, half * 512:(half + 1) * 512])
                               for kc in range(16)], [("wu", p % 2, fb)] + xr_, b)
        act(rl[0][:], ps[b][:, :], AF.Relu, [("ps", b)], [("rl", 0)])
        act(hT[p % 2][:, fb, half * 512:(half + 1) * 512], rl[0][:], AF.Square, [("rl", 0)], [("hT", p % 2, fb, half)])

    def down_tile(p, i, parts=None):
        parts = parts or (p,)
        last = parts[-1] == NPART - 1
        for mb in range(4):
            b = bank()
            pairs, rd = [], []
            for pp in parts:
                pairs += [(hT[pp % 2][:, c, i * 128:(i + 1) * 128], wd[pp % 2][:, c, mb * 512:(mb + 1) * 512]) for c in range(4)]
                rd += [("wd", pp % 2)] + [("hT", pp % 2, c, i // 4) for c in range(4)]
            mm_group(ps[b][:, :], pairs, rd, b)
            xs = x1[:, i, mb * 512:(mb + 1) * 512]
            if parts[0] == 0:
                dve_stt(xs, xs, ALPHA, ps[b][:, :], ALU.mult, ALU.add, [("x1", i, mb), ("ps", b)], [("x1", i, mb)])
            else:
                dve_tt(xs, ps[b][:, :], xs, ALU.add, [("x1", i, mb), ("ps", b)], [("x1", i, mb)])
            if last:
                ln_stats(i, mb)

    for i in range(8):
        mix_group(3, i)
        if i + 4 < 8:
            mix_group(2, i + 4)
        ln_front(i)
        if i >= 1:
            ln1_back(i - 1)
        if i >= 3:
            transposes_c(i - 3)
        if i == 3:
            fence([("wo", 0)] + [("wu", 0, fb) for fb in range(4)])
            load_wu(0)
    ln1_back(7)
    if debug:
        outs.append(load("sp", dbg["x1"], x1[:].rearrange("p i d -> p (i d)"), "dbgx1", [],
                         reads=[("x1", i, q) for i in range(8) for q in range(4)]))
    fence(yT_reads + HT_KEYS + ["lng2", "lnb2"])
    fence([("wo", 1)] + [("wu", 1, fb) for fb in range(4)])
    fence(["lng", "lnb", ("wd", 0)])
    load_wd(0)
    load_wu(1)
    transposes_c(5)
    up_bank(0, 0, 0)
    transposes_c(6)
    up_bank(0, 1, 0)
    transposes_c(7)
    up_bank(0, 2, 0)
    fence([("x1bf", 0), ("x1bf", 1), ("x1bf", 2), ("wd", 1)])
    load_wd(1)
    load("sp", lng2[:], ln2g_d, "lng2", ["lng2"], reads=[("wd", 1)])
    load("sp", lnb2[:], ln2b_d, "lnb2", ["lnb2"], reads=[("wd", 1)])
    up_bank(0, 3, 0)
    for fb in range(4):
        up_bank(0, fb, 1)

    def ln2_back(i):
        ln_back(i, lng2[:], lnb2[:], "lng2", "lnb2", on_dve=(i >= 6))
        outs.append(load("sp", out_d[i * 128:(i + 1) * 128, :], x1[:, i, :], ("out", i), [],
                         reads=[("x1", i, q) for q in range(4)]))

    for p in range(NPART - 2):
        for i in range(8):
            down_tile(p, i)
            up_bank(p + 1, i // 2, i % 2)
        load_wu(p + 2)
        load_wd(p + 2)
    for fb in range(4):
        for half in range(2):
            up_bank(NPART - 1, fb, half)
    for i in range(8):
        down_tile(None, i, parts=(NPART - 2, NPART - 1))
        ln_front(i)
        if i >= 1:
            ln2_back(i - 1)
    ln2_back(7)
    S.flush(outs)
    return nc


def _consts():
    c = np.zeros((128, 1024), np.float32)
    c[:, 0:128] = np.eye(128, dtype=np.float32)
    s = np.arange(128)
    tri = (s[:, None] <= s[None, :]).astype(np.float32)
    for h in range(4):
        c[:, 128 + h * 128:128 + (h + 1) * 128] = tri
    c[:, 640:768] = (s[:, None] > s[None, :]).astype(np.float32)
    c[:, 768:896] = 1.0
    return c


def make_in_maps(x, w_in, conv_w, conv_norm_g, w_gate_up, gate_bias, gla_norm_g, w_out,
                 ln1_g, ln1_b, w_ff_up, w_ff_down, ln2_g, ln2_b):
    f = np.float32
    x = np.asarray(x, f)
    B, SEQ, _ = x.shape
    shared = {
        "w_in": np.ascontiguousarray(np.asarray(w_in, f)[0]),
        "w_out": np.ascontiguousarray(np.asarray(w_out, f)[0]),
        "w_up": np.ascontiguousarray(np.asarray(w_ff_up, f)[0]),
        "w_down": np.ascontiguousarray(np.asarray(w_ff_down, f)[0]),
        "wg1": np.ascontiguousarray(np.concatenate([np.asarray(w_gate_up, f)[0], np.asarray(gate_bias, f)[0][None, :]], 0)),
        "convw": np.ascontiguousarray(np.asarray(conv_w, f)[0].reshape(3, 8, 128).transpose(2, 1, 0).reshape(128, 24)),
        "convg": np.ascontiguousarray(np.asarray(conv_norm_g, f)[0].reshape(8, 128).T),
        "glag": np.ascontiguousarray(np.broadcast_to(np.asarray(gla_norm_g, f)[0][None, :], (128, 1024))),
        "ln1g": np.ascontiguousarray(np.broadcast_to(np.asarray(ln1_g, f)[0][None, :], (128, D))),
        "ln1b": np.ascontiguousarray(np.broadcast_to(np.asarray(ln1_b, f)[0][None, :], (128, D))),
        "ln2g": np.ascontiguousarray(np.broadcast_to(np.asarray(ln2_g, f)[0][None, :], (128, D))),
        "ln2b": np.ascontiguousarray(np.broadcast_to(np.asarray(ln2_b, f)[0][None, :], (128, D))),
        "cst": _consts(),
    }
    in_maps = []
    nq = SEQ // NT
    for c in range(8):
        b, j = c // nq, c % nq
        s0 = j * NT
        hal = np.zeros((2, D), f)
        if s0 > 0:
            hal[:] = x[b, s0 - 2:s0]
        m = dict(shared)
        m["x"] = np.ascontiguousarray(x[b, s0:s0 + NT])
        m["xT"] = np.ascontiguousarray(x[b, s0:s0 + NT].T)
        m["xTh"] = np.ascontiguousarray(hal.T)
        wm = np.zeros((128, 4), f)
        wm[:, :j] = 1.0
        m["wmask"] = wm
        in_maps.append(m)
    return in_maps


def kernel(**inputs):
    in_maps = make_in_maps(**inputs)
    nc = build_program()
    res = run_bass_kernel_spmd(nc, in_maps, core_ids=list(range(8)))
    x = inputs["x"]
    B, SEQ, _ = x.shape
    out = np.empty((B, SEQ, D), np.float32)
    nq = SEQ // NT
    for c in range(8):
        b, j = c // nq, c % nq
        out[b, j * NT:(j + 1) * NT] = np.asarray(res.results[c]["out"], np.float32)
    return out
```
